# Optimizing a Trainium2 kernel written in Bass

```python
import math
import numpy as np
import jax, jax.numpy as jnp
from jax import lax

D_MODEL = 1024
BATCH = 16
SEQ = 256
DEPTH = 2
DEC_BATCH = 2
DEC_SEQ = 1024
PAST_LEN = 512

GRID_W = 64
Q_BLOCK = 128
EPS = 1e-6
ROPE_BASE = 10000.0

MLA_HEADS = 8
MLA_Q_RANK = 256
MLA_KV_RANK = 128
MLA_NOPE = 64
MLA_ROPE = 32
MLA_V = 64
MLA_QK = MLA_NOPE + MLA_ROPE
RET_HEADS = 4
RET_DK = 64
RET_DV = 128
RET_CHUNK = 64
DIFF_HEADS = 4
DIFF_DH = 64
DIFF_W = DIFF_HEADS * 2 * DIFF_DH
RWKV_HEADS = 8
RWKV_HS = 64
RWKV_W = RWKV_HEADS * RWKV_HS
RWKV_W_LORA = 64
RWKV_A_LORA = 64
RWKV_G_LORA = 128
D_FF = 2816

EVEN_SIZES = (MLA_Q_RANK, MLA_KV_RANK, MLA_ROPE, RET_HEADS * RET_DK, RET_HEADS * RET_DK, RET_HEADS * RET_DV, RET_HEADS * RET_DV)
EVEN_IN = sum(EVEN_SIZES)
RWKV_SIZES = (RWKV_W, RWKV_W, RWKV_W, RWKV_W_LORA, RWKV_W_LORA, RWKV_A_LORA, RWKV_A_LORA, RWKV_G_LORA)
RWKV_IN = sum(RWKV_SIZES)
ODD_SIZES = (DIFF_W, DIFF_W, DIFF_W, RWKV_IN)
ODD_IN = sum(ODD_SIZES)

kernel_name = 'hybrid_diffusion_prefix_trunk_step'


def rms_norm(x, g):
    xf = x.astype(jnp.float32)
    y = xf * lax.rsqrt(jnp.mean(jnp.square(xf), axis=-1, keepdims=True) + EPS)
    return (y * g.astype(jnp.float32)).astype(x.dtype)


def split_cols(x, sizes):
    idx = np.cumsum(sizes)[:-1].tolist()
    return jnp.split(x, idx, axis=-1)


def modulation(cond, ada_w, ada_b):
    m = (jax.nn.silu(cond) @ ada_w + ada_b)[:, None, :]
    return jnp.split(m, 6, axis=-1)


def axial_rope(row, col, rot_dim):
    n_freq = rot_dim // 4
    inv = ROPE_BASE ** (-jnp.arange(n_freq, dtype=jnp.float32) / n_freq)
    ang = jnp.concatenate([row.astype(jnp.float32)[:, None] * inv, col.astype(jnp.float32)[:, None] * inv], -1)
    return jnp.cos(ang), jnp.sin(ang)


def apply_rope(x, cos, sin):
    shape = (x.shape[1],) + (1,) * (x.ndim - 3) + (cos.shape[-1],)
    cos, sin = cos.reshape(shape), sin.reshape(shape)
    xf = x.astype(jnp.float32)
    x1, x2 = xf[..., 0::2], xf[..., 1::2]
    out = jnp.stack([x1 * cos - x2 * sin, x1 * sin + x2 * cos], -1).reshape(x.shape)
    return out.astype(x.dtype)


def rope_tail(x, cos, sin):
    r = 2 * cos.shape[-1]
    return jnp.concatenate([x[..., :-r], apply_rope(x[..., -r:], cos, sin)], -1)


def centred_neighbours(x):
    xp = jnp.pad(x, ((0, 0), (1, 1), (0, 0)))
    return 0.5 * (xp[:, :-2] + xp[:, 2:])


def dwconv3(x, w, b):
    xp = jnp.pad(x, ((0, 0), (1, 1), (0, 0)))
    return xp[:, :-2] * w[0] + x * w[1] + xp[:, 2:] * w[2] + b


def attend_blocks(q, k, v):
    b, nq, h, dq = q.shape
    nb = nq // Q_BLOCK
    scale = dq ** -0.5
    qb = jnp.moveaxis(q.reshape(b, nb, Q_BLOCK, h, dq), 1, 0)

    def one_block(qi):
        s = jnp.einsum('bqhd,bkhd->bhqk', qi, k).astype(jnp.float32) * scale
        p = jax.nn.softmax(s, axis=-1).astype(v.dtype)
        return jnp.einsum('bhqk,bkhe->bqhe', p, v)

    o = lax.map(one_block, qb)
    return jnp.moveaxis(o, 0, 1).reshape(b, nq, h, v.shape[-1])


def diff_attend_blocks(q, k, v, lam):
    b, nq, h, _, d = q.shape
    nb = nq // Q_BLOCK
    scale = d ** -0.5
    qb = jnp.moveaxis(q.reshape(b, nb, Q_BLOCK, h, 2, d), 1, 0)

    def one_block(qi):
        s = jnp.einsum('bqhcd,bkhcd->bchqk', qi, k).astype(jnp.float32) * scale
        p = jax.nn.softmax(s, axis=-1)
        w = (p[:, 0] - lam * p[:, 1]).astype(v.dtype)
        return jnp.einsum('bhqk,bkhe->bqhe', w, v)

    o = lax.map(one_block, qb)
    return jnp.moveaxis(o, 0, 1).reshape(b, nq, h, v.shape[-1])


def retention_chunks(q, k, v, log_g, s0):
    b, n, h, _ = q.shape
    dv = v.shape[-1]
    nc = n // RET_CHUNK

    def chunks(t):
        return jnp.moveaxis(t.astype(jnp.float32).reshape(b, nc, RET_CHUNK, h, t.shape[-1]), 1, 0)

    pos = jnp.arange(RET_CHUNK, dtype=jnp.float32)
    diff = pos[:, None] - pos[None, :]
    intra = jnp.where(diff >= 0, jnp.exp(log_g[:, None, None] * jnp.maximum(diff, 0.0)), 0.0)
    q_dec = jnp.exp(log_g[None, :] * (pos[:, None] + 1.0))
    k_dec = jnp.exp(log_g[None, :] * (RET_CHUNK - 1.0 - pos[:, None]))
    c_dec = jnp.exp(log_g * RET_CHUNK)

    def step(state, inp):
        qc, kc, vc = inp
        scores = jnp.einsum('bihd,bjhd->bhij', qc, kc) * intra
        o = jnp.einsum('bhij,bjhe->bihe', scores, vc) + jnp.einsum('bihd,bhde->bihe', qc * q_dec[:, :, None], state)
        state = state * c_dec[:, None, None] + jnp.einsum('bjhd,bjhe->bhde', kc * k_dec[:, :, None], vc)
        return state, o

    state, o = lax.scan(step, s0.astype(jnp.float32), (chunks(q), chunks(k), chunks(v)))
    return jnp.moveaxis(o, 0, 1).reshape(b, n, h, dv), state


def retention_bidir(q, k, v, decay_logit, s0f, s0b):
    log_g = -jax.nn.softplus(-decay_logit.astype(jnp.float32))
    o_f, s_f = retention_chunks(q, k, v, log_g[0], s0f)
    o_b, s_b = retention_chunks(q[:, ::-1], k[:, ::-1], v[:, ::-1], log_g[1], s0b)
    return o_f + o_b[:, ::-1], s_f, s_b


def rwkv7_scan(r, decay, k, v, kk, a, s0):
    def step(state, inp):
        r_t, w_t, k_t, v_t, kk_t, a_t = inp
        sa = jnp.einsum('bhvk,bhk->bhv', state, kk_t)
        state = (state * w_t[:, :, None, :] - sa[..., None] * (kk_t * a_t)[:, :, None, :]
                 + v_t[..., None] * k_t[:, :, None, :])
        return state, jnp.einsum('bhvk,bhk->bhv', state, r_t)

    xs = tuple(jnp.moveaxis(t, 1, 0) for t in (r, decay, k, v, kk, a))
    state, y = lax.scan(step, s0.astype(jnp.float32), xs)
    return jnp.moveaxis(y, 0, 1), state


def rwkv_inputs(p, mu, w0, w_up, a0, a_up, g_up, k_k, k_a):
    b, n, _ = p.shape
    p = p + (centred_neighbours(p) - p) * mu
    r, k, v, wdf, wdb, adf, adb, gd = split_cols(p, RWKV_SIZES)
    hs = (b, n, RWKV_HEADS, RWKV_HS)
    r, k, v = (t.astype(jnp.float32).reshape(hs) for t in (r, k, v))
    kk = k * k_k.reshape(RWKV_HEADS, RWKV_HS)
    kk = kk * lax.rsqrt(jnp.sum(kk * kk, axis=-1, keepdims=True) + EPS)
    g = jax.nn.sigmoid(gd) @ g_up
    dirs = []
    for d, (wd, ad) in enumerate(((wdf, adf), (wdb, adb))):
        pre = (w0[d] + jnp.tanh(wd) @ w_up[d]).astype(jnp.float32)
        decay = jnp.exp(-jnp.exp(-jax.nn.softplus(-pre) - 0.5)).reshape(hs)
        a = jax.nn.sigmoid((a0[d] + ad @ a_up[d]).astype(jnp.float32)).reshape(hs)
        k_d = k * (1.0 + (a - 1.0) * k_a.reshape(RWKV_HEADS, RWKV_HS))
        dirs.append((decay, a, k_d))
    return r, v, kk, g, dirs


def rwkv_mix(rw, s0f, s0b, r_k, gn):
    r, v, kk, g, dirs = rw
    (w_f, a_f, k_f), (w_b, a_b, k_b) = dirs
    b, n = r.shape[:2]
    y_f, s_f = rwkv7_scan(r, w_f, k_f, v, kk, a_f, s0f)
    y_b, s_b = rwkv7_scan(r[:, ::-1], w_b[:, ::-1], k_b[:, ::-1], v[:, ::-1], kk[:, ::-1], a_b[:, ::-1], s0b)
    y = rms_norm(y_f + y_b[:, ::-1], gn.reshape(RWKV_HEADS, RWKV_HS))
    bonus = (jnp.sum(r * k_f * r_k, -1, keepdims=True) + jnp.sum(r * k_b * r_k, -1, keepdims=True)) * v
    return (y + bonus).reshape(b, n, RWKV_W) * g, s_f, s_b


def mla_keys_values(ckv, krope, w_ukv, kn):
    b, n, _ = ckv.shape
    kv = (ckv @ w_ukv).reshape(b, n, MLA_HEADS, MLA_NOPE + MLA_V)
    k_nope, v = kv[..., :MLA_NOPE], kv[..., MLA_NOPE:]
    k_rope = jnp.broadcast_to(krope[:, :, None, :], (b, n, MLA_HEADS, MLA_ROPE))
    return rms_norm(jnp.concatenate([k_nope, k_rope], -1), kn), v


def even_inputs(h, pe):
    w_in, q_norm, kv_norm, w_uq, w_ukv, qn, kn = pe[:7]
    b, n, _ = h.shape
    cq, ckv, krope, rq, rk, rv, rg = split_cols(h @ w_in, EVEN_SIZES)
    q = rms_norm((rms_norm(cq, q_norm) @ w_uq).reshape(b, n, MLA_HEADS, MLA_QK), qn)
    ckv = rms_norm(ckv, kv_norm)
    k, v = mla_keys_values(ckv, krope, w_ukv, kn)
    rq = rq.reshape(b, n, RET_HEADS, RET_DK)
    rk = rk.reshape(b, n, RET_HEADS, RET_DK) * (RET_DK ** -0.5)
    rv = rv.reshape(b, n, RET_HEADS, RET_DV)
    return q, k, v, ckv, krope, rq, rk, rv, rg


def even_output(mla_o, ret_o, rg, ret_gn, dtype):
    b, n = mla_o.shape[:2]
    ret = jax.nn.silu(rg) * rms_norm(ret_o, ret_gn.reshape(RET_HEADS, RET_DV)).reshape(b, n, -1)
    return jnp.concatenate([mla_o.reshape(b, n, -1).astype(dtype), ret.astype(dtype)], -1)


def even_context(h, pe):
    q, k, v, ckv, krope, rq, rk, rv, rg = even_inputs(h, pe)
    mla_o = attend_blocks(q, k, v)
    s0 = jnp.zeros((h.shape[0], RET_HEADS, RET_DK, RET_DV), jnp.float32)
    ret_o, s_f, s_b = retention_bidir(rq, rk, rv, pe[7], s0, s0)
    out = even_output(mla_o, ret_o, rg, pe[8], h.dtype)
    return out, ckv, krope, jnp.stack([s_f, s_b], 1).astype(h.dtype)


def even_latent(h, ckv_c, krope_c, st, pe, cos, sin):
    q, k, v, _, _, rq, rk, rv, rg = even_inputs(h, pe)
    q, k = rope_tail(q, cos, sin), rope_tail(k, cos, sin)
    k_c, v_c = mla_keys_values(ckv_c, krope_c, pe[4], pe[6])
    mla_o = attend_blocks(q, jnp.concatenate([k, k_c], 1), jnp.concatenate([v, v_c], 1))
    ret_o, _, _ = retention_bidir(rq, rk, rv, pe[7], st[:, 0], st[:, 1])
    return even_output(mla_o, ret_o, rg, pe[8], h.dtype)


def odd_inputs(h, po):
    w_in, qn, kn = po[:3]
    b, n, _ = h.shape
    dq, dk, dv, p = split_cols(h @ w_in, ODD_SIZES)
    q = rms_norm(dq.reshape(b, n, DIFF_HEADS, 2, DIFF_DH), qn)
    k = rms_norm(dk.reshape(b, n, DIFF_HEADS, 2, DIFF_DH), kn)
    v = dv.reshape(b, n, DIFF_HEADS, 2 * DIFF_DH)
    rw = rwkv_inputs(p, *po[5:13])
    return q, k, v, rw


def diff_lambda(lam_vec, lam_init):
    lv = lam_vec.astype(jnp.float32)
    return jnp.exp(jnp.sum(lv[0] * lv[1])) - jnp.exp(jnp.sum(lv[2] * lv[3])) + lam_init


def odd_output(diff_o, rw_o, diff_gn, lam_init, dtype):
    b, n = diff_o.shape[:2]
    d = rms_norm(diff_o, diff_gn.reshape(DIFF_HEADS, 2 * DIFF_DH)) * (1.0 - lam_init)
    return jnp.concatenate([d.reshape(b, n, -1).astype(dtype), rw_o.astype(dtype)], -1)


def odd_context(h, po, lam_init):
    q, k, v, rw = odd_inputs(h, po)
    diff_o = diff_attend_blocks(q, k, v, diff_lambda(po[3], lam_init))
    s0 = jnp.zeros((h.shape[0], RWKV_HEADS, RWKV_HS, RWKV_HS), jnp.float32)
    rw_o, s_f, s_b = rwkv_mix(rw, s0, s0, po[13], po[14])
    out = odd_output(diff_o, rw_o, po[4], lam_init, h.dtype)
    return out, k, v, jnp.stack([s_f, s_b], 1).astype(h.dtype)


def odd_latent(h, k_c, v_c, st, po, lam_init, cos, sin):
    q, k, v, rw = odd_inputs(h, po)
    q, k = apply_rope(q, cos, sin), apply_rope(k, cos, sin)
    diff_o = diff_attend_blocks(q, jnp.concatenate([k, k_c], 1), jnp.concatenate([v, v_c], 1),
                                diff_lambda(po[3], lam_init))
    rw_o, _, _ = rwkv_mix(rw, st[:, 0], st[:, 1], po[13], po[14])
    return odd_output(diff_o, rw_o, po[4], lam_init, h.dtype)


def conv_ffn(h, up, cw, cb, down):
    u = dwconv3(h @ up, cw, cb)
    a, b = jnp.split(u, 2, axis=-1)
    return (jax.nn.silu(a) * b) @ down


def setup_inputs(seed: int = 0) -> dict:
    key = jax.random.key(seed)
    kit = iter(jax.random.split(key, 64))

    def nrm(shape, s=1.0):
        return s * jax.random.normal(next(kit), shape, jnp.float32)

    def gain(shape):
        return 1.0 + 0.02 * jax.random.normal(next(kit), shape, jnp.float32)

    ne, no = (DEPTH + 1) // 2, DEPTH // 2
    d = D_MODEL
    eps = 2.0 ** (-5.0 - jnp.arange(RET_HEADS, dtype=jnp.float32))
    ret_base = jnp.log((1.0 - eps) / eps)
    conv_centre = jnp.zeros((3, 1), jnp.float32).at[1].set(1.0)
    return {
        'x_prompt': nrm((BATCH, SEQ, d)),
        'x_sample': nrm((DEC_BATCH, DEC_SEQ, d)),
        'cache_mla_ckv': nrm((DEC_BATCH, ne, PAST_LEN, MLA_KV_RANK)),
        'cache_mla_krope': nrm((DEC_BATCH, ne, PAST_LEN, MLA_ROPE)),
        'state_ret': nrm((DEC_BATCH, ne, 2, RET_HEADS, RET_DK, RET_DV)),
        'cache_diff_k': nrm((DEC_BATCH, no, PAST_LEN, DIFF_HEADS, 2, DIFF_DH)),
        'cache_diff_v': nrm((DEC_BATCH, no, PAST_LEN, DIFF_HEADS, 2 * DIFF_DH)),
        'state_rwkv': nrm((DEC_BATCH, no, 2, RWKV_HEADS, RWKV_HS, RWKV_HS), 0.3),
        'c': nrm((DEC_BATCH, d)),
        'c_ctx': nrm((d,)),
        'ada_w': nrm((DEPTH, d, 6 * d), 0.5 * d ** -0.5),
        'ada_b': nrm((DEPTH, 6 * d), 0.01),
        'norm_mix_g': gain((DEPTH, d)),
        'norm_ffn_g': gain((DEPTH, d)),
        'w_out': nrm((DEPTH, d, d), d ** -0.5),
        'ffn_up': nrm((DEPTH, d, 2 * D_FF), d ** -0.5),
        'ffn_conv_w': conv_centre + nrm((DEPTH, 3, 2 * D_FF), 0.3),
        'ffn_conv_b': nrm((DEPTH, 2 * D_FF), 0.01),
        'ffn_down': nrm((DEPTH, D_FF, d), D_FF ** -0.5),
        'a_w_in': nrm((ne, d, EVEN_IN), d ** -0.5),
        'mla_q_norm': gain((ne, MLA_Q_RANK)),
        'mla_kv_norm': gain((ne, MLA_KV_RANK)),
        'mla_w_uq': nrm((ne, MLA_Q_RANK, MLA_HEADS * MLA_QK), MLA_Q_RANK ** -0.5),
        'mla_w_ukv': nrm((ne, MLA_KV_RANK, MLA_HEADS * (MLA_NOPE + MLA_V)), MLA_KV_RANK ** -0.5),
        'mla_qn': gain((ne, MLA_QK)),
        'mla_kn': gain((ne, MLA_QK)),
        'ret_decay': ret_base + nrm((ne, 2, RET_HEADS), 0.1),
        'ret_gn': gain((ne, RET_HEADS * RET_DV)),
        'b_w_in': nrm((no, d, ODD_IN), d ** -0.5),
        'diff_qn': gain((no, DIFF_DH)),
        'diff_kn': gain((no, DIFF_DH)),
        'diff_lam': nrm((no, 4, DIFF_DH), 0.1),
        'diff_gn': gain((no, DIFF_W)),
        'rwkv_mu': jax.random.uniform(next(kit), (no, RWKV_IN), jnp.float32),
        'rwkv_w0': -2.0 + nrm((no, 2, RWKV_W)),
        'rwkv_w_up': nrm((no, 2, RWKV_W_LORA, RWKV_W), 0.5 * RWKV_W_LORA ** -0.5),
        'rwkv_a0': nrm((no, 2, RWKV_W), 0.5),
        'rwkv_a_up': nrm((no, 2, RWKV_A_LORA, RWKV_W), RWKV_A_LORA ** -0.5),
        'rwkv_g_up': nrm((no, RWKV_G_LORA, RWKV_W), RWKV_G_LORA ** -0.5),
        'rwkv_k_k': 0.85 + nrm((no, RWKV_W), 0.05),
        'rwkv_k_a': gain((no, RWKV_W)),
        'rwkv_r_k': nrm((no, RWKV_HEADS, RWKV_HS), 0.1),
        'rwkv_gn': gain((no, RWKV_W)),
    }


def reference(x_prompt, x_sample, cache_mla_ckv, cache_mla_krope, state_ret, cache_diff_k, cache_diff_v, state_rwkv,
              c, c_ctx, ada_w, ada_b, norm_mix_g, norm_ffn_g, w_out, ffn_up, ffn_conv_w, ffn_conv_b, ffn_down,
              a_w_in, mla_q_norm, mla_kv_norm, mla_w_uq, mla_w_ukv, mla_qn, mla_kn, ret_decay, ret_gn,
              b_w_in, diff_qn, diff_kn, diff_lam, diff_gn, rwkv_mu, rwkv_w0, rwkv_w_up, rwkv_a0, rwkv_a_up,
              rwkv_g_up, rwkv_k_k, rwkv_k_a, rwkv_r_k, rwkv_gn):
    n_lat = x_sample.shape[1]
    rows = n_lat // GRID_W
    t = jnp.arange(rows * GRID_W)
    row, col = t // GRID_W, t % GRID_W
    cos_m, sin_m = axial_rope(row, col, MLA_ROPE)
    cos_d, sin_d = axial_rope(row, col, DIFF_DH)
    cond_ctx = c_ctx[None, :]
    xc, xl = x_prompt, x_sample
    new_ckv, new_krope, new_ret, new_dk, new_dv, new_rwkv = [], [], [], [], [], []
    for l in range(DEPTH):
        sh_c, sc_c, ga_c, shf_c, scf_c, gf_c = modulation(cond_ctx, ada_w[l], ada_b[l])
        sh_l, sc_l, ga_l, shf_l, scf_l, gf_l = modulation(c, ada_w[l], ada_b[l])
        hc = rms_norm(xc, norm_mix_g[l]) * (1.0 + sc_c) + sh_c
        hl = rms_norm(xl, norm_mix_g[l]) * (1.0 + sc_l) + sh_l
        j = l // 2
        if l % 2 == 0:
            pe = (a_w_in[j], mla_q_norm[j], mla_kv_norm[j], mla_w_uq[j], mla_w_ukv[j], mla_qn[j], mla_kn[j],
                  ret_decay[j], ret_gn[j])
            oc, ckv, krope, st = even_context(hc, pe)
            ol = even_latent(hl, cache_mla_ckv[:, j], cache_mla_krope[:, j], state_ret[:, j], pe, cos_m, sin_m)
            new_ckv.append(ckv)
            new_krope.append(krope)
            new_ret.append(st)
        else:
            lam_init = 0.8 - 0.6 * math.exp(-0.3 * l)
            po = (b_w_in[j], diff_qn[j], diff_kn[j], diff_lam[j], diff_gn[j], rwkv_mu[j], rwkv_w0[j], rwkv_w_up[j],
                  rwkv_a0[j], rwkv_a_up[j], rwkv_g_up[j], rwkv_k_k[j], rwkv_k_a[j], rwkv_r_k[j], rwkv_gn[j])
            oc, k_new, v_new, st = odd_context(hc, po, lam_init)
            ol = odd_latent(hl, cache_diff_k[:, j], cache_diff_v[:, j], state_rwkv[:, j], po, lam_init, cos_d, sin_d)
            new_dk.append(k_new)
            new_dv.append(v_new)
            new_rwkv.append(st)
        xc = xc + ga_c * (oc @ w_out[l])
        xl = xl + ga_l * (ol @ w_out[l])
        hc = rms_norm(xc, norm_ffn_g[l]) * (1.0 + scf_c) + shf_c
        hl = rms_norm(xl, norm_ffn_g[l]) * (1.0 + scf_l) + shf_l
        xc = xc + gf_c * conv_ffn(hc, ffn_up[l], ffn_conv_w[l], ffn_conv_b[l], ffn_down[l])
        xl = xl + gf_l * conv_ffn(hl, ffn_up[l], ffn_conv_w[l], ffn_conv_b[l], ffn_down[l])
    return (xc, xl, jnp.stack(new_ckv, 1), jnp.stack(new_krope, 1), jnp.stack(new_ret, 1),
            jnp.stack(new_dk, 1), jnp.stack(new_dv, 1), jnp.stack(new_rwkv, 1))
```

```python
import math
import numpy as np
import concourse.bass as bass
import concourse.mybir as mybir
from concourse.bass_utils import run_bass_kernel_spmd

F32 = mybir.dt.float32
BF16 = mybir.dt.bfloat16
AF = mybir.ActivationFunctionType
ALU = mybir.AluOpType
AX = mybir.AxisListType

ENGS = ["pe", "act", "dve", "pool", "sp"]


class Buf:
    def __init__(self, t, name):
        self.t = t
        self.name = name
        self.writers = {}
        self.readers = {}

    def __getitem__(self, idx):
        return self.t[idx]


class Op:
    __slots__ = ("eng", "idx", "fn", "deps", "is_dma", "dsem", "dval", "signal",
                 "sigval", "waits", "snap", "gidx")

    def __init__(self, eng, fn):
        self.eng = eng
        self.fn = fn
        self.deps = {}
        self.is_dma = False
        self.dsem = None
        self.dval = 0
        self.signal = False
        self.sigval = 0
        self.waits = []
        self.snap = None


def _norm(acc):
    out = []
    for a in acc:
        if a is None:
            continue
        if isinstance(a, Buf):
            out.append((a, None))
        else:
            out.append((a[0], a[1]))
    return out


class Prog:
    def __init__(self, nc):
        self.nc = nc
        self.ops = {e: [] for e in ENGS}
        self.allops = []
        self.dma_cnt = {}
        self.keymap = {}
        self.free_sems = []
        self.sem_q = {}
        self.ctx = []
        self.nbuf = 0
        self.dmas_since_bar = []
        self.bar_labels = []

    def sbuf(self, shape, dt, name=None):
        self.nbuf += 1
        name = (name or "sb") + f"_{self.nbuf}"
        cm = self.nc.sbuf_tensor(name, list(shape), dt)
        t = cm.__enter__()
        self.ctx.append(cm)
        return Buf(t, name)

    def psum(self, shape, dt, name=None):
        self.nbuf += 1
        name = (name or "ps") + f"_{self.nbuf}"
        cm = self.nc.psum_tensor(name, list(shape), dt)
        t = cm.__enter__()
        self.ctx.append(cm)
        return Buf(t, name)

    def _track(self, op, reads, writes):
        def add(d, raw):
            if d is op:
                return
            if raw or d not in op.deps:
                op.deps[d] = raw or op.deps.get(d, False)
        for b, k in _norm(reads):
            if k is None:
                for w in b.writers.values():
                    add(w, True)
            else:
                w = b.writers.get(k)
                if w is not None:
                    add(w, True)
                w = b.writers.get(None)
                if w is not None:
                    add(w, True)
            b.readers.setdefault(k, []).append(op)
        for b, k in _norm(writes):
            if k is None:
                for w in b.writers.values():
                    add(w, False)
                for rl in b.readers.values():
                    for r in rl:
                        add(r, False)
                b.writers = {None: op}
                b.readers = {}
            else:
                for kk in (k, None):
                    w = b.writers.get(kk)
                    if w is not None:
                        add(w, False)
                    for r in b.readers.get(kk, ()):
                        add(r, False)
                b.writers[k] = op
                b.readers[k] = []

    def op(self, eng, fn, reads=(), writes=()):
        o = Op(eng, fn)
        o.idx = len(self.ops[eng])
        o.gidx = len(self.allops)
        self.ops[eng].append(o)
        self.allops.append(o)
        self._track(o, reads, writes)
        return o

    def dma(self, q, out, in_, reads=(), writes=(), sem=None, **kw):
        def fn(e, out=out, in_=in_, kw=kw):
            return e.dma_start(out=out, in_=in_, **kw)
        o = self.op(q, fn, reads, writes)
        o.is_dma = True
        if sem is None:
            b, k = _norm(writes)[0]
            sem = (b.name, k)
        sem = (q, sem)
        if sem not in self.keymap:
            fl = [i for i in self.free_sems if self.sem_q[i] == q]
            if fl:
                self.free_sems.remove(fl[0])
                self.keymap[sem] = fl[0]
            else:
                self.keymap[sem] = len(self.dma_cnt)
                self.dma_cnt[self.keymap[sem]] = 0
                self.sem_q[self.keymap[sem]] = q
        sem = self.keymap[sem]
        o.dsem = sem
        self.dma_cnt[sem] += 1
        o.dval = 16 * self.dma_cnt[sem]
        self.dmas_since_bar.append(o)
        return o

    def barrier(self):
        if getattr(self, "_bar_mark", -1) == len(self.allops):
            return
        last = [self.ops[e][-1] for e in ENGS if self.ops[e]]
        dm = list(self.dmas_since_bar)
        self.dmas_since_bar = []
        self.free_sems = sorted(set(self.free_sems) | set(self.keymap.values()))
        self.keymap = {}
        for e in ENGS:
            o = self.op(e, lambda en: en.nop())
            for d in last + dm:
                if d is not o:
                    o.deps[d] = True
        self._bar_mark = len(self.allops)

    class _Scope:
        def __init__(self, p, name=None):
            self.p = p
            self.name = name

        def __enter__(self):
            self.n = len(self.p.ctx)
            return self

        def __exit__(self, *a):
            self.p.bar_labels.append(self.name)
            self.p.barrier()
            while len(self.p.ctx) > self.n:
                self.p.ctx.pop().__exit__(None, None, None)
            return False

    def scope(self, name=None):
        return Prog._Scope(self, name)

    def mm(self, out, lhsT, rhs, start=True, stop=True, reads=(), writes=()):
        return self.op("pe", lambda e: e.matmul(out, lhsT, rhs, start=start, stop=stop), reads, writes)

    def act(self, out, in_, func, reads=(), writes=(), **kw):
        return self.op("act", lambda e: e.activation(out=out, in_=in_, func=func, **kw), reads, writes)

    def tt(self, eng, out, in0, in1, op, reads=(), writes=()):
        return self.op(eng, lambda e: e.tensor_tensor(out=out, in0=in0, in1=in1, op=op), reads, writes)

    def ts(self, eng, out, in0, s1, s2, op0, op1=None, reads=(), writes=()):
        if op1 is None:
            return self.op(eng, lambda e: e.tensor_scalar(out=out, in0=in0, scalar1=s1, scalar2=None, op0=op0),
                           reads, writes)
        return self.op(eng, lambda e: e.tensor_scalar(out=out, in0=in0, scalar1=s1, scalar2=s2, op0=op0, op1=op1),
                       reads, writes)

    def stt(self, out, in0, scalar, in1, op0, op1, reads=(), writes=()):
        return self.op("dve", lambda e: e.scalar_tensor_tensor(out=out, in0=in0, scalar=scalar, in1=in1,
                                                               op0=op0, op1=op1), reads, writes)

    def copy(self, eng, out, in_, reads=(), writes=()):
        if eng == "act":
            return self.op(eng, lambda e: e.copy(out=out, in_=in_), reads, writes)
        return self.op(eng, lambda e: e.tensor_copy(out=out, in_=in_), reads, writes)

    def memset(self, eng, ap, val, writes=()):
        return self.op(eng, lambda e: e.memset(ap, val), (), writes)

    def recip(self, out, in_, reads=(), writes=()):
        return self.op("dve", lambda e: e.reciprocal(out=out, in_=in_), reads, writes)

    def emit(self, final_dma_ops=()):
        nc = self.nc
        obs_eng = {e: {p: -1 for p in ENGS} for e in ENGS}
        obs_dma = {e: {} for e in ENGS}
        for o in self.allops:
            oe = obs_eng[o.eng]
            od = obs_dma[o.eng]
            need = {}
            dneed = {}
            for d, raw in o.deps.items():
                if (not d.is_dma) and d.eng == o.eng and o.eng == "pe":
                    continue
                if d.is_dma:
                    if od.get(d.dsem, 0) < d.dval:
                        if dneed.get(d.dsem, (0, None))[0] < d.dval:
                            dneed[d.dsem] = (d.dval, d)
                else:
                    if oe[d.eng] < d.idx:
                        if d.eng not in need or need[d.eng].idx < d.idx:
                            need[d.eng] = d
            for pe_, d in need.items():
                if oe[pe_] >= d.idx:
                    continue
                d.signal = True
                o.waits.append(("eng", d))
                oe[pe_] = max(oe[pe_], d.idx)
                if d.snap is not None:
                    se, sd = d.snap
                    for k, v in se.items():
                        if k != o.eng and oe[k] < v:
                            oe[k] = v
                    for k, v in sd.items():
                        if od.get(k, 0) < v:
                            od[k] = v
            for sk, (v, d) in dneed.items():
                o.waits.append(("dma", sk, v))
                od[sk] = max(od.get(sk, 0), v)
                if d.snap is not None:
                    se, sd = d.snap
                    for k, vv in se.items():
                        if k != o.eng and oe[k] < vv:
                            oe[k] = vv
                    for k, vv in sd.items():
                        if od.get(k, 0) < vv:
                            od[k] = vv
            o.snap = (dict(oe), dict(od))
        fin_waits = {k: 16 * v for k, v in self.dma_cnt.items() if v > 0}
        for e in ENGS:
            c = 0
            for o in self.ops[e]:
                if o.signal:
                    c += 1
                    o.sigval = c
        sem_cms = []
        esem = {}
        for e in ENGS:
            cm = nc.semaphore(f"s_{e}")
            esem[e] = cm.__enter__()
            sem_cms.append(cm)
        dsem = {}
        for i, k in enumerate(self.dma_cnt.keys()):
            cm = nc.semaphore(f"d_{i}")
            dsem[k] = cm.__enter__()
            sem_cms.append(cm)
        self.n_sems = len(sem_cms)

        def run(engname, eobj):
            for o in self.ops[engname]:
                for w in o.waits:
                    if w[0] == "eng":
                        d = w[1]
                        eobj.wait_ge(esem[d.eng], d.sigval)
                    else:
                        eobj.wait_ge(dsem[w[1]], w[2])
                ins = o.fn(eobj)
                if o.is_dma:
                    ins.then_inc(dsem[o.dsem], 16)
                elif o.signal:
                    ins.then_inc(esem[o.eng], 1)
            if engname == "sp":
                for k, v in fin_waits.items():
                    eobj.wait_ge(dsem[k], v)

        with nc.Block() as block:
            @block.tensor
            def _(e):
                run("pe", e)

            @block.scalar
            def _(e):
                run("act", e)

            @block.vector
            def _(e):
                run("dve", e)

            @block.gpsimd
            def _(e):
                run("pool", e)

            @block.sync
            def _(e):
                run("sp", e)
        for cm in reversed(sem_cms):
            cm.__exit__(None, None, None)
        while self.ctx:
            self.ctx.pop().__exit__(None, None, None)


D = 1024
KC = 8
NPS = 2
SEQ = 256
TP = NPS * SEQ
TS = 1024
T = TP + TS
PAST = 512
EPS = 1e-6
DFF = 2816
NFC = 22
GRID_W = 64
GROUPS = [(0, TP, 0), (TP, T, 1)]
TBLK = [(0, 512), (512, 1024), (1024, 1536)]
RING_ELEMS = 6144
LAM_INIT1 = 0.8 - 0.6 * math.exp(-0.3 * 1)

EVEN_OFF = dict(cq=0, ckv=256, krope=384, rq=416, rk=672, rv=928, rg=1440)


def fm(v):
    v = np.asarray(v, np.float32).reshape(-1, 128)
    return np.ascontiguousarray(v.T)


class Pack:
    def __init__(self):
        self.cols = {}
        self.parts = []
        self.n = 0

    def add(self, name, arr):
        arr = np.asarray(arr, np.float32)
        if arr.ndim == 1:
            arr = arr[:, None]
        a = np.zeros((128, arr.shape[1]), np.float32)
        a[:arr.shape[0]] = arr
        self.cols[name] = (self.n, arr.shape[1])
        self.parts.append(a)
        self.n += arr.shape[1]

    def array(self):
        return np.ascontiguousarray(np.concatenate(self.parts, axis=1))


def rope_tables(rot_dim):
    n_freq = rot_dim // 4
    t = np.arange(TS)
    row, col = t // GRID_W, t % GRID_W
    inv = (10000.0 ** (-np.arange(n_freq, dtype=np.float32) / n_freq)).astype(np.float32)
    ang = np.concatenate([row.astype(np.float32)[:, None] * inv, col.astype(np.float32)[:, None] * inv], -1)
    cos, sin = np.cos(ang).astype(np.float32), np.sin(ang).astype(np.float32)
    C = np.repeat(cos.T, 2, axis=0)
    S = np.repeat(sin.T, 2, axis=0)
    S[0::2] *= -1.0
    return C.astype(np.float32), S.astype(np.float32)


def pair_swap(n, lo=0):
    m = np.zeros((n, n), np.float32)
    for i in range(lo, n, 2):
        m[i, i + 1] = 1.0
        m[i + 1, i] = 1.0
    return m


def make_consts():
    c = {}
    c["ident"] = np.eye(128, dtype=np.float32)
    c["ones"] = np.ones((128, 128), np.float32)
    bd = np.zeros((128, 128), np.float32)
    bd[:64, :64] = 1.0
    bd[64:, 64:] = 1.0
    c["bd64"] = bd
    C32, S32 = rope_tables(32)
    C96 = np.ones((96, TS), np.float32)
    S96 = np.zeros((96, TS), np.float32)
    C96[64:] = C32
    S96[64:] = S32
    c["C96"], c["S96"] = C96, S96
    c["C32"], c["S32"] = C32, S32
    c["perm96"] = pair_swap(96, 64)
    c["perm32"] = pair_swap(32)
    C64, S64 = rope_tables(64)
    c["C128"] = np.concatenate([C64, C64], 0)
    c["S128"] = np.concatenate([S64, S64], 0)
    c["perm128"] = pair_swap(128)
    Dm = (np.arange(1920)[None, :] - 896 - np.arange(128)[:, None]).astype(np.float32)
    c["Dpos"] = np.maximum(Dm, 0.0)
    c["Dneg"] = np.maximum(-Dm, 0.0)
    c["Ddiag"] = (Dm == 0).astype(np.float32)
    tt = np.arange(TS, dtype=np.float32)
    c["tpl1"] = np.broadcast_to(tt + 1.0, (128, TS)).copy()
    c["tNm"] = np.broadcast_to(TS - tt, (128, TS)).copy()
    j = (np.arange(2)[None, :] * 128 + np.arange(128)[:, None]).astype(np.float32)
    c["pj"] = np.stack([255.0 - j, j], axis=1).astype(np.float32)
    s_ = np.arange(64)[:, None]
    t_ = np.arange(64)[None, :]
    up_s = (s_ < t_).astype(np.float32)
    up_i = (s_ <= t_).astype(np.float32)
    lo_s = (s_ > t_).astype(np.float32)
    lo_i = (s_ >= t_).astype(np.float32)
    c["m_ab"] = np.stack([np.concatenate([-up_s, up_s], 0), np.concatenate([-lo_s, lo_s], 0)], 0)
    c["m_cd"] = np.stack([np.concatenate([up_i, up_i], 0), np.concatenate([lo_i, lo_i], 0)], 0)
    c["m_na"] = np.stack([-(lo_s), -(up_s)], 0)
    c["eye64"] = np.eye(64, dtype=np.float32)
    return c


def tbs_of(c0, c1):
    return [i for i, (a, b) in enumerate(TBLK) if a < c1 and c0 < b]


def XK(buf, kc, c0, c1):
    return [(buf, (kc, tb)) for tb in tbs_of(c0, c1)]


class Builder:
    def __init__(self, pack_cols, npk, dbg=(), upto="all"):
        self.nc = bass.Bass("TRN2", target_bir_lowering=False)
        self.p = Prog(self.nc)
        self.pc = pack_cols
        self.npk = npk
        self.dbg = set(dbg)
        self.upto = upto
        self.out_dmas = []
        self.ring_i = 0
        self.ps_i = 0
        self.outs = {}

    def din(self, name, shape):
        return self.nc.dram_tensor(name, list(shape), F32, kind="ExternalInput").ap()

    def dout(self, name, shape):
        ap = self.nc.dram_tensor(name, list(shape), F32, kind="ExternalOutput").ap()
        self.outs[name] = list(shape)
        return ap

    def store(self, dst, src, reads, sem):
        d = self.p.dma("sp", dst, src, reads=reads, sem=sem)
        self.out_dmas.append(d)
        return d

    def debug_dump(self, name, buf, ap, shape):
        if name not in self.dbg:
            return
        dst = self.dout("dbg_" + name, shape)
        self.store(dst, ap, [buf], "dbg_" + name)

    def P(self, name):
        o, n = self.pc[name]
        return self.prm[:, o:o + n]

    def wload(self, src, shape):
        slot = self.ring[self.ring_i % len(self.ring)]
        self.ring_i += 1
        n = int(np.prod(shape[1:]))
        assert n <= RING_ELEMS, (shape, n)
        view = slot.t[0:shape[0], 0:n]
        if len(shape) == 3:
            view = view.rearrange("p (a b) -> p a b", b=shape[2])
        self.p.dma("pool", view, src, writes=[slot])
        return slot, view

    def psn(self):
        b = self.ps[self.ps_i % getattr(self, 'ps_mod', 6)]
        self.ps_i += 1
        return b

    def setup(self):
        p = self.p
        self.x_in = self.din("xT", [D, T])
        self.prm_in = self.din("prm", [128, self.npk])
        self.cb_in = self.din("cbf", [128, 5, 128])
        self.prm = p.sbuf([128, self.npk], F32, "prm")
        p.dma("sp", self.prm[:], self.prm_in[:, :], writes=[self.prm])
        self.cbf = p.sbuf([128, 5, 128], BF16, "cbf")
        p.dma("pool", self.cbf[:], self.cb_in[:, :, :], writes=[self.cbf])
        self.ident = self.cbf[:, 0, :]
        self.ones = self.cbf[:, 1, :]
        self.bd64 = self.cbf[:, 2, :]
        self.perm96 = self.cbf[:, 3, :]
        self.perm128 = self.cbf[:, 4, :]
        self.eps = p.sbuf([128, 1], F32, "eps")
        p.memset("dve", self.eps[:], EPS, writes=[self.eps])
        self.xT = p.sbuf([128, KC, T], F32, "xT")
        for kc in range(KC):
            p.dma("sp", self.xT[:, kc, :], self.x_in[kc * 128:(kc + 1) * 128, :], writes=XK(self.xT, kc, 0, T))
        self.hT = p.sbuf([128, KC, T], BF16, "hT")
        self.ring = [p.sbuf([128, RING_ELEMS], BF16, f"ring{i}") for i in range(3)]
        cm = self.nc.psum_tensor("psall", [128, 8 * 512], F32)
        self.psall = cm.__enter__()
        p.ctx.append(cm)
        self.ps = [Buf(self.psall[:, i * 512:(i + 1) * 512], f"psb{i}") for i in range(8)]
        self.ffn_up_in = self.din("ffn_up", [2, D, 2 * DFF])
        self.ffn_down_in = self.din("ffn_down", [2, DFF, D])
        self.mod = [p.sbuf([128, 48, 2], F32, f"mod{l}") for l in range(2)]
        self.gsc = [p.sbuf([128, 2, KC, 2], F32, f"gsc{l}") for l in range(2)]
        self.scond = p.sbuf([128, KC, 2], BF16, "scond")
        self.cond_in = self.din("condT", [128, KC, 2])
        condf = p.sbuf([128, KC, 2], F32, "condf")
        p.dma("sp", condf[:], self.cond_in[:, :, :], writes=[condf])
        p.act(self.scond[:], condf[:], AF.Silu, reads=[condf], writes=[self.scond])
        self.ada_in = self.din("ada_w", [2, D, 6 * D])
        self.a_w_in = self.din("a_w_in", [D, 1952])
        self.w_out_in = self.din("w_out", [2, D, D])
        self.b_w_in = self.din("b_w_in", [D, 3456])

    def mod_piece(self, l, piece, mps):
        p = self.p
        mview = mps.t[:, 0:12].rearrange("p (j c) -> p j c", c=2)
        src = self.ada_in[l].rearrange("(kc p) n -> p kc n", p=128)
        slot, wv = self.wload(src[:, :, piece * 768:(piece + 1) * 768], [128, KC, 768])
        for nch in range(6):
            for kc in range(KC):
                p.mm(mview[:, nch, :], wv[:, kc, nch * 128:(nch + 1) * 128], self.scond[:, kc, :],
                     start=(kc == 0), stop=(kc == KC - 1), reads=[slot, self.scond], writes=[mps])
        ab = self.P(f"ab{l}")[:, piece * 6:(piece + 1) * 6]
        mod = self.mod[l]
        p.tt("dve", mod[:, piece * 6:(piece + 1) * 6, :], mview, ab.unsqueeze(2).to_broadcast([128, 6, 2]), ALU.add,
             reads=[mps, self.prm], writes=[(mod, piece)])

    def mod_end(self, l):
        p = self.p
        mod = self.mod[l]
        gsc = self.gsc[l]
        for w, (scj, gname) in enumerate(((1, f"gm{l}"), (4, f"gf{l}"))):
            g = self.P(gname)
            p.stt(gsc[:, w, :, :], mod[:, scj * 8:(scj + 1) * 8, :], 1.0, g.unsqueeze(2).to_broadcast([128, KC, 2]),
                  ALU.add, ALU.mult, reads=[mod, self.prm], writes=[(gsc, w)])

    def modulation(self, l):
        for piece in range(8):
            self.mod_piece(l, piece, self.psn())
        self.mod_end(l)

    def norm_mod(self, l, w):
        p = self.p
        gsc = self.gsc[l]
        mod = self.mod[l]
        shj = 0 if w == 0 else 3
        with p.scope("norm_mod:598"):
            sqb = [p.sbuf([128, 512], BF16, f"sq{i}") for i in range(3)]
            tmpb = [p.sbuf([128, 1024], F32, f"nt{i}") for i in range(2)]
            rt = p.sbuf([128, 512], F32, "rt")
            self.rstd = p.sbuf([128, T], F32, "rstd")
            i = 0
            for tb, (c0, c1) in enumerate(TBLK):
                ps = self.psn()
                for kc in range(KC):
                    sq = sqb[i % 3]
                    i += 1
                    p.act(sq[:], self.xT[:, kc, c0:c1], AF.Square, reads=[(self.xT, (kc, tb))], writes=[sq])
                    p.mm(ps[:], self.ones, sq[:], start=(kc == 0), stop=(kc == KC - 1), reads=[sq, self.cbf],
                         writes=[ps])
                p.act(rt[:], ps[:], AF.Ln, bias=self.eps[:], scale=1.0 / D, reads=[ps, self.eps], writes=[rt])
                p.act(self.rstd[:, c0:c1], rt[:], AF.Exp, scale=-0.5, reads=[rt], writes=[(self.rstd, tb)])
            i = 0
            for kc in range(KC):
                for (c0, c1, ci) in GROUPS:
                    tmp = tmpb[i % 2]
                    i += 1
                    n = c1 - c0
                    p.stt(tmp[:, 0:n], self.xT[:, kc, c0:c1], gsc[:, w, kc, ci:ci + 1], self.rstd[:, c0:c1],
                          ALU.mult, ALU.mult, reads=XK(self.xT, kc, c0, c1) + [(gsc, w)] + [(self.rstd, t) for t in tbs_of(c0, c1)],
                          writes=[tmp])
                    p.act(self.hT[:, kc, c0:c1], tmp[:, 0:n], AF.Identity, bias=mod[:, shj * 8 + kc, ci:ci + 1], scale=1.0,
                          reads=[tmp, mod], writes=XK(self.hT, kc, c0, c1))


def build_pack(I):
    pk = Pack()
    for l in range(2):
        pk.add(f"ab{l}", fm(I["ada_b"][l]))
        pk.add(f"gm{l}", fm(I["norm_mix_g"][l]))
        pk.add(f"gf{l}", fm(I["norm_ffn_g"][l]))
        cw = np.stack([fm(I["ffn_conv_w"][l][k]) for k in range(3)], axis=2)
        pk.add(f"cw{l}", cw.reshape(128, 44 * 3))
        pk.add(f"cb{l}", fm(I["ffn_conv_b"][l]))
    pk.add("qnorm", fm(I["mla_q_norm"][0]))
    pk.add("kvnorm", fm(I["mla_kv_norm"][0]))
    pk.add("qn", I["mla_qn"][0])
    pk.add("kn", I["mla_kn"][0])
    pk.add("knr", I["mla_kn"][0][64:96])
    pk.add("retdec", np.broadcast_to(I["ret_decay"][0].reshape(1, 8), (128, 8)))
    pk.add("retgn", fm(I["ret_gn"][0]))
    pk.add("dqn", np.tile(I["diff_qn"][0], 2))
    pk.add("dkn", np.tile(I["diff_kn"][0], 2))
    pk.add("lam", np.broadcast_to(I["diff_lam"][0].reshape(1, 256), (128, 256)))
    pk.add("dgn", fm(I["diff_gn"][0]))
    pk.add("mu", fm(I["rwkv_mu"][0]))
    pk.add("w0", np.concatenate([fm(I["rwkv_w0"][0][d]) for d in range(2)], 1))
    pk.add("a0", np.concatenate([fm(I["rwkv_a0"][0][d]) for d in range(2)], 1))
    pk.add("kk", fm(I["rwkv_k_k"][0]))
    pk.add("ka", fm(I["rwkv_k_a"][0]))
    pk.add("rrk", fm(I["rwkv_r_k"][0].reshape(-1)))
    pk.add("rgn", fm(I["rwkv_gn"][0]))
    return pk


def build_in_maps(I):
    I = {k: np.asarray(v) for k, v in I.items()}
    cst = make_consts()
    pk = build_pack(I)
    prm = pk.array()
    cbf = np.zeros((128, 5, 128), np.float32)
    cbf[:, 0] = cst["ident"]
    cbf[:, 1] = cst["ones"]
    cbf[:, 2] = cst["bd64"]
    cbf[:96, 3, :96] = cst["perm96"]
    cbf[:, 4] = cst["perm128"]
    shared = {
        "prm": prm, "cbf": cbf,
        "ada_w": I["ada_w"], "w_out": I["w_out"], "ffn_up": I["ffn_up"], "ffn_down": I["ffn_down"],
        "a_w_in": I["a_w_in"][0], "w_uq": I["mla_w_uq"][0], "w_ukv": I["mla_w_ukv"][0],
        "b_w_in": I["b_w_in"][0], "rw_w_up": I["rwkv_w_up"][0], "rw_a_up": I["rwkv_a_up"][0],
        "rw_g_up": I["rwkv_g_up"][0],
    }
    for k in ("C96", "S96", "C32", "S32", "C128", "S128", "Dpos", "Dneg", "Ddiag", "tpl1", "tNm", "pj",
              "m_ab", "m_cd", "m_na", "eye64", "perm32"):
        shared["c_" + k] = cst[k]
    maps = []
    for c in range(8):
        s = c // 4
        xp = I["x_prompt"][2 * c:2 * c + 2].reshape(TP, D)
        xs = I["x_sample"][s]
        xT = np.ascontiguousarray(np.concatenate([xp, xs], 0).T)
        cond = np.stack([I["c_ctx"], I["c"][s]], 0)
        condT = np.ascontiguousarray(cond.T.reshape(KC, 128, 2).transpose(1, 0, 2))
        m = dict(shared)
        m["xT"] = xT
        m["condT"] = condT
        m["ckv_cT"] = np.ascontiguousarray(I["cache_mla_ckv"][s, 0].T)
        m["krope_cT"] = np.ascontiguousarray(I["cache_mla_krope"][s, 0].T)
        m["ret_s0"] = np.ascontiguousarray(I["state_ret"][s, 0])
        m["dk_cT"] = np.ascontiguousarray(I["cache_diff_k"][s, 0].reshape(PAST, 4, 128).transpose(1, 2, 0))
        m["dv_c"] = np.ascontiguousarray(I["cache_diff_v"][s, 0].reshape(PAST, 512))
        m["rw_s0"] = np.ascontiguousarray(I["state_rwkv"][s, 0].transpose(0, 1, 3, 2))
        maps.append(m)
    return maps, pk.cols, pk.n


def layer1(b, own_mod=False):
    p = b.p
    if own_mod:
        b.modulation(1)
    else:
        b.mod_end(1)
    b.norm_mod(1, 0)
    with p.scope("layer1:706"):
        oT = p.sbuf([128, 4, T], BF16, "oT1")
        if b.upto != "rwkv_only":
            diff_attn(b, oT)
            if "odiff" in b.dbg:
                dst = b.dout("dbg_odiff", [128, 4, T])
                b.out_dmas.append(p.dma("pool", dst[:, :, :], oT[:], reads=[oT], sem="dbg_odiff"))
            wout_half(b, 1, 0, oT)
        if b.upto == "diff":
            return
        rwkv(b, oT)
        if "orw" in b.dbg:
            dst = b.dout("dbg_orw", [128, 4, T])
            b.out_dmas.append(p.dma("pool", dst[:, :, :], oT[:], reads=[oT], sem="dbg_orw"))
        wout_half(b, 1, 1, oT)
    if b.upto in ("rwkv", "rwkv_only"):
        return
    b.norm_mod(1, 1)
    conv_ffn(b, 1)
    yo = b.dout("o_yT", [D, T])
    for kc in range(KC):
        b.store(yo[kc * 128:(kc + 1) * 128, :], b.xT[:, kc, :], XK(b.xT, kc, 0, T), ("o_y", kc))


def build_program(pack_cols, npk, dbg=(), upto="all"):
    b = Builder(pack_cols, npk, dbg=dbg, upto=upto)
    b.setup()
    b.modulation(0)
    b.norm_mod(0, 0)
    if "h0" in b.dbg:
        dst = b.dout("dbg_h0", [128, KC, T])
        b.out_dmas.append(b.p.dma("pool", dst[:, :, :], b.hT[:], reads=[b.hT], sem="dbg_h0"))
    if "mod0" in b.dbg:
        dst = b.dout("dbg_mod0", [128, 48, 2])
        b.store(dst[:, :, :], b.mod[0][:], [b.mod[0]], "dbg_mod0")
    if upto != "h0":
        with b.p.scope("build_program:743"):
            oT = b.p.sbuf([128, 4, T], BF16, "oT")
            mla(b, oT)
            if "omla" in b.dbg:
                dst = b.dout("dbg_omla", [128, 4, T])
                b.out_dmas.append(b.p.dma("pool", dst[:, :, :], oT[:], reads=[oT], sem="dbg_omla"))
            if upto != "mla":
                wout_half(b, 0, 0, oT)
                retention(b, oT)
                if "oret" in b.dbg:
                    dst = b.dout("dbg_oret", [128, 4, T])
                    b.out_dmas.append(b.p.dma("pool", dst[:, :, :], oT[:], reads=[oT], sem="dbg_oret"))
                wout_half(b, 0, 1, oT)
        if "xm0" in b.dbg:
            dst = b.dout("dbg_xm0", [128, KC, T])
            b.store(dst[:, :, :], b.xT[:], [b.xT], "dbg_xm0")
        if upto not in ("mla", "mix0"):
            b.norm_mod(0, 1)
            if upto == "l0":
                conv_ffn(b, 0)
            else:
                conv_ffn(b, 0, hook=lambda g: [b.mod_piece(1, 2 * g + i, b.ps[6 + i]) for i in range(2)])
            if "x0" in b.dbg:
                dst = b.dout("dbg_x0", [128, KC, T])
                b.store(dst[:, :, :], b.xT[:], [b.xT], "dbg_x0")
            if upto != "l0":
                layer1(b)
    b.p.emit(final_dma_ops=b.out_dmas)
    return b


def kernel(**inputs):
    I = {k: np.asarray(v) for k, v in inputs.items()}
    maps, cols, npk = build_in_maps(I)
    b = build_program(cols, npk)
    res = run_bass_kernel_spmd(b.nc, maps, core_ids=list(range(8)))
    R = res.results
    B = I["x_prompt"].shape[0]
    y_p = np.zeros((B, SEQ, D), np.float32)
    y_s = np.zeros((2, TS, D), np.float32)
    ckv = np.zeros((B, 1, SEQ, 128), np.float32)
    kro = np.zeros((B, 1, SEQ, 32), np.float32)
    rst = np.zeros((B, 1, 2, 4, 64, 128), np.float32)
    dk = np.zeros((B, 1, SEQ, 4, 2, 64), np.float32)
    dv = np.zeros((B, 1, SEQ, 4, 128), np.float32)
    rws = np.zeros((B, 1, 2, 8, 64, 64), np.float32)
    for c in range(8):
        r = R[c]
        sl = slice(2 * c, 2 * c + 2)
        yT = np.asarray(r["o_yT"])
        y_p[sl] = yT[:, 0:TP].T.reshape(NPS, SEQ, D)
        s_, q = c // 4, c % 4
        y_s[s_, q * 256:(q + 1) * 256] = yT[:, TP + q * 256:TP + (q + 1) * 256].T
        ckv[sl, 0] = np.asarray(r["o_ckvT"]).T.reshape(NPS, SEQ, 128)
        kro[sl, 0] = np.asarray(r["o_kropeT"]).T.reshape(NPS, SEQ, 32)
        rst[sl, 0] = np.asarray(r["o_retst"]).transpose(0, 1, 3, 2, 4)
        dk[sl, 0] = np.asarray(r["o_dkT"]).transpose(2, 0, 1).reshape(NPS, SEQ, 4, 2, 64)
        dv[sl, 0] = np.asarray(r["o_dv"]).reshape(NPS, SEQ, 4, 128)
        rws[sl, 0] = np.asarray(r["o_rwst"]).transpose(0, 1, 2, 4, 3)
    return (y_p, y_s, ckv, kro, rst, dk, dv, rws)


def _rms_rstd(b, ps_s, M, n, nfeat, rt, rstd, legacy=False):
    p = b.p
    if legacy:
        p.act(rt[0:M, 0:n], ps_s[0:M, 0:n], AF.Sqrt, bias=b.eps[0:M, :], scale=1.0 / nfeat, reads=[ps_s, b.eps], writes=[rt])
        p.recip(rstd[0:M, 0:n], rt[0:M, 0:n], reads=[rt], writes=[rstd])
        return
    p.act(rt[0:M, 0:n], ps_s[0:M, 0:n], AF.Ln, bias=b.eps[0:M, :], scale=1.0 / nfeat, reads=[ps_s, b.eps], writes=[rt])
    p.act(rstd[0:M, 0:n], rt[0:M, 0:n], AF.Exp, scale=-0.5, reads=[rt], writes=[rstd])


def mla(b, oT):
    p = b.p
    nc = b.nc
    a_w_in = b.a_w_in
    with p.scope("mla:816"):
        ckvn = p.sbuf([128, 2048], BF16, "ckvn")
        krg = p.sbuf([96, 2048], BF16, "krg")
        sqk = [p.sbuf([96, 512], BF16, f"sqk{i}") for i in range(4)]
        cqn = p.sbuf([128, 2, T], BF16, "cqn")
        vaug = p.sbuf([128, 16, 512], BF16, "vtok")
        C96 = p.sbuf([96, TS], F32, "C96")
        S96 = p.sbuf([96, TS], F32, "S96")
        p.dma("sp", C96[:], b.din("c_C96", [96, TS])[:, :], writes=[C96])
        p.dma("sp", S96[:], b.din("c_S96", [96, TS])[:, :], writes=[S96])
        p.dma("pool", ckvn[:, T:T + PAST], b.din("ckv_cT", [128, PAST])[:, :], writes=[(ckvn, 3)])
        krc = p.sbuf([96, PAST], F32, "krc")
        p.dma("sp", krc[64:96, :], b.din("krope_cT", [32, PAST])[:, :], writes=[krc])
        kn = b.P("kn")
        qn = b.P("qn")
        out_ckv = b.dout("o_ckvT", [128, TP])
        out_kr = b.dout("o_kropeT", [32, TP])

        slotA, wA = b.wload(a_w_in.rearrange("(kc p) n -> p kc n", p=128)[:, :, 0:416], [128, KC, 416])
        with p.scope("mla:837"):
            sqt = [p.sbuf([128, 512], BF16, f"sqt{i}") for i in range(2)]
            rt = p.sbuf([128, 512], F32, "rt")
            rs = p.sbuf([128, 512], F32, "rs")
            ckf = p.sbuf([128, 512], F32, "ckf")
            krf = p.sbuf([96, 512], F32, "krf")
            krf2 = p.sbuf([96, 512], F32, "krf2")
            krb = p.sbuf([96, 512], BF16, "krb")
            t1 = p.sbuf([96, 512], F32, "t1")
            t2 = p.sbuf([96, 512], F32, "t2")
            p.memset("dve", krb[:], 0.0, writes=[krb])

            def krope_block(src_ap, src_reads, c0, n, rope_t0):
                kc_ = c0 // 512
                p.act(sqk[kc_][64:96, 0:n], src_ap, AF.Square, reads=src_reads, writes=[(sqk[kc_], "r")])
                if rope_t0 is None:
                    p.ts("dve", krg[64:96, c0:c0 + n], src_ap, kn[64:96, :], None, ALU.mult,
                         reads=src_reads + [b.prm], writes=[(krg, kc_)])
                else:
                    p.ts("dve", krf2[64:96, 0:n], src_ap, kn[64:96, :], None, ALU.mult,
                         reads=src_reads + [b.prm], writes=[krf2])
                    p.copy("dve", krb[64:96, 0:n], krf2[64:96, 0:n], reads=[krf2], writes=[krb])
                    pp = b.psn()
                    p.mm(pp[0:96, 0:n], b.perm96[0:96, 0:96], krb[:, 0:n], reads=[krb, b.cbf], writes=[pp])
                    p.tt("dve", t1[64:96, 0:n], krf2[64:96, 0:n], C96[64:96, rope_t0:rope_t0 + n], ALU.mult,
                         reads=[krf2, C96], writes=[t1])
                    p.tt("dve", t2[64:96, 0:n], pp[64:96, 0:n], S96[64:96, rope_t0:rope_t0 + n], ALU.mult,
                         reads=[pp, S96], writes=[t2])
                    p.tt("dve", krg[64:96, c0:c0 + n], t1[64:96, 0:n], t2[64:96, 0:n], ALU.add,
                         reads=[t1, t2], writes=[(krg, kc_)])

            for tb, (c0, c1) in enumerate(TBLK):
                n = c1 - c0
                pcq = [b.psn(), b.psn()]
                pss = b.psn()
                for j in range(2):
                    for kc in range(KC):
                        p.mm(pcq[j][:, 0:n], wA[:, kc, j * 128:(j + 1) * 128], b.hT[:, kc, c0:c1],
                             start=(kc == 0), stop=(kc == KC - 1), reads=[slotA, (b.hT, (kc, tb))], writes=[pcq[j]])
                    sq = sqt[j]
                    p.act(sq[:, 0:n], pcq[j][:, 0:n], AF.Square, reads=[pcq[j]], writes=[sq])
                    p.mm(pss[:, 0:n], b.ones, sq[:, 0:n], start=(j == 0), stop=(j == 1), reads=[sq, b.cbf], writes=[pss])
                _rms_rstd(b, pss, 128, n, 256, rt, rs)
                for j in range(2):
                    p.stt(cqn[:, j, c0:c1], pcq[j][:, 0:n], b.P("qnorm")[:, j:j + 1], rs[:, 0:n], ALU.mult, ALU.mult,
                          reads=[pcq[j], rs, b.prm], writes=[(cqn, (j, tb))])
                pck = b.psn()
                pss = b.psn()
                for kc in range(KC):
                    p.mm(pck[:, 0:n], wA[:, kc, 256:384], b.hT[:, kc, c0:c1], start=(kc == 0), stop=(kc == KC - 1),
                         reads=[slotA, (b.hT, (kc, tb))], writes=[pck])
                sq = sqt[0]
                p.act(sq[:, 0:n], pck[:, 0:n], AF.Square, reads=[pck], writes=[sq])
                p.mm(pss[:, 0:n], b.ones, sq[:, 0:n], reads=[sq, b.cbf], writes=[pss])
                _rms_rstd(b, pss, 128, n, 128, rt, rs)
                if tb == 0:
                    p.stt(ckf[:, 0:n], pck[:, 0:n], b.P("kvnorm")[:, 0:1], rs[:, 0:n], ALU.mult, ALU.mult,
                          reads=[pck, rs, b.prm], writes=[ckf])
                    b.store(out_ckv[:, :], ckf[:, 0:n], [ckf], "o_ckv")
                    p.copy("act", ckvn[:, c0:c1], ckf[:, 0:n], reads=[ckf], writes=[(ckvn, tb)])
                else:
                    p.stt(ckvn[:, c0:c1], pck[:, 0:n], b.P("kvnorm")[:, 0:1], rs[:, 0:n], ALU.mult, ALU.mult,
                          reads=[pck, rs, b.prm], writes=[(ckvn, tb)])
                pkr = b.psn()
                for kc in range(KC):
                    p.mm(pkr[0:32, 0:n], wA[:, kc, 384:416], b.hT[:, kc, c0:c1], start=(kc == 0), stop=(kc == KC - 1),
                         reads=[slotA, (b.hT, (kc, tb))], writes=[pkr])
                p.copy("act", krf[64:96, 0:n], pkr[0:32, 0:n], reads=[pkr], writes=[krf])
                if tb == 0:
                    b.store(out_kr[:, :], krf[64:96, 0:n], [krf], "o_kr")
                krope_block(krf[64:96, 0:n], [krf], c0, n, None if tb == 0 else c0 - TP)
            krope_block(krc[64:96, :], [krc], T, PAST, None)

        slotU = p.sbuf([128, 2 * 768 + 1024], BF16, "wU")
        wuq = slotU.t[:, 0:1536].rearrange("p (a b) -> p a b", b=768)
        wukv = slotU.t[:, 1536:2560]
        p.dma("pool", wuq, b.din("w_uq", [256, 768]).rearrange("(kc p) n -> p kc n", p=128), writes=[(slotU, 0)])
        p.dma("pool", wukv, b.din("w_ukv", [128, 1024])[:, :], writes=[(slotU, 1)])
        wukv_h = wukv.rearrange("p (h two e) -> p h two e", two=2, e=64)

        for kt in range(16):
            pv = b.psn()
            pvv = pv.t[:, :].rearrange("p (h e) -> p h e", e=64)
            p.mm(pvv, ckvn[:, kt * 128:(kt + 1) * 128], wukv_h[:, :, 1, :], reads=[(ckvn, kt // 4), (slotU, 1)], writes=[pv])
            p.copy("act" if kt % 2 == 0 else "dve", vaug[:, kt, :], pv[:, :], reads=[pv], writes=[(vaug, kt)])

        with p.scope("mla:930"):
            Qh = [p.sbuf([96, T], BF16, f"Qh{i}") for i in range(2)]
            Kh = [p.sbuf([96, 2048], BF16, f"Kh{i}") for i in range(2)]
            sqt = [p.sbuf([96, 512], BF16, f"sqh{i}") for i in range(2)]
            rt = p.sbuf([96, 512], F32, "rt")
            rs = p.sbuf([96, 512], F32, "rs")
            t1 = p.sbuf([96, 512], F32, "t1")
            t2 = p.sbuf([96, 512], F32, "t2")
            qgb = p.sbuf([96, 512], BF16, "qgb")
            ex = [p.sbuf([128, 512], BF16, f"ex{i}") for i in range(3)]
            rden = p.sbuf([128, 512], F32, "rden")
            exi = 0
            acci = 0
            sc = 96.0 ** -0.5
            cnt = {"ex": 0, "acc": 0}

            def k_block(h, kb):
                Kt = Kh[h % 2]
                c0 = kb * 512
                pk = b.psn()
                p.mm(pk[0:64, :], wukv[:, h * 128:h * 128 + 64], ckvn[:, c0:c0 + 512], reads=[(slotU, 1), (ckvn, kb)], writes=[pk])
                sq = sqk[kb]
                p.act(sq[0:64, :], pk[0:64, :], AF.Square, reads=[pk], writes=[(sq, "n")])
                pss = b.psn()
                p.mm(pss[0:96, :], b.ones[0:96, 0:96], sq[0:96, :], reads=[(sq, "n"), (sq, "r"), b.cbf], writes=[pss])
                _rms_rstd(b, pss, 96, 512, 96, rt, rs)
                p.stt(Kt[0:64, c0:c0 + 512], pk[0:64, :], kn[0:64, :], rs[0:64, :], ALU.mult, ALU.mult,
                      reads=[pk, rs, b.prm], writes=[(Kt, kb)])
                p.tt("pool", Kt[64:96, c0:c0 + 512], krg[64:96, c0:c0 + 512], rs[64:96, :], ALU.mult,
                     reads=[(krg, kb), rs], writes=[(Kt, kb)])

            def q_block(h, tb):
                Q = Qh[h % 2]
                c0, c1 = TBLK[tb]
                pq = b.psn()
                for kc in range(2):
                    p.mm(pq[0:96, :], wuq[:, kc, h * 96:(h + 1) * 96], cqn[:, kc, c0:c1], start=(kc == 0), stop=(kc == 1),
                         reads=[(slotU, 0), (cqn, (kc, tb))], writes=[pq])
                sq = sqt[tb % 2]
                p.act(sq[0:96, :], pq[0:96, :], AF.Square, reads=[pq], writes=[sq])
                pss = b.psn()
                p.mm(pss[0:96, :], b.ones[0:96, 0:96], sq[0:96, :], reads=[sq, b.cbf], writes=[pss])
                _rms_rstd(b, pss, 96, 512, 96, rt, rs)
                if tb == 0:
                    p.stt(Q[:, c0:c1], pq[0:96, :], qn[0:96, :], rs[0:96, :], ALU.mult, ALU.mult,
                          reads=[pq, rs, b.prm], writes=[(Q, tb)])
                else:
                    r0 = c0 - TP
                    p.stt(t1[:, :], pq[0:96, :], qn[0:96, :], C96[:, r0:r0 + 512], ALU.mult, ALU.mult,
                          reads=[pq, C96, b.prm], writes=[t1])
                    p.act(qgb[:, :], pq[0:96, :], AF.Identity, scale=qn[0:96, :], reads=[pq, b.prm], writes=[qgb])
                    pp = b.psn()
                    p.mm(pp[0:96, :], b.perm96[0:96, 0:96], qgb[:, :], reads=[qgb, b.cbf], writes=[pp])
                    p.tt("dve", t2[:, :], pp[0:96, :], S96[:, r0:r0 + 512], ALU.mult, reads=[pp, S96], writes=[t2])
                    p.tt("pool", t1[:, :], t1[:, :], t2[:, :], ALU.add, reads=[t1, t2], writes=[t1])
                    p.tt("dve", Q[:, c0:c1], t1[:, :], rs[0:96, :], ALU.mult, reads=[t1, rs], writes=[(Q, tb)])

            def build_steps(h):
                return [lambda kb=kb: k_block(h, kb) for kb in range(4)] + [lambda tb=tb: q_block(h, tb) for tb in range(3)]

            def attn_steps(h):
                Q = Qh[h % 2]
                Kt = Kh[h % 2]
                jobs = [(a * SEQ, SEQ, [2 * a, 2 * a + 1]) for a in range(NPS)]
                jobs += [(TP + qb * 512, 512, list(range(4, 16))) for qb in range(2)]
                hc, hu = h // 2, h % 2
                lo, hi = (0, 64) if hu == 0 else (64, 128)
                dlo, dhi = (64, 128) if hu == 0 else (0, 64)
                steps = []
                for (q0, nq, kts) in jobs:
                    st = {}

                    def first(st=st):
                        st["acc"] = b.ps[4 + 2 * (cnt["acc"] % 2)]
                        st["accd"] = b.ps[5 + 2 * (cnt["acc"] % 2)]
                        cnt["acc"] += 1
                    for i, kt in enumerate(kts):
                        def tile(i=i, kt=kt, q0=q0, nq=nq, kts=kts, st=st, first=first):
                            def score(k):
                                ktk = kts[k]
                                ps_ = b.psn()
                                p.mm(ps_[:, 0:nq], Kt[:, ktk * 128:(ktk + 1) * 128], Q[:, q0:q0 + nq],
                                     reads=[(Kt, ktk // 4)] + [(Q, t) for t in tbs_of(q0, q0 + nq)], writes=[ps_])
                                st[("ps", k)] = ps_
                            LA = not (len(DBG_KNOB) > 2 and DBG_KNOB[2] == 2)
                            if i == 0:
                                first()
                                score(0)
                                if len(kts) > 1:
                                    score(1)
                                if len(kts) > 2:
                                    score(2)
                            if i + 3 < len(kts):
                                score(i + 3)
                            acc = st["acc"]
                            accd = st["accd"]
                            pscore = st.pop(("ps", i))
                            e = ex[cnt["ex"] % 3]
                            cnt["ex"] += 1
                            p.act(e[:, 0:nq], pscore[:, 0:nq], AF.Exp, scale=sc, reads=[pscore], writes=[e])
                            p.mm(acc[:, 0:nq], vaug[:, kt, hc * 128:(hc + 1) * 128], e[:, 0:nq], start=(i == 0),
                                 stop=(i == len(kts) - 1), reads=[(vaug, kt), e], writes=[acc])
                            p.mm(accd[:, 0:nq], b.ones, e[:, 0:nq], start=(i == 0),
                                 stop=(i == len(kts) - 1), reads=[b.cbf, e], writes=[accd])
                            if i == len(kts) - 1:
                                p.act(rden[lo:hi, 0:nq], accd[lo:hi, 0:nq], AF.Ln, reads=[accd], writes=[rden])
                                p.act(rden[lo:hi, 0:nq], rden[lo:hi, 0:nq], AF.Exp, scale=-1.0, reads=[rden], writes=[rden])
                                p.tt("dve", oT[lo:hi, hc, q0:q0 + nq], acc[lo:hi, 0:nq], rden[lo:hi, 0:nq], ALU.mult,
                                     reads=[acc, rden], writes=[(oT, (hc, hu, q0))])
                        steps.append(tile)
                return steps

            b.ps_mod = 4
            for f in build_steps(0):
                f()
            for h in range(8):
                A = attn_steps(h)
                B = build_steps(h + 1) if h < 7 else []
                bi = 0
                for ai, f in enumerate(A):
                    f()
                    if ai == 3:
                        while bi < len(B):
                            B[bi]()
                            bi += 1
                while bi < len(B):
                    B[bi]()
                    bi += 1


def wout_half(b, l, half, oT):
    p = b.p
    b.ps_mod = 6
    src = b.w_out_in[l][half * 512:(half + 1) * 512].rearrange("(c p) n -> p c n", p=128)
    slot, wv = b.wload(src, [128, 4, 1024])
    mod = b.mod[l]
    for n in range(KC):
        for tb, (c0, c1) in enumerate(TBLK):
            ps = b.psn()
            for c in range(4):
                p.mm(ps[:, :], wv[:, c, n * 128:(n + 1) * 128], oT[:, c, c0:c1], start=(c == 0), stop=(c == 3),
                     reads=[slot, oT], writes=[ps])
            ci = 0 if tb == 0 else 1
            p.stt(b.xT[:, n, c0:c1], ps[:, :], mod[:, 16 + n, ci:ci + 1], b.xT[:, n, c0:c1], ALU.mult, ALU.add,
                  reads=[ps, mod, (b.xT, (n, tb))], writes=[(b.xT, (n, tb))])


def retention(b, oT):
    p = b.p
    a_w = b.a_w_in.rearrange("(kc p) n -> p kc n", p=128)
    with p.scope("retention:1033"):
        G = p.sbuf([128, 4, 1920], BF16, "G")
        lg = p.sbuf([128, 8], F32, "lg")
        lgT = p.sbuf([128, 8], F32, "lgT")
        dec = p.sbuf([128, 2, 4, 2], F32, "decst")
        p.act(lg[:], b.P("retdec"), AF.Sigmoid, reads=[b.prm], writes=[lg])
        p.act(lg[:], lg[:], AF.Ln, reads=[lg], writes=[lg])
        p.ts("dve", lgT[:], lg[:], float(TS + 1), None, ALU.mult, reads=[lg], writes=[lgT])
        nlg = p.sbuf([128, 8], F32, "nlg")
        p.ts("dve", nlg[:], lg[:], -1.0, None, ALU.mult, reads=[lg], writes=[nlg])
        with p.scope("retention:1043"):
            Dp = p.sbuf([128, 1920], F32, "Dp")
            Dn = p.sbuf([128, 1920], F32, "Dn")
            Dd = p.sbuf([128, 1920], F32, "Dd")
            E1 = p.sbuf([128, 1920], F32, "E1")
            E2 = p.sbuf([128, 1920], F32, "E2")
            pj = p.sbuf([128, 2, 2], F32, "pj")
            p.dma("sp", Dp[:], b.din("c_Dpos", [128, 1920])[:, :], writes=[Dp])
            p.dma("sp", Dn[:], b.din("c_Dneg", [128, 1920])[:, :], writes=[Dn])
            p.dma("sp", Dd[:], b.din("c_Ddiag", [128, 1920])[:, :], writes=[Dd])
            p.dma("sp", pj[:], b.din("c_pj", [128, 2, 2])[:, :, :], writes=[pj])
            for h in range(4):
                p.act(E1[:], Dp[:], AF.Exp, scale=lg[:, h:h + 1], reads=[Dp, lg], writes=[E1])
                p.act(E2[:], Dn[:], AF.Exp, scale=lg[:, 4 + h:5 + h], reads=[Dn, lg], writes=[E2])
                p.tt("dve", E1[:], E1[:], E2[:], ALU.mult, reads=[E1, E2], writes=[E1])
                p.tt("dve", G[:, h, :], E1[:], Dd[:], ALU.add, reads=[E1, Dd], writes=[(G, h)])
                for d in range(2):
                    p.act(dec[:, d, h, :], pj[:, d, :], AF.Exp, scale=lg[:, d * 4 + h:d * 4 + h + 1],
                          reads=[pj, lg], writes=[(dec, (d, h))])
        rq = p.sbuf([128, 2, T], BF16, "rq")
        rk = p.sbuf([128, 2, T], BF16, "rk")
        rvt = p.sbuf([128, 12, 512], BF16, "rvt")
        S0 = p.sbuf([128, 2, 2, 128], BF16, "S0")
        tpos = p.sbuf([128, TS], F32, "tpos")
        p.dma("sp", tpos[:], b.din("c_tpl1", [128, TS])[:, :], writes=[tpos])
        p.dma("pool", S0[:], b.din("ret_s0", [2, 4, 64, 128]).rearrange("r (j u) d e -> (u d) r j e", u=2), writes=[S0])
        slotB, wB = b.wload(a_w[:, :, 416:928], [128, KC, 512])
        for tb, (c0, c1) in enumerate(TBLK):
            for j in range(4):
                ps = b.psn()
                for kc in range(KC):
                    p.mm(ps[:, :], wB[:, kc, j * 128:(j + 1) * 128], b.hT[:, kc, c0:c1], start=(kc == 0), stop=(kc == KC - 1),
                         reads=[slotB, (b.hT, (kc, tb))], writes=[ps])
                if j < 2:
                    p.copy("act", rq[:, j, c0:c1], ps[:, :], reads=[ps], writes=[(rq, (j, tb))])
                else:
                    p.ts("dve", rk[:, j - 2, c0:c1], ps[:, :], 0.125, None, ALU.mult, reads=[ps], writes=[(rk, (j - 2, tb))])
        slotC, wC = b.wload(a_w[:, :, 928:1440], [128, KC, 512])
        for tl in range(12):
            ps = b.psn()
            for kc in range(KC):
                p.mm(ps[:, :], b.hT[:, kc, tl * 128:(tl + 1) * 128], wC[:, kc, :], start=(kc == 0), stop=(kc == KC - 1),
                     reads=[slotC, (b.hT, (kc, tl // 4))], writes=[ps])
            p.copy("act" if tl % 2 else "dve", rvt[:, tl, :], ps[:, :], reads=[ps], writes=[(rvt, tl)])
        out_st = b.dout("o_retst", [NPS, 2, 64, 4, 128])
        with p.scope("retention:1090"):
            rkt = p.sbuf([128, 4, 256], BF16, "rkt")
            for tl in range(4):
                ps = b.psn()
                for kc in range(KC):
                    p.mm(ps[:, 0:256], b.hT[:, kc, tl * 128:(tl + 1) * 128], wB[:, kc, 256:512], start=(kc == 0), stop=(kc == KC - 1),
                         reads=[slotB, (b.hT, (kc, 0))], writes=[ps])
                p.ts("dve", rkt[:, tl, :], ps[:, 0:256], 0.125, None, ALU.mult, reads=[ps], writes=[(rkt, tl)])
            kd = [p.sbuf([128, 64], BF16, f"kd{i}") for i in range(4)]
            stt_ = [p.sbuf([64, 4, 128], F32, f"st{i}") for i in range(2)]
            ki = 0
            for a in range(NPS):
                for d in range(2):
                    ps = b.psn()
                    for h in range(4):
                        for tl2 in range(2):
                            tl = 2 * a + tl2
                            k_ = kd[ki % 4]
                            ki += 1
                            p.ts("dve", k_[:], rkt[:, tl, h * 64:(h + 1) * 64], dec[:, d, h, tl2:tl2 + 1], None, ALU.mult,
                                 reads=[(rkt, tl), (dec, (d, h))], writes=[k_])
                            p.mm(ps[0:64, h * 128:(h + 1) * 128], k_[:], rvt[:, tl, h * 128:(h + 1) * 128],
                                 start=(tl2 == 0), stop=(tl2 == 1), reads=[k_, (rvt, tl)], writes=[ps])
                    st = stt_[(a * 2 + d) % 2]
                    p.copy("act", st[:], ps[0:64, :].rearrange("p (h e) -> p h e", e=128), reads=[ps], writes=[st])
                    b.store(out_st[a, d], st[:], [st], ("o_retst", (a * 2 + d) % 2))
        slotD, wD = b.wload(a_w[:, :, 1440:1952], [128, KC, 512])
        with p.scope("retention:1118"):
            ms = [p.sbuf([128, 512], BF16, f"ms{i}") for i in range(2)]
            decr = p.sbuf([128, TS], F32, "decr")
            qd = [p.sbuf([128, TS], BF16, f"qd{i}") for i in range(2)]
            ro = p.sbuf([128, 512], F32, "ro")
            sq = p.sbuf([128, 512], BF16, "sq")
            rt = p.sbuf([128, 512], F32, "rt")
            rs = p.sbuf([128, 512], F32, "rs")
            yn = p.sbuf([128, 512], BF16, "yn")
            sg = p.sbuf([128, 512], BF16, "sg")
            msi = 0
            acci = 0
            pending = [None]
            for h in range(4):
                j, u = h // 2, h % 2
                r0, r1 = u * 64, (u + 1) * 64
                for d in range(2):
                    if d == 0:
                        p.act(decr[:], tpos[:], AF.Exp, scale=lg[:, h:h + 1], reads=[tpos, lg], writes=[decr])
                    else:
                        p.act(decr[:], tpos[:], AF.Exp, scale=nlg[:, 4 + h:5 + h], bias=lgT[:, 4 + h:5 + h],
                              reads=[tpos, nlg, lgT], writes=[decr])
                    p.tt("dve", qd[d][r0:r1, :], rq[r0:r1, j, TP:T], decr[r0:r1, :], ALU.mult,
                         reads=[(rq, (j, 1)), (rq, (j, 2)), decr], writes=[qd[d]])
                jobs = [(a * SEQ, SEQ, [2 * a, 2 * a + 1], False) for a in range(NPS)]
                jobs += [(TP + qb * 512, 512, list(range(4, 12)), True) for qb in range(2)]
                for (q0, nq, sts, init) in jobs:
                    acc = b.ps[6 + acci % 2]
                    acci += 1
                    pend = {}

                    def rscore(k):
                        sk = sts[k]
                        ps_ = b.psn()
                        p.mm(ps_[:, 0:nq], rk[r0:r1, j, sk * 128:(sk + 1) * 128], rq[r0:r1, j, q0:q0 + nq],
                             reads=[(rk, (j, sk // 4))] + [(rq, (j, t)) for t in tbs_of(q0, q0 + nq)], writes=[ps_])
                        pend[k] = ps_
                    rscore(0)
                    if len(sts) > 1:
                        rscore(1)
                    if len(sts) > 2:
                        rscore(2)
                    for i, st_ in enumerate(sts):
                        if i + 3 < len(sts):
                            rscore(i + 3)
                        pscore = pend.pop(i)
                        off = (q0 - st_ * 128) + 896
                        m = ms[msi % 2]
                        msi += 1
                        p.tt("dve", m[:, 0:nq], pscore[:, 0:nq], G[:, h, off:off + nq], ALU.mult, reads=[pscore, (G, h)], writes=[m])
                        p.mm(acc[:, 0:nq], rvt[:, st_, h * 128:(h + 1) * 128], m[:, 0:nq], start=(i == 0),
                             stop=(i == len(sts) - 1 and not init), reads=[(rvt, st_), m], writes=[acc])
                    if init:
                        qoff = q0 - TP
                        p.mm(acc[:, 0:nq], S0[r0:r1, 0, j, :], qd[0][r0:r1, qoff:qoff + nq], start=False, stop=False,
                             reads=[S0, qd[0]], writes=[acc])
                        p.mm(acc[:, 0:nq], S0[r0:r1, 1, j, :], qd[1][r0:r1, qoff:qoff + nq], start=False, stop=True,
                             reads=[S0, qd[1]], writes=[acc])
                    def tail(acc=acc, q0=q0, nq=nq, h=h):
                        p.copy("act", ro[:, 0:nq], acc[:, 0:nq], reads=[acc], writes=[ro])
                        p.act(sq[:, 0:nq], ro[:, 0:nq], AF.Square, reads=[ro], writes=[sq])
                        pss = b.psn()
                        p.mm(pss[:, 0:nq], b.ones, sq[:, 0:nq], reads=[sq, b.cbf], writes=[pss])
                        _rms_rstd(b, pss, 128, nq, 128, rt, rs)
                        p.stt(yn[:, 0:nq], ro[:, 0:nq], b.P("retgn")[:, h:h + 1], rs[:, 0:nq], ALU.mult, ALU.mult,
                              reads=[ro, rs, b.prm], writes=[yn])
                        pg = b.psn()
                        for kc in range(KC):
                            p.mm(pg[:, 0:nq], wD[:, kc, h * 128:(h + 1) * 128], b.hT[:, kc, q0:q0 + nq], start=(kc == 0), stop=(kc == KC - 1),
                                 reads=[slotD] + XK(b.hT, kc, q0, q0 + nq), writes=[pg])
                        p.act(sg[:, 0:nq], pg[:, 0:nq], AF.Silu, reads=[pg], writes=[sg])
                        p.tt("pool", oT[:, h, q0:q0 + nq], sg[:, 0:nq], yn[:, 0:nq], ALU.mult, reads=[sg, yn], writes=[(oT, (h, q0))])
                    if pending[0] is not None:
                        pending[0]()
                    pending[0] = tail
            if pending[0] is not None:
                pending[0]()


def conv_ffn(b, l, hook=None):
    p = b.p
    up = b.ffn_up_in[l].rearrange("(kc p) n -> p kc n", p=128)
    down = b.ffn_down_in[l]
    mod = b.mod[l]
    cw = b.P(f"cw{l}")
    cb = b.P(f"cb{l}")
    with p.scope("conv_ffn:1191"):
        ncw = p.sbuf([128, 44 * 3], F32, "ncw")
        p.ts("dve", ncw[:], cw, -1.0, None, ALU.mult, reads=[b.prm], writes=[ncw])
        actT = [p.sbuf([128, 6, T], BF16, f"actT{i}") for i in range(2)]
        acc = [[p.sbuf([128, T], F32, f"acc{i}{j}") for j in range(2)] for i in range(2)]
        sa = [p.sbuf([128, T], BF16, f"sa{i}") for i in range(2)]
        psets = [(b.psall[:, 0:1536], b.ps[0:3]), (b.psall[:, 1536:3072], b.ps[3:6])]
        upslot = None
        for g6 in range(4):
            nfc = 6 if g6 < 3 else 4
            at = actT[g6 % 2]
            for c in range(nfc):
                fc = g6 * 6 + c
                if fc % 3 == 0:
                    ng = min(3, NFC - fc)
                    upslot = b.ring[b.ring_i % 3]
                    b.ring_i += 1
                    upv = upslot.t[:, 0:2 * KC * 384].rearrange("p (h k n) -> p h k n", h=2, k=KC)
                    p.dma("pool", upv[:, 0, :, 0:ng * 128], up[:, :, fc * 128:(fc + ng) * 128], writes=[(upslot, "a")])
                    p.dma("pool", upv[:, 1, :, 0:ng * 128], up[:, :, DFF + fc * 128:DFF + (fc + ng) * 128], writes=[(upslot, "b")])
                ci3 = fc % 3
                par = fc % 2
                for half in range(2):
                    pview, pbufs = psets[half]
                    ch = half * NFC + fc
                    for tb, (c0, c1) in enumerate(TBLK):
                        for kc in range(KC):
                            p.mm(pbufs[tb][:, :], upv[:, half, kc, ci3 * 128:(ci3 + 1) * 128], b.hT[:, kc, c0:c1],
                                 start=(kc == 0), stop=(kc == KC - 1), reads=[(upslot, "ab"[half]), (b.hT, (kc, tb))], writes=[pbufs[tb]])
                    A = acc[par][half]
                    w0 = cw[:, ch * 3 + 0:ch * 3 + 1]
                    w1 = cw[:, ch * 3 + 1:ch * 3 + 2]
                    w2 = cw[:, ch * 3 + 2:ch * 3 + 3]
                    p.act(A[:], pview, AF.Identity, scale=w1, bias=cb[:, ch:ch + 1], reads=list(pbufs) + [b.prm], writes=[A])
                    p.stt(A[:, 1:T], pview[:, 0:T - 1], w0, A[:, 1:T], ALU.mult, ALU.add, reads=list(pbufs) + [A, b.prm], writes=[A])
                    p.stt(A[:, 0:T - 1], pview[:, 1:T], w2, A[:, 0:T - 1], ALU.mult, ALU.add, reads=list(pbufs) + [A, b.prm], writes=[A])
                    p.stt(A[:, SEQ:2 * SEQ + 1:SEQ], pview[:, SEQ - 1:2 * SEQ:SEQ], ncw[:, ch * 3 + 0:ch * 3 + 1], A[:, SEQ:2 * SEQ + 1:SEQ],
                          ALU.mult, ALU.add, reads=list(pbufs) + [A, ncw], writes=[A])
                    p.stt(A[:, SEQ - 1:2 * SEQ:SEQ], pview[:, SEQ:2 * SEQ + 1:SEQ], ncw[:, ch * 3 + 2:ch * 3 + 3], A[:, SEQ - 1:2 * SEQ:SEQ],
                          ALU.mult, ALU.add, reads=list(pbufs) + [A, ncw], writes=[A])
                s_ = sa[par]
                p.act(s_[:], acc[par][0][:], AF.Silu, reads=[acc[par][0]], writes=[s_])
                p.tt("pool", at[:, c, :], s_[:], acc[par][1][:], ALU.mult, reads=[s_, acc[par][1]], writes=[(at, c)])
            dslot, dv = b.wload(down[g6 * 768:g6 * 768 + nfc * 128].rearrange("(c p) n -> p c n", p=128), [128, nfc, 1024])
            for n in range(KC):
                for tb, (c0, c1) in enumerate(TBLK):
                    ps = b.ps[6 + (n * 3 + tb) % 2]
                    for c in range(nfc):
                        p.mm(ps[:, :], dv[:, c, n * 128:(n + 1) * 128], at[:, c, c0:c1], start=(c == 0), stop=(c == nfc - 1),
                             reads=[dslot, (at, c)], writes=[ps])
                    ci = 0 if tb == 0 else 1
                    p.stt(b.xT[:, n, c0:c1], ps[:, :], mod[:, 40 + n, ci:ci + 1], b.xT[:, n, c0:c1], ALU.mult, ALU.add,
                          reads=[ps, mod, (b.xT, (n, tb))], writes=[(b.xT, (n, tb))])
            if hook is not None:
                hook(g6)


def diff_attn(b, oT):
    p = b.p
    bw = b.b_w_in.rearrange("(kc p) n -> p kc n", p=128)
    c1m = 1.0 - LAM_INIT1
    with p.scope("diff_attn:1255"):
        Qd = p.sbuf([128, 4, T], BF16, "Qd")
        Kd = p.sbuf([128, 4, T + PAST], BF16, "Kd")
        Vd = p.sbuf([128, 16, 512], BF16, "Vd")
        C128 = p.sbuf([128, TS], F32, "C128")
        S128 = p.sbuf([128, TS], F32, "S128")
        p.dma("sp", C128[:], b.din("c_C128", [128, TS])[:, :], writes=[C128])
        p.dma("sp", S128[:], b.din("c_S128", [128, TS])[:, :], writes=[S128])
        dkc = b.din("dk_cT", [4, 128, PAST])
        for h in range(4):
            p.dma("pool", Kd[:, h, T:T + PAST], dkc[h], writes=[(Kd, (h, 3))])
        dvc = b.din("dv_c", [PAST, 512])
        p.dma("pool", Vd[:, 12:16, :], dvc.rearrange("(t p) n -> p t n", p=128), writes=[(Vd, 12), (Vd, 13), (Vd, 14), (Vd, 15)])
        lamt = p.sbuf([128, 4], F32, "lamt")
        lprod = p.sbuf([128, 2, 64], F32, "lprod")
        lam = b.P("lam").rearrange("p (r d) -> p r d", d=64)
        p.tt("dve", lprod[:, 0, :], lam[:, 0, :], lam[:, 1, :], ALU.mult, reads=[b.prm], writes=[lprod])
        p.tt("dve", lprod[:, 1, :], lam[:, 2, :], lam[:, 3, :], ALU.mult, reads=[b.prm, lprod], writes=[lprod])
        p.op("dve", lambda e: e.tensor_reduce(out=lamt[:, 0:2], in_=lprod[:], axis=AX.X, op=ALU.add), reads=[lprod], writes=[lamt])
        p.act(lamt[:, 0:2], lamt[:, 0:2], AF.Exp, reads=[lamt], writes=[lamt])
        p.tt("dve", lamt[:, 2:3], lamt[:, 1:2], lamt[:, 0:1], ALU.subtract, reads=[lamt], writes=[lamt])
        p.ts("dve", lamt[:, 3:4], lamt[:, 2:3], -LAM_INIT1, None, ALU.add, reads=[lamt], writes=[lamt])
        nlam = lamt[:, 3:4]
        epsc = p.sbuf([128, 1], F32, "epsc")
        p.memset("dve", epsc[:], EPS / (c1m * c1m), writes=[epsc])
        out_dk = b.dout("o_dkT", [4, 128, TP])
        out_dv = b.dout("o_dv", [TP, 512])
        with p.scope("diff_attn:1285"):
            sqt = [p.sbuf([128, 512], BF16, f"sq{i}") for i in range(2)]
            rt = p.sbuf([128, 512], F32, "rt")
            rs = p.sbuf([128, 512], F32, "rs")
            t1 = p.sbuf([128, 512], F32, "t1")
            t2 = p.sbuf([128, 512], F32, "t2")
            gb = p.sbuf([128, 512], BF16, "gb")
            kst = [p.sbuf([128, 512], F32, f"kst{i}") for i in range(2)]
            si = 0
            for which in range(2):
                slot, wv = b.wload(bw[:, :, which * 512:(which + 1) * 512], [128, KC, 512])
                gname = "dqn" if which == 0 else "dkn"
                gn = b.P(gname)
                dst = Qd if which == 0 else Kd
                for h in range(4):
                    for tb, (c0, c1) in enumerate(TBLK):
                        ps = b.psn()
                        for kc in range(KC):
                            p.mm(ps[:, :], wv[:, kc, h * 128:(h + 1) * 128], b.hT[:, kc, c0:c1], start=(kc == 0), stop=(kc == KC - 1),
                                 reads=[slot, (b.hT, (kc, tb))], writes=[ps])
                        sq = sqt[si % 2]
                        si += 1
                        p.act(sq[:], ps[:], AF.Square, reads=[ps], writes=[sq])
                        pss = b.psn()
                        p.mm(pss[:], b.bd64, sq[:], reads=[sq, b.cbf], writes=[pss])
                        _rms_rstd(b, pss, 128, 512, 64, rt, rs, legacy=True)
                        if tb == 0:
                            if which == 1:
                                ks = kst[h % 2]
                                p.stt(ks[:], ps[:], gn[:, 0:1], rs[:], ALU.mult, ALU.mult, reads=[ps, rs, b.prm], writes=[ks])
                                b.store(out_dk[h], ks[:], [ks], ("o_dk", h % 2))
                                p.copy("act", dst[:, h, c0:c1], ks[:], reads=[ks], writes=[(dst, (h, tb))])
                            else:
                                p.stt(dst[:, h, c0:c1], ps[:], gn[:, 0:1], rs[:], ALU.mult, ALU.mult, reads=[ps, rs, b.prm], writes=[(dst, (h, tb))])
                        else:
                            r0 = c0 - TP
                            p.stt(t1[:], ps[:], gn[:, 0:1], C128[:, r0:r0 + 512], ALU.mult, ALU.mult, reads=[ps, C128, b.prm], writes=[t1])
                            p.act(gb[:], ps[:], AF.Identity, scale=gn[:, 0:1], reads=[ps, b.prm], writes=[gb])
                            pp = b.psn()
                            p.mm(pp[:], b.perm128, gb[:], reads=[gb, b.cbf], writes=[pp])
                            p.tt("dve", t2[:], pp[:], S128[:, r0:r0 + 512], ALU.mult, reads=[pp, S128], writes=[t2])
                            p.tt("pool", t1[:], t1[:], t2[:], ALU.add, reads=[t1, t2], writes=[t1])
                            p.tt("dve", dst[:, h, c0:c1], t1[:], rs[:], ALU.mult, reads=[t1, rs], writes=[(dst, (h, tb))])
            slot, wv = b.wload(bw[:, :, 1024:1536], [128, KC, 512])
            for tl in range(12):
                ps = b.psn()
                for kc in range(KC):
                    p.mm(ps[:, :], b.hT[:, kc, tl * 128:(tl + 1) * 128], wv[:, kc, :], start=(kc == 0), stop=(kc == KC - 1),
                         reads=[slot, (b.hT, (kc, tl // 4))], writes=[ps])
                if tl < 4:
                    vs = kst[tl % 2]
                    p.copy("act", vs[:], ps[:], reads=[ps], writes=[vs])
                    b.store(out_dv[tl * 128:(tl + 1) * 128, :], vs[:], [vs], ("o_dv", tl % 2))
                    p.copy("dve", Vd[:, tl, :], vs[:], reads=[vs], writes=[(Vd, tl)])
                else:
                    p.copy("act" if tl % 2 else "dve", Vd[:, tl, :], ps[:], reads=[ps], writes=[(Vd, tl)])
        with p.scope("diff_attn:1344"):
            ex = [p.sbuf([128, 512], BF16, f"ex{i}") for i in range(3)]
            r1 = p.sbuf([128, 512], F32, "r1")
            r2 = p.sbuf([128, 512], F32, "r2")
            o1b = [p.sbuf([128, 512], F32, f"o1{i}") for i in range(2)]
            o2 = p.sbuf([128, 512], F32, "o2")
            jobi = 0
            pending = [None]
            sq = p.sbuf([128, 512], BF16, "sq")
            rt = p.sbuf([128, 512], F32, "rt")
            rs = p.sbuf([128, 512], F32, "rs")
            exi = 0
            sci = 0
            for h in range(4):
                jobs = [(a * SEQ, SEQ, [2 * a, 2 * a + 1]) for a in range(NPS)]
                jobs += [(TP + qb * 512, 512, list(range(4, 16))) for qb in range(2)]
                for (q0, nq, kts) in jobs:
                    num = [b.ps[4], b.ps[5]]
                    den = [b.ps[6], b.ps[7]]
                    for c in range(2):
                        ra, rb = c * 64, (c + 1) * 64
                        pend = {}

                        def dscore(k):
                            nonlocal sci
                            ktk = kts[k]
                            ps_ = b.ps[sci % 4]
                            sci += 1
                            p.mm(ps_[:, 0:nq], Kd[ra:rb, h, ktk * 128:(ktk + 1) * 128], Qd[ra:rb, h, q0:q0 + nq],
                                 reads=[(Kd, (h, ktk // 4))] + [(Qd, (h, t)) for t in tbs_of(q0, q0 + nq)], writes=[ps_])
                            pend[k] = ps_
                        dscore(0)
                        if len(kts) > 1:
                            dscore(1)
                        for i, kt in enumerate(kts):
                            if i + 2 < len(kts):
                                dscore(i + 2)
                            pscore = pend.pop(i)
                            e = ex[exi % 3]
                            exi += 1
                            p.act(e[:, 0:nq], pscore[:, 0:nq], AF.Exp, scale=0.125, reads=[pscore], writes=[e])
                            p.mm(num[c][:, 0:nq], Vd[:, kt, h * 128:(h + 1) * 128], e[:, 0:nq], start=(i == 0), stop=(i == len(kts) - 1),
                                 reads=[(Vd, kt), e], writes=[num[c]])
                            p.mm(den[c][:, 0:nq], b.ones, e[:, 0:nq], start=(i == 0), stop=(i == len(kts) - 1),
                                 reads=[b.cbf, e], writes=[den[c]])
                    if DBG_KNOB[1] in (0, 2):
                        p.recip(r1[:, 0:nq], den[0][:, 0:nq], reads=[den[0]], writes=[r1])
                        p.recip(r2[:, 0:nq], den[1][:, 0:nq], reads=[den[1]], writes=[r2])
                    else:
                        p.act(r1[:, 0:nq], den[0][:, 0:nq], AF.Ln, reads=[den[0]], writes=[r1])
                        p.act(r2[:, 0:nq], den[1][:, 0:nq], AF.Ln, reads=[den[1]], writes=[r2])
                        p.act(r1[:, 0:nq], r1[:, 0:nq], AF.Exp, scale=-1.0, reads=[r1], writes=[r1])
                        p.act(r2[:, 0:nq], r2[:, 0:nq], AF.Exp, scale=-1.0, reads=[r2], writes=[r2])
                    o1 = o1b[jobi % 2]
                    jobi += 1
                    p.tt("dve", o1[:, 0:nq], num[0][:, 0:nq], r1[:, 0:nq], ALU.mult, reads=[num[0], r1], writes=[o1])
                    p.tt("dve", o2[:, 0:nq], num[1][:, 0:nq], r2[:, 0:nq], ALU.mult, reads=[num[1], r2], writes=[o2])
                    p.stt(o1[:, 0:nq], o2[:, 0:nq], nlam, o1[:, 0:nq], ALU.mult, ALU.add, reads=[o1, o2, lamt], writes=[o1])

                    def tail(o1=o1, h=h, q0=q0, nq=nq):
                        nonlocal sci
                        p.act(sq[:, 0:nq], o1[:, 0:nq], AF.Square, reads=[o1], writes=[sq])
                        pss = b.ps[sci % 4]
                        sci += 1
                        p.mm(pss[:, 0:nq], b.ones, sq[:, 0:nq], reads=[sq, b.cbf], writes=[pss])
                        p.act(rt[:, 0:nq], pss[:, 0:nq], AF.Ln, bias=epsc[:], scale=1.0 / (128.0 * c1m * c1m), reads=[pss, epsc], writes=[rt])
                        p.act(rs[:, 0:nq], rt[:, 0:nq], AF.Exp, scale=-0.5, reads=[rt], writes=[rs])
                        p.stt(oT[:, h, q0:q0 + nq], o1[:, 0:nq], b.P("dgn")[:, h:h + 1], rs[:, 0:nq], ALU.mult, ALU.mult,
                              reads=[o1, rs, b.prm], writes=[(oT, (h, q0))])
                    if pending[0] is not None:
                        pending[0]()
                    pending[0] = tail
            if pending[0] is not None:
                pending[0]()


CH = 64
NCH = T // CH
SEQS = [(0, 4, False), (4, 4, False), (8, 16, True)]
CDEC = math.exp(-0.5)
DBG_KNOB = [99, 3]


def shift3(b, dst, pview, pbufs, w1, wn, nwn, bias=None, extra=()):
    p = b.p
    rd = list(pbufs) + list(extra)
    if bias is None:
        p.act(dst[:], pview, AF.Identity, scale=w1, reads=rd + [b.prm], writes=[dst])
    else:
        p.act(dst[:], pview, AF.Identity, scale=w1, bias=bias, reads=rd + [b.prm], writes=[dst])
    p.stt(dst[:, 1:T], pview[:, 0:T - 1], wn[0], dst[:, 1:T], ALU.mult, ALU.add, reads=rd + [dst], writes=[dst])
    p.stt(dst[:, 0:T - 1], pview[:, 1:T], wn[1], dst[:, 0:T - 1], ALU.mult, ALU.add, reads=rd + [dst], writes=[dst])
    p.stt(dst[:, SEQ:2 * SEQ + 1:SEQ], pview[:, SEQ - 1:2 * SEQ:SEQ], nwn[0], dst[:, SEQ:2 * SEQ + 1:SEQ], ALU.mult, ALU.add,
          reads=rd + [dst], writes=[dst])
    p.stt(dst[:, SEQ - 1:2 * SEQ:SEQ], pview[:, SEQ:2 * SEQ + 1:SEQ], nwn[1], dst[:, SEQ - 1:2 * SEQ:SEQ], ALU.mult, ALU.add,
          reads=rd + [dst], writes=[dst])


def rwkv(b, oT):
    p = b.p
    bw = b.b_w_in.rearrange("(kc p) n -> p kc n", p=128)
    RW0 = 1536
    psetA = (b.psall[:, 0:1536], b.ps[0:3])
    psetB = (b.psall[:, 1536:3072], b.ps[3:6])
    with p.scope("rwkv:1424"):
        mu = b.P("mu")
        mu1 = p.sbuf([128, 15], F32, "mu1")
        muh = p.sbuf([128, 15], F32, "muh")
        nmuh = p.sbuf([128, 15], F32, "nmuh")
        p.ts("dve", mu1[:], mu, -1.0, 1.0, ALU.mult, ALU.add, reads=[b.prm], writes=[mu1])
        p.ts("dve", muh[:], mu, 0.5, None, ALU.mult, reads=[b.prm], writes=[muh])
        p.ts("dve", nmuh[:], mu, -0.5, None, ALU.mult, reads=[b.prm], writes=[nmuh])
        omka = p.sbuf([128, 4], F32, "omka")
        p.ts("dve", omka[:], b.P("ka"), -1.0, 1.0, ALU.mult, ALU.add, reads=[b.prm], writes=[omka])
        lw_ = p.sbuf([128, 3, 512], BF16, "lora_w")
        wup_in = b.din("rw_w_up", [2, 64, 512])
        aup_in = b.din("rw_a_up", [2, 64, 512])
        p.dma("pool", lw_[:, 0, :], wup_in.rearrange("d l n -> (d l) n"), writes=[(lw_, 0)])
        p.dma("pool", lw_[:, 1, :], aup_in.rearrange("d l n -> (d l) n"), writes=[(lw_, 1)])
        p.dma("pool", lw_[:, 2, :], b.din("rw_g_up", [128, 512])[:, :], writes=[(lw_, 2)])
        masks = p.sbuf([128, 2, 2, 64], BF16, "masks")
        mna = p.sbuf([64, 2, 64], BF16, "mna")
        eye = p.sbuf([64, 64], BF16, "eye")
        p.dma("pool", masks[:, 0, :, :], b.din("c_m_ab", [2, 128, 64]).rearrange("d p c -> p d c"), writes=[(masks, 0)])
        p.dma("pool", masks[:, 1, :, :], b.din("c_m_cd", [2, 128, 64]).rearrange("d p c -> p d c"), writes=[(masks, 1)])
        p.dma("pool", mna[:], b.din("c_m_na", [2, 64, 64]).rearrange("d p c -> p d c"), writes=[mna])
        p.dma("pool", eye[:], b.din("c_eye64", [64, 64])[:, :], writes=[eye])
        onesf = p.sbuf([128, 512], BF16, "onesb")
        p.memset("pool", onesf[:], 1.0, writes=[onesf])
        s0_in = b.din("rw_s0", [2, 8, 64, 64])
        out_st = b.dout("o_rwst", [NPS, 2, 8, 64, 64])
        twd = p.sbuf([128, T], BF16, "twd")
        adb = p.sbuf([128, T], BF16, "adb")
        sgd = p.sbuf([128, T], BF16, "sgd")
        slotL, wL = b.wload(bw[:, :, RW0 + 1536:RW0 + 1920], [128, KC, 384])
        with p.scope("rwkv:1457"):
            tmp = p.sbuf([128, T], F32, "ltmp")
            for i, (dst, fn) in enumerate(((twd, AF.Tanh), (adb, AF.Identity), (sgd, AF.Sigmoid))):
                pview, pbufs = psetA if i % 2 == 0 else psetB
                for tb, (c0, c1) in enumerate(TBLK):
                    for kc in range(KC):
                        p.mm(pbufs[tb][:, :], wL[:, kc, i * 128:(i + 1) * 128], b.hT[:, kc, c0:c1], start=(kc == 0), stop=(kc == KC - 1),
                             reads=[slotL, (b.hT, (kc, tb))], writes=[pbufs[tb]])
                ch = 12 + i
                shift3(b, tmp, pview, pbufs, mu1[:, ch:ch + 1], (muh[:, ch:ch + 1], muh[:, ch:ch + 1]),
                       (nmuh[:, ch:ch + 1], nmuh[:, ch:ch + 1]), extra=[mu1, muh, nmuh])
                p.act(dst[:], tmp[:], fn, reads=[tmp], writes=[dst])
        for j in range(4):
            if DBG_KNOB[0] <= 1 or (DBG_KNOB[0] < 99 and j > 0):
                break
            with p.scope("rwkv:1473"):
                rwkv_pair(b, oT, j, bw, RW0, psetA, psetB, mu1, muh, nmuh, omka, lw_, masks, mna, eye, onesf, twd, adb, sgd,
                          s0_in, out_st)


def rwkv_pair(b, oT, j, bw, RW0, psetA, psetB, mu1, muh, nmuh, omka, lw_, masks, mna, eye, onesb, twd, adb, sgd, s0_in, out_st):
    p = b.p
    R0, R1 = slice(0, 64), slice(64, 128)
    RU = [R0, R1]
    vb = p.sbuf([128, T], BF16, "vb")
    rtile = [p.sbuf([128, T], BF16, f"rtl{d}") for d in range(2)]
    kkt = [p.sbuf([128, T], BF16, f"kkt{d}") for d in range(2)]
    BK = [p.sbuf([128, NCH, 2, CH], BF16, f"BK{d}") for d in range(2)]
    eLend = [p.sbuf([128, NCH], F32, f"eLe{d}") for d in range(2)]
    bsum = p.sbuf([128, T], BF16, "bsum")
    with p.scope("rwkv_pair:1490"):
        rb = p.sbuf([128, T], BF16, "rb")
        kb = p.sbuf([128, T], BF16, "kb")
        kkb = p.sbuf([128, T], BF16, "kkb")
        with p.scope("rwkv_pair:1494"):
            FB = [p.sbuf([128, T], F32, f"f{i}") for i in range(3)]
            slot = b.ring[b.ring_i % 3]
            b.ring_i += 1
            wv = slot.t[:, 0:3 * KC * 128].rearrange("p (i k n) -> p i k n", i=3, k=KC)
            for i in range(3):
                c_ = RW0 + i * 512 + j * 128
                p.dma("pool", wv[:, i, :, :], bw[:, :, c_:c_ + 128], writes=[(slot, i)])
            for i, dstb in enumerate((rb, kb, vb)):
                pview, pbufs = psetA if i % 2 == 0 else psetB
                for tb, (c0, c1) in enumerate(TBLK):
                    for kc in range(KC):
                        p.mm(pbufs[tb][:, :], wv[:, i, kc, :], b.hT[:, kc, c0:c1], start=(kc == 0), stop=(kc == KC - 1),
                             reads=[(slot, i), (b.hT, (kc, tb))], writes=[pbufs[tb]])
                ch = i * 4 + j
                f0 = FB[i]
                f1 = FB[i]
                shift3(b, f0, pview, pbufs, mu1[:, ch:ch + 1], (muh[:, ch:ch + 1], muh[:, ch:ch + 1]),
                       (nmuh[:, ch:ch + 1], nmuh[:, ch:ch + 1]), extra=[mu1, muh, nmuh])
                p.copy("act", dstb[:], f0[:], reads=[f0], writes=[dstb])
                if i == 1:
                    p.ts("dve", f1[:], f0[:], b.P("kk")[:, j:j + 1], None, ALU.mult, reads=[f0, b.prm], writes=[f1])
                    sq = p.sbuf([128, 512], BF16, "sq")
                    rt = p.sbuf([128, 512], F32, "rt")
                    rs = p.sbuf([128, 512], F32, "rs")
                    for tb, (c0, c1) in enumerate(TBLK):
                        p.act(sq[:], f1[:, c0:c1], AF.Square, reads=[f1], writes=[sq])
                        pss = b.ps[6 + tb % 2]
                        p.mm(pss[:], b.bd64, sq[:], reads=[sq, b.cbf], writes=[pss])
                        _rms_rstd(b, pss, 128, 512, 1.0, rt, rs)
                        p.tt("dve", kkb[:, c0:c1], f1[:, c0:c1], rs[:], ALU.mult, reads=[f1, rs], writes=[(kkb, tb)])
        with p.scope("rwkv_pair:C"):
            TS_ = []
            for d in range(2):
                TS_.append(dict(fs=p.sbuf([128, 512], F32, "fs"), fa=p.sbuf([128, 512], F32, "fa"), fe=p.sbuf([128, 512], F32, "fe"),
                                fg=p.sbuf([128, 512], F32, "fg"), fkd=p.sbuf([128, 512], BF16, "fkd"), fb=p.sbuf([128, 512], BF16, "fb"),
                                gs=p.sbuf([128, 8], F32, "gs")))
            rrk = b.P("rrk")
            ka = b.P("ka")

            def c_block(d, tb):
                t_ = TS_[d]
                fs, fa, fe, fg, fkd, fb, gs = t_["fs"], t_["fa"], t_["fe"], t_["fg"], t_["fkd"], t_["fb"], t_["gs"]
                DR = RU[d]
                c0, c1 = TBLK[tb]
                ch0 = c0 // CH
                ps1 = b.ps[6 - 2 * d]
                ps2 = b.ps[7 - 2 * d]
                fgv = fg.t[:, :].rearrange("p (c t) -> p c t", t=CH)
                fdv = fa.t[:, :].rearrange("p (c t) -> p c t", t=CH)
                fev = fe.t[:, :].rearrange("p (c t) -> p c t", t=CH)
                sg_ = -CDEC if d == 0 else CDEC
                ecol = CH - 1 if d == 0 else 0
                ops = []
                A_ = ops.append
                A_(lambda: p.mm(ps1[:], lw_[DR, 0, j * 128:(j + 1) * 128], twd[DR, c0:c1], reads=[(lw_, 0), twd], writes=[ps1]))
                A_(lambda: p.act(fs[:], ps1[:], AF.Sigmoid, bias=b.P("w0")[:, d * 4 + j:d * 4 + j + 1], scale=1.0, reads=[ps1, b.prm], writes=[fs]))
                A_(lambda: p.mm(ps2[:], lw_[DR, 1, j * 128:(j + 1) * 128], adb[DR, c0:c1], reads=[(lw_, 1), adb], writes=[ps2]))
                A_(lambda: p.act(fa[:], ps2[:], AF.Sigmoid, bias=b.P("a0")[:, d * 4 + j:d * 4 + j + 1], scale=1.0, reads=[ps2, b.prm], writes=[fa]))
                A_(lambda: p.ts("dve", fe[:], fa[:], ka[:, j:j + 1], omka[:, j:j + 1], ALU.mult, ALU.add, reads=[fa, b.prm, omka], writes=[fe]))
                A_(lambda: p.tt("dve", fkd[:], kb[:, c0:c1], fe[:], ALU.mult, reads=[kb, fe], writes=[fkd]))
                A_(lambda: p.stt(fe[:], rb[:, c0:c1], rrk[:, j:j + 1], fkd[:], ALU.mult, ALU.mult, reads=[rb, fkd, b.prm], writes=[fe]))
                A_(lambda: p.tt("pool", bsum[:, c0:c1], bsum[:, c0:c1], fe[:], ALU.add, reads=[(bsum, tb), fe], writes=[(bsum, tb)]))
                A_(lambda: p.tt("dve", fb[:], fa[:], kkb[:, c0:c1], ALU.mult, reads=[fa, (kkb, tb)], writes=[fb]))
                A_(lambda: p.op("dve", lambda e: e.tensor_tensor_scan(out=fg[:], data0=onesb[:, 0:512], data1=fs[:], initial=0.0,
                                                                      op0=ALU.mult, op1=ALU.add), reads=[fs, onesb], writes=[fg]))
                if d == 0:
                    A_(lambda: p.memset("dve", gs[:, 0:1], 0.0, writes=[gs]))
                    A_(lambda: p.copy("dve", gs[:, 1:8], fg[:, CH - 1:512 - 1:CH], reads=[fg], writes=[gs]))
                    A_(lambda: p.tt("dve", fdv, fgv, gs[:, :].unsqueeze(2).to_broadcast([128, 8, CH]), ALU.subtract, reads=[fg, gs], writes=[fa]))
                else:
                    A_(lambda: p.tt("dve", fe[:], fg[:], fs[:], ALU.subtract, reads=[fg, fs], writes=[fe]))
                    A_(lambda: p.tt("dve", fdv, fev, fgv[:, :, CH - 1:CH].to_broadcast([128, 8, CH]), ALU.subtract, reads=[fg, fe], writes=[fa]))
                A_(lambda: p.act(fe[:], fa[:], AF.Exp, scale=sg_, reads=[fa], writes=[fe]))
                A_(lambda: p.tt("dve", rtile[d][:, c0:c1], rb[:, c0:c1], fe[:], ALU.mult, reads=[rb, fe], writes=[(rtile[d], tb)]))
                A_(lambda: p.copy("dve", eLend[d][:, ch0:ch0 + 8], fe[:, ecol:512:CH], reads=[fe], writes=[(eLend[d], tb)]))
                A_(lambda: p.act(fe[:], fa[:], AF.Exp, scale=-sg_, reads=[fa], writes=[fe]))
                A_(lambda: p.tt("dve", BK[d][:, ch0:ch0 + 8, 0, :], fb.t[:, :].rearrange("p (c t) -> p c t", t=CH), fev, ALU.mult,
                                reads=[fb, fe], writes=[(BK[d], (tb, 0))]))
                A_(lambda: p.tt("pool", BK[d][:, ch0:ch0 + 8, 1, :], fkd.t[:, :].rearrange("p (c t) -> p c t", t=CH), fev, ALU.mult,
                                reads=[fkd, fe], writes=[(BK[d], (tb, 1))]))
                A_(lambda: p.tt("dve", fa[:], fa[:], fs[:], ALU.subtract if d == 0 else ALU.add, reads=[fa, fs], writes=[fa]))
                A_(lambda: p.act(fe[:], fa[:], AF.Exp, scale=sg_, reads=[fa], writes=[fe]))
                A_(lambda: p.tt("dve", kkt[d][:, c0:c1], kkb[:, c0:c1], fe[:], ALU.mult, reads=[(kkb, tb), fe], writes=[(kkt[d], tb)]))
                return ops

            p.memset("pool", bsum[:], 0.0, writes=[bsum])
            for tb in range(3):
                o0, o1 = c_block(0, tb), c_block(1, tb)
                for i in range(max(len(o0), len(o1))):
                    if i < len(o0):
                        o0[i]()
                    if i < len(o1):
                        o1[i]()
    if DBG_KNOB[0] <= 2:
        return
    Zall = p.sbuf([128, NCH, 2, CH], BF16, "Zall")
    y = p.sbuf([128, T], BF16, "y")
    p.memset("pool", y[:], 0.0, writes=[y])
    for g in range(NCH // 8):
        E = 6
        pvu = b.psall[0:64, E * 512:(E + 2) * 512].rearrange("p (u c v) -> p u c v", u=2, c=8)
        for u in range(2):
            for cc in range(8):
                ch = g * 8 + cc
                p.mm(pvu[:, u, cc, :], vb[RU[u], ch * CH:(ch + 1) * CH], b.ident[RU[u], u * 64:(u + 1) * 64],
                     reads=[vb, b.cbf], writes=[b.ps[E + u]])
        p.copy("act" if g % 2 else "dve", Zall[64:128, g * 8:(g + 1) * 8, :, :].rearrange("p c u v -> p u c v"), pvu,
               reads=[b.ps[E], b.ps[E + 1]], writes=[(Zall, ("v", g))])
    if DBG_KNOB[0] == 3 and j == 0:
        dz = b.dout("dbg_Z", [128, NCH, 2, CH])
        b.out_dmas.append(p.dma("pool", dz[64:128, :, :, :], Zall[64:128, :, :, :], reads=[Zall], sem="dbg_Z"))
        dvb = b.dout("dbg_vb", [128, T])
        b.out_dmas.append(p.dma("pool", dvb[:, :], vb[:], reads=[vb], sem="dbg_vb"))
    if DBG_KNOB[0] <= 3:
        return
    with p.scope("rwkv_pair:1613"):
        bufsets = []
        for _ in range(4):
            st = {}
            st["Hf"] = p.sbuf([128, 64], F32, "Hf")
            st["Hb"] = p.sbuf([128, 64], BF16, "Hb")
            st["ht"] = p.sbuf([128, 64], F32, "ht")
            for nm, shp in (("NtB", [128, 2, CH]), ("CDt", [128, 2, CH]), ("BKt", [128, 2, CH]), ("Pm", [64, 2, CH])):
                st[nm] = [p.sbuf(shp, BF16, nm) for _ in range(2)]
            st["Ntp"] = [p.sbuf([64, 2, CH], BF16, "Ntp") for _ in range(2)]
            st["Nap"] = [p.sbuf([64, 2, CH], BF16, "Nap") for _ in range(2)]
            st["Pp"] = p.sbuf([64, 2, CH], BF16, "Pp")
            st["Xs"] = p.sbuf([64, 2, CH], BF16, "Xs")
            st["X2"] = p.sbuf([64, 2, CH], F32, "X2")
            bufsets.append(st)

        def make_chains(seq_ids):
            chains = []
            for d in range(2):
                for si in seq_ids:
                    cs, n, init = SEQS[si]
                    order = list(range(cs, cs + n)) if d == 0 else list(range(cs + n - 1, cs - 1, -1))
                    st = dict(bufsets[len(chains)])
                    st.update(d=d, si=si, order=order, init=init)
                    if init:
                        p.dma("sp", st["Hf"][:], s0_in[d, 2 * j:2 * j + 2].rearrange("u k v -> (u k) v"), writes=[st["Hf"]])
                    else:
                        p.memset("dve", st["Hf"][:], 0.0, writes=[st["Hf"]])
                    p.copy("act", st["Hb"][:], st["Hf"][:], reads=[st["Hf"]], writes=[st["Hb"]])
                    chains.append(st)
            return chains

        psi = [0]

        def nps():
            psi[0] += 1
            return b.ps[psi[0] % 8]

        pri = [0]

        def npair():
            pri[0] += 1
            return 2 * (pri[0] % 4)

        def bc(ap):
            return ap.unsqueeze(1).to_broadcast([ap.shape[0], 2, CH])

        def pre_stages(st, k):
            d = st["d"]
            ch = st["order"][k]
            cc = slice(ch * CH, (ch + 1) * CH)
            tb = (ch * CH) // 512
            par = k % 2
            NtB, CDt, BKt, Pm = st["NtB"][par], st["CDt"][par], st["BKt"][par], st["Pm"][par]
            Ntp, Nap = st["Ntp"], st["Nap"]
            bkr = [(BK[d], (tb, 0)), (BK[d], (tb, 1))]

            def s0():
                E = npair()
                pe2 = [b.ps[E], b.ps[E + 1]]
                pu3 = b.psall[:, E * 512:(E + 2) * 512].rearrange("p (u x) -> p u x", u=2)
                for u in range(2):
                    bku = BK[d][RU[u], ch, :, :].rearrange("p a t -> p (a t)")
                    p.mm(pu3[:, u, 0:64], bku, kkt[d][RU[u], cc], reads=bkr + [(kkt[d], tb)], writes=[pe2[u]])
                    p.mm(pu3[0:64, u, 64:128], kkt[d][RU[u], cc], BK[d][RU[u], ch, 0, :], reads=bkr + [(kkt[d], tb)], writes=[pe2[u]])
                    p.mm(pu3[:, u, 128:192], bku, rtile[d][RU[u], cc], reads=bkr + [(rtile[d], tb)], writes=[pe2[u]])
                p.tt("dve", NtB[:], pu3[:, :, 0:64], bc(masks[:, 0, d, :]), ALU.mult, reads=pe2 + [(masks, 0)], writes=[NtB])
                p.tt("dve", Nap[0][:], pu3[0:64, :, 64:128], bc(mna[:, d, :]), ALU.mult, reads=pe2 + [mna], writes=[Nap[0]])
                p.tt("dve", CDt[:], pu3[:, :, 128:192], bc(masks[:, 1, d, :]), ALU.mult, reads=pe2 + [(masks, 1)], writes=[CDt])
                p.tt("pool", st["Pp"][:], NtB[0:64, :, :], bc(eye[:, :]), ALU.add, reads=[NtB, eye], writes=[st["Pp"]])
            yield s0

            for i in range(5):
                def lv(i=i):
                    Nt_i = NtB[0:64, :, :] if i == 0 else Ntp[i % 2]
                    Nt_r = NtB if i == 0 else Ntp[i % 2]
                    Na_i = Nap[i % 2]
                    Na_n = Nap[(i + 1) % 2]
                    pq = nps()
                    pqv = pq.t[0:64, 0:128].rearrange("p (u t) -> p u t", u=2)
                    for u in range(2):
                        p.mm(pqv[:, u, :], Nt_i[:, u, :], Na_i[:, u, :], reads=[Nt_r, Na_i], writes=[pq])
                    p.copy("act", Na_n[:], pqv, reads=[pq], writes=[Na_n])
                    if i < 4:
                        Nt_n = Ntp[(i + 1) % 2]
                        pq2 = nps()
                        pq2v = pq2.t[0:64, 0:128].rearrange("p (u t) -> p u t", u=2)
                        for u in range(2):
                            p.mm(pq2v[:, u, :], Na_i[:, u, :], Nt_i[:, u, :], reads=[Nt_r, Na_i], writes=[pq2])
                        p.copy("act", Nt_n[:], pq2v, reads=[pq2], writes=[Nt_n])
                yield lv

                def pu(i=i):
                    Na_n = Nap[(i + 1) % 2]
                    Pin = st["Pp"] if i % 2 == 0 else Pm
                    Pout = Pm if i % 2 == 0 else st["Pp"]
                    pq = nps()
                    pqv = pq.t[0:64, 0:128].rearrange("p (u t) -> p u t", u=2)
                    for u in range(2):
                        p.mm(pqv[:, u, :], Na_n[:, u, :], Pin[:, u, :], reads=[Na_n, Pin], writes=[pq])
                    p.tt("dve", Pout[:], Pin[:], pqv, ALU.add, reads=[Pin, pq], writes=[Pout])
                yield pu

            def s5():
                E = npair()
                pe2 = [b.ps[E], b.ps[E + 1]]
                pu3 = b.psall[:, E * 512:(E + 2) * 512].rearrange("p (u x) -> p u x", u=2)
                for u in range(2):
                    bku = BK[d][RU[u], ch, :, :].rearrange("p a t -> p (a t)")
                    p.mm(pu3[:, u, 0:64], bku, b.ident[RU[u], u * 64:(u + 1) * 64], reads=bkr + [b.cbf], writes=[pe2[u]])
                p.copy("act", BKt[:], pu3[:, :, 0:64], reads=pe2, writes=[BKt])
            yield s5

        def seq_stages(st, k):
            d = st["d"]
            ch = st["order"][k]
            cc = slice(ch * CH, (ch + 1) * CH)
            tb = (ch * CH) // 512
            par = k % 2
            NtB, CDt, BKt, Pm = st["NtB"][par], st["CDt"][par], st["BKt"][par], st["Pm"][par]
            Hf, Hb, Xs, ht = st["Hf"], st["Hb"], st["Xs"], st["ht"]
            zv = (Zall, ("v", ch // 8))
            zu = (Zall, ("u", ch))

            def s1():
                E = npair()
                pe2 = [b.ps[E], b.ps[E + 1]]
                pu3 = b.psall[0:64, E * 512:(E + 2) * 512].rearrange("p (u x) -> p u x", u=2)
                px = b.ps[npair()]
                pxv = px.t[0:64, 0:128].rearrange("p (u t) -> p u t", u=2)
                for u in range(2):
                    p.mm(pu3[:, u, 0:64], kkt[d][RU[u], cc], Hb[RU[u], :], reads=[(kkt[d], tb), Hb], writes=[pe2[u]])
                for u in range(2):
                    p.mm(pxv[:, u, :], NtB[64:128, u, :], Zall[64:128, ch, u, :], reads=[NtB, zv], writes=[px])
                p.copy("act", st["X2"][:], pxv, reads=[px], writes=[st["X2"]])
                p.tt("dve", Xs[:], pu3[:, :, 0:64], st["X2"][:], ALU.add, reads=pe2 + [st["X2"]], writes=[Xs])
            yield s1

            def s2():
                pu_ = nps()
                puv = pu_.t[0:64, 0:128].rearrange("p (u t) -> p u t", u=2)
                for u in range(2):
                    p.mm(puv[:, u, :], Pm[:, u, :], Xs[:, u, :], reads=[Pm, Xs], writes=[pu_])
                p.ts("dve", Zall[0:64, ch, :, :], puv, -1.0, None, ALU.mult, reads=[pu_], writes=[zu])
            yield s2

            def s3():
                py = nps()
                ph = nps()
                for u in range(2):
                    p.mm(py[RU[u], 0:CH], Hb[RU[u], :], rtile[d][RU[u], cc], start=True, stop=False, reads=[Hb, (rtile[d], tb)], writes=[py])
                    p.mm(py[RU[u], 0:CH], Zall[:, ch, u, :], CDt[:, u, :], start=False, stop=True, reads=[zv, zu, CDt], writes=[py])
                    p.mm(ph[RU[u], 0:CH], BKt[:, u, :], Zall[:, ch, u, :], reads=[BKt, zv, zu], writes=[ph])
                p.tt("dve", y[:, cc], y[:, cc], py[:, 0:CH], ALU.add, reads=[(y, ch), py], writes=[(y, ch)])
                p.tt("dve", ht[:], Hf[:], ph[:, 0:CH], ALU.add, reads=[Hf, ph], writes=[ht])
                p.ts("dve", Hf[:], ht[:], eLend[d][:, ch:ch + 1], None, ALU.mult, reads=[ht, (eLend[d], tb)], writes=[Hf])
                p.copy("act", Hb[:], Hf[:], reads=[Hf], writes=[Hb])
                if k == len(st["order"]) - 1 and not st["init"]:
                    a = st["si"]
                    b.store(out_st[a, d, 2 * j:2 * j + 2].rearrange("u k v -> (u k) v"), Hf[:], [Hf], ("o_rwst", a, d))
            yield s3

        for seq_ids in ((0, 1), (2,)):
            chains = make_chains(seq_ids)
            maxk = max(len(st["order"]) for st in chains)
            for k in range(-1, maxk):
                if DBG_KNOB[0] == 4 and k >= 0:
                    break
                if DBG_KNOB[0] == 5 and k >= 1:
                    break
                gens = []
                for st in chains:
                    n = len(st["order"])
                    if 0 <= k + 1 < n:
                        gens.append(pre_stages(st, k + 1))
                    if 0 <= k < n:
                        gens.append(seq_stages(st, k))
                active = [iter(g) for g in gens]
                while active:
                    nxt = []
                    for it in active:
                        try:
                            f = next(it)
                        except StopIteration:
                            continue
                        f()
                        nxt.append(it)
                    active = nxt
    if DBG_KNOB[0] <= 6:
        return
    with p.scope("rwkv_pair:1807"):
        sq = p.sbuf([128, 512], BF16, "sq")
        rt = p.sbuf([128, 512], F32, "rt")
        rs = p.sbuf([128, 512], F32, "rs")
        yn = p.sbuf([128, 512], F32, "yn")
        bo = p.sbuf([128, 512], F32, "bo")
        for tb, (c0, c1) in enumerate(TBLK):
            p.act(sq[:], y[:, c0:c1], AF.Square, reads=[y], writes=[sq])
            pss = b.psn()
            p.mm(pss[:], b.bd64, sq[:], reads=[sq, b.cbf], writes=[pss])
            _rms_rstd(b, pss, 128, 512, 64, rt, rs)
            p.stt(yn[:], y[:, c0:c1], b.P("rgn")[:, j:j + 1], rs[:], ALU.mult, ALU.mult, reads=[y, rs, b.prm], writes=[yn])
            psb = b.psn()
            p.mm(psb[:], b.bd64, bsum[:, c0:c1], reads=[(bsum, tb), b.cbf], writes=[psb])
            p.tt("dve", bo[:], psb[:], vb[:, c0:c1], ALU.mult, reads=[psb, vb], writes=[bo])
            p.tt("pool", yn[:], yn[:], bo[:], ALU.add, reads=[yn, bo], writes=[yn])
            psg = b.psn()
            p.mm(psg[:], lw_[:, 2, j * 128:(j + 1) * 128], sgd[:, c0:c1], reads=[(lw_, 2), sgd], writes=[psg])
            p.tt("dve", oT[:, j, c0:c1], yn[:], psg[:], ALU.mult, reads=[yn, psg], writes=[(oT, (j, tb))])
```

```python
import math
import numpy as np
import concourse.bass as bass
import concourse.mybir as mybir
from concourse.bass_utils import run_bass_kernel_spmd

F32 = mybir.dt.float32
BF16 = mybir.dt.bfloat16
AF = mybir.ActivationFunctionType
ALU = mybir.AluOpType
AX = mybir.AxisListType

ENGS = ["pe", "act", "dve", "pool", "sp"]


class Buf:
    def __init__(self, t, name):
        self.t = t
        self.name = name
        self.writers = {}
        self.readers = {}

    def __getitem__(self, idx):
        return self.t[idx]


class Op:
    __slots__ = ("eng", "idx", "fn", "deps", "is_dma", "dsem", "dval", "signal",
                 "sigval", "waits", "snap", "gidx")

    def __init__(self, eng, fn):
        self.eng = eng
        self.fn = fn
        self.deps = {}
        self.is_dma = False
        self.dsem = None
        self.dval = 0
        self.signal = False
        self.sigval = 0
        self.waits = []
        self.snap = None


def _norm(acc):
    out = []
    for a in acc:
        if a is None:
            continue
        if isinstance(a, Buf):
            out.append((a, None))
        else:
            out.append((a[0], a[1]))
    return out


class Prog:
    def __init__(self, nc):
        self.nc = nc
        self.ops = {e: [] for e in ENGS}
        self.allops = []
        self.dma_cnt = {}
        self.keymap = {}
        self.free_sems = []
        self.sem_q = {}
        self.ctx = []
        self.nbuf = 0
        self.dmas_since_bar = []
        self.bar_labels = []

    def sbuf(self, shape, dt, name=None):
        self.nbuf += 1
        name = (name or "sb") + f"_{self.nbuf}"
        cm = self.nc.sbuf_tensor(name, list(shape), dt)
        t = cm.__enter__()
        self.ctx.append(cm)
        return Buf(t, name)

    def psum(self, shape, dt, name=None):
        self.nbuf += 1
        name = (name or "ps") + f"_{self.nbuf}"
        cm = self.nc.psum_tensor(name, list(shape), dt)
        t = cm.__enter__()
        self.ctx.append(cm)
        return Buf(t, name)

    def _track(self, op, reads, writes):
        def add(d, raw):
            if d is op:
                return
            if raw or d not in op.deps:
                op.deps[d] = raw or op.deps.get(d, False)
        for b, k in _norm(reads):
            if k is None:
                for w in b.writers.values():
                    add(w, True)
            else:
                w = b.writers.get(k)
                if w is not None:
                    add(w, True)
                w = b.writers.get(None)
                if w is not None:
                    add(w, True)
            b.readers.setdefault(k, []).append(op)
        for b, k in _norm(writes):
            if k is None:
                for w in b.writers.values():
                    add(w, False)
                for rl in b.readers.values():
                    for r in rl:
                        add(r, False)
                b.writers = {None: op}
                b.readers = {}
            else:
                for kk in (k, None):
                    w = b.writers.get(kk)
                    if w is not None:
                        add(w, False)
                    for r in b.readers.get(kk, ()):
                        add(r, False)
                b.writers[k] = op
                b.readers[k] = []

    def op(self, eng, fn, reads=(), writes=()):
        o = Op(eng, fn)
        o.idx = len(self.ops[eng])
        o.gidx = len(self.allops)
        self.ops[eng].append(o)
        self.allops.append(o)
        self._track(o, reads, writes)
        return o

    def dma(self, q, out, in_, reads=(), writes=(), sem=None, **kw):
        def fn(e, out=out, in_=in_, kw=kw):
            return e.dma_start(out=out, in_=in_, **kw)
        o = self.op(q, fn, reads, writes)
        o.is_dma = True
        if sem is None:
            b, k = _norm(writes)[0]
            sem = (b.name, k)
        sem = (q, sem)
        if sem not in self.keymap:
            fl = [i for i in self.free_sems if self.sem_q[i] == q]
            if fl:
                self.free_sems.remove(fl[0])
                self.keymap[sem] = fl[0]
            else:
                self.keymap[sem] = len(self.dma_cnt)
                self.dma_cnt[self.keymap[sem]] = 0
                self.sem_q[self.keymap[sem]] = q
        sem = self.keymap[sem]
        o.dsem = sem
        self.dma_cnt[sem] += 1
        o.dval = 16 * self.dma_cnt[sem]
        self.dmas_since_bar.append(o)
        return o

    def barrier(self):
        if getattr(self, "_bar_mark", -1) == len(self.allops):
            return
        last = [self.ops[e][-1] for e in ENGS if self.ops[e]]
        dm = list(self.dmas_since_bar)
        self.dmas_since_bar = []
        self.free_sems = sorted(set(self.free_sems) | set(self.keymap.values()))
        self.keymap = {}
        for e in ENGS:
            o = self.op(e, lambda en: en.nop())
            for d in last + dm:
                if d is not o:
                    o.deps[d] = True
        self._bar_mark = len(self.allops)

    class _Scope:
        def __init__(self, p, name=None):
            self.p = p
            self.name = name

        def __enter__(self):
            self.n = len(self.p.ctx)
            return self

        def __exit__(self, *a):
            self.p.bar_labels.append(self.name)
            self.p.barrier()
            while len(self.p.ctx) > self.n:
                self.p.ctx.pop().__exit__(None, None, None)
            return False

    def scope(self, name=None):
        return Prog._Scope(self, name)

    def mm(self, out, lhsT, rhs, start=True, stop=True, reads=(), writes=()):
        return self.op("pe", lambda e: e.matmul(out, lhsT, rhs, start=start, stop=stop), reads, writes)

    def act(self, out, in_, func, reads=(), writes=(), **kw):
        return self.op("act", lambda e: e.activation(out=out, in_=in_, func=func, **kw), reads, writes)

    def tt(self, eng, out, in0, in1, op, reads=(), writes=()):
        return self.op(eng, lambda e: e.tensor_tensor(out=out, in0=in0, in1=in1, op=op), reads, writes)

    def ts(self, eng, out, in0, s1, s2, op0, op1=None, reads=(), writes=()):
        if op1 is None:
            return self.op(eng, lambda e: e.tensor_scalar(out=out, in0=in0, scalar1=s1, scalar2=None, op0=op0),
                           reads, writes)
        return self.op(eng, lambda e: e.tensor_scalar(out=out, in0=in0, scalar1=s1, scalar2=s2, op0=op0, op1=op1),
                       reads, writes)

    def stt(self, out, in0, scalar, in1, op0, op1, reads=(), writes=()):
        return self.op("dve", lambda e: e.scalar_tensor_tensor(out=out, in0=in0, scalar=scalar, in1=in1,
                                                               op0=op0, op1=op1), reads, writes)

    def copy(self, eng, out, in_, reads=(), writes=()):
        if eng == "act":
            return self.op(eng, lambda e: e.copy(out=out, in_=in_), reads, writes)
        return self.op(eng, lambda e: e.tensor_copy(out=out, in_=in_), reads, writes)

    def memset(self, eng, ap, val, writes=()):
        return self.op(eng, lambda e: e.memset(ap, val), (), writes)

    def recip(self, out, in_, reads=(), writes=()):
        return self.op("dve", lambda e: e.reciprocal(out=out, in_=in_), reads, writes)

    def emit(self, final_dma_ops=()):
        nc = self.nc
        obs_eng = {e: {p: -1 for p in ENGS} for e in ENGS}
        obs_dma = {e: {} for e in ENGS}
        for o in self.allops:
            oe = obs_eng[o.eng]
            od = obs_dma[o.eng]
            need = {}
            dneed = {}
            for d, raw in o.deps.items():
                if (not d.is_dma) and d.eng == o.eng and o.eng == "pe":
                    continue
                if d.is_dma:
                    if od.get(d.dsem, 0) < d.dval:
                        if dneed.get(d.dsem, (0, None))[0] < d.dval:
                            dneed[d.dsem] = (d.dval, d)
                else:
                    if oe[d.eng] < d.idx:
                        if d.eng not in need or need[d.eng].idx < d.idx:
                            need[d.eng] = d
            for pe_, d in need.items():
                if oe[pe_] >= d.idx:
                    continue
                d.signal = True
                o.waits.append(("eng", d))
                oe[pe_] = max(oe[pe_], d.idx)
                if d.snap is not None:
                    se, sd = d.snap
                    for k, v in se.items():
                        if k != o.eng and oe[k] < v:
                            oe[k] = v
                    for k, v in sd.items():
                        if od.get(k, 0) < v:
                            od[k] = v
            for sk, (v, d) in dneed.items():
                o.waits.append(("dma", sk, v))
                od[sk] = max(od.get(sk, 0), v)
                if d.snap is not None:
                    se, sd = d.snap
                    for k, vv in se.items():
                        if k != o.eng and oe[k] < vv:
                            oe[k] = vv
                    for k, vv in sd.items():
                        if od.get(k, 0) < vv:
                            od[k] = vv
            o.snap = (dict(oe), dict(od))
        fin_waits = {k: 16 * v for k, v in self.dma_cnt.items() if v > 0}
        for e in ENGS:
            c = 0
            for o in self.ops[e]:
                if o.signal:
                    c += 1
                    o.sigval = c
        sem_cms = []
        esem = {}
        for e in ENGS:
            cm = nc.semaphore(f"s_{e}")
            esem[e] = cm.__enter__()
            sem_cms.append(cm)
        dsem = {}
        for i, k in enumerate(self.dma_cnt.keys()):
            cm = nc.semaphore(f"d_{i}")
            dsem[k] = cm.__enter__()
            sem_cms.append(cm)
        self.n_sems = len(sem_cms)

        def run(engname, eobj):
            for o in self.ops[engname]:
                for w in o.waits:
                    if w[0] == "eng":
                        d = w[1]
                        eobj.wait_ge(esem[d.eng], d.sigval)
                    else:
                        eobj.wait_ge(dsem[w[1]], w[2])
                ins = o.fn(eobj)
                if o.is_dma:
                    ins.then_inc(dsem[o.dsem], 16)
                elif o.signal:
                    ins.then_inc(esem[o.eng], 1)
            if engname == "sp":
                for k, v in fin_waits.items():
                    eobj.wait_ge(dsem[k], v)

        with nc.Block() as block:
            @block.tensor
            def _(e):
                run("pe", e)

            @block.scalar
            def _(e):
                run("act", e)

            @block.vector
            def _(e):
                run("dve", e)

            @block.gpsimd
            def _(e):
                run("pool", e)

            @block.sync
            def _(e):
                run("sp", e)
        for cm in reversed(sem_cms):
            cm.__exit__(None, None, None)
        while self.ctx:
            self.ctx.pop().__exit__(None, None, None)


D = 1024
KC = 8
NPS = 2
SEQ = 256
TP = NPS * SEQ
TS = 1024
T = TP + TS
PAST = 512
EPS = 1e-6
DFF = 2816
NFC = 22
GRID_W = 64
GROUPS = [(0, TP, 0), (TP, T, 1)]
TBLK = [(0, 512), (512, 1024), (1024, 1536)]
RING_ELEMS = 6144
LAM_INIT1 = 0.8 - 0.6 * math.exp(-0.3 * 1)

EVEN_OFF = dict(cq=0, ckv=256, krope=384, rq=416, rk=672, rv=928, rg=1440)


def fm(v):
    v = np.asarray(v, np.float32).reshape(-1, 128)
    return np.ascontiguousarray(v.T)


class Pack:
    def __init__(self):
        self.cols = {}
        self.parts = []
        self.n = 0

    def add(self, name, arr):
        arr = np.asarray(arr, np.float32)
        if arr.ndim == 1:
            arr = arr[:, None]
        a = np.zeros((128, arr.shape[1]), np.float32)
        a[:arr.shape[0]] = arr
        self.cols[name] = (self.n, arr.shape[1])
        self.parts.append(a)
        self.n += arr.shape[1]

    def array(self):
        return np.ascontiguousarray(np.concatenate(self.parts, axis=1))


def rope_tables(rot_dim):
    n_freq = rot_dim // 4
    t = np.arange(TS)
    row, col = t // GRID_W, t % GRID_W
    inv = (10000.0 ** (-np.arange(n_freq, dtype=np.float32) / n_freq)).astype(np.float32)
    ang = np.concatenate([row.astype(np.float32)[:, None] * inv, col.astype(np.float32)[:, None] * inv], -1)
    cos, sin = np.cos(ang).astype(np.float32), np.sin(ang).astype(np.float32)
    C = np.repeat(cos.T, 2, axis=0)
    S = np.repeat(sin.T, 2, axis=0)
    S[0::2] *= -1.0
    return C.astype(np.float32), S.astype(np.float32)


def pair_swap(n, lo=0):
    m = np.zeros((n, n), np.float32)
    for i in range(lo, n, 2):
        m[i, i + 1] = 1.0
        m[i + 1, i] = 1.0
    return m


def make_consts():
    c = {}
    c["ident"] = np.eye(128, dtype=np.float32)
    c["ones"] = np.ones((128, 128), np.float32)
    bd = np.zeros((128, 128), np.float32)
    bd[:64, :64] = 1.0
    bd[64:, 64:] = 1.0
    c["bd64"] = bd
    C32, S32 = rope_tables(32)
    C96 = np.ones((96, TS), np.float32)
    S96 = np.zeros((96, TS), np.float32)
    C96[64:] = C32
    S96[64:] = S32
    c["C96"], c["S96"] = C96, S96
    c["C32"], c["S32"] = C32, S32
    c["perm96"] = pair_swap(96, 64)
    c["perm32"] = pair_swap(32)
    C64, S64 = rope_tables(64)
    c["C128"] = np.concatenate([C64, C64], 0)
    c["S128"] = np.concatenate([S64, S64], 0)
    c["perm128"] = pair_swap(128)
    Dm = (np.arange(1920)[None, :] - 896 - np.arange(128)[:, None]).astype(np.float32)
    c["Dpos"] = np.maximum(Dm, 0.0)
    c["Dneg"] = np.maximum(-Dm, 0.0)
    c["Ddiag"] = (Dm == 0).astype(np.float32)
    tt = np.arange(TS, dtype=np.float32)
    c["tpl1"] = np.broadcast_to(tt + 1.0, (128, TS)).copy()
    c["tNm"] = np.broadcast_to(TS - tt, (128, TS)).copy()
    j = (np.arange(2)[None, :] * 128 + np.arange(128)[:, None]).astype(np.float32)
    c["pj"] = np.stack([255.0 - j, j], axis=1).astype(np.float32)
    s_ = np.arange(64)[:, None]
    t_ = np.arange(64)[None, :]
    up_s = (s_ < t_).astype(np.float32)
    up_i = (s_ <= t_).astype(np.float32)
    lo_s = (s_ > t_).astype(np.float32)
    lo_i = (s_ >= t_).astype(np.float32)
    c["m_ab"] = np.stack([np.concatenate([-up_s, up_s], 0), np.concatenate([-lo_s, lo_s], 0)], 0)
    c["m_cd"] = np.stack([np.concatenate([up_i, up_i], 0), np.concatenate([lo_i, lo_i], 0)], 0)
    c["m_na"] = np.stack([-(lo_s), -(up_s)], 0)
    c["eye64"] = np.eye(64, dtype=np.float32)
    return c


def tbs_of(c0, c1):
    return [i for i, (a, b) in enumerate(TBLK) if a < c1 and c0 < b]


def XK(buf, kc, c0, c1):
    return [(buf, (kc, tb)) for tb in tbs_of(c0, c1)]


class Builder:
    def __init__(self, pack_cols, npk, dbg=(), upto="all"):
        self.nc = bass.Bass("TRN2", target_bir_lowering=False)
        self.p = Prog(self.nc)
        self.pc = pack_cols
        self.npk = npk
        self.dbg = set(dbg)
        self.upto = upto
        self.out_dmas = []
        self.ring_i = 0
        self.ps_i = 0
        self.outs = {}

    def din(self, name, shape):
        return self.nc.dram_tensor(name, list(shape), F32, kind="ExternalInput").ap()

    def dout(self, name, shape):
        ap = self.nc.dram_tensor(name, list(shape), F32, kind="ExternalOutput").ap()
        self.outs[name] = list(shape)
        return ap

    def store(self, dst, src, reads, sem):
        d = self.p.dma("sp", dst, src, reads=reads, sem=sem)
        self.out_dmas.append(d)
        return d

    def debug_dump(self, name, buf, ap, shape):
        if name not in self.dbg:
            return
        dst = self.dout("dbg_" + name, shape)
        self.store(dst, ap, [buf], "dbg_" + name)

    def P(self, name):
        o, n = self.pc[name]
        return self.prm[:, o:o + n]

    def wload(self, src, shape):
        slot = self.ring[self.ring_i % len(self.ring)]
        self.ring_i += 1
        n = int(np.prod(shape[1:]))
        assert n <= RING_ELEMS, (shape, n)
        view = slot.t[0:shape[0], 0:n]
        if len(shape) == 3:
            view = view.rearrange("p (a b) -> p a b", b=shape[2])
        self.p.dma("pool", view, src, writes=[slot])
        return slot, view

    def psn(self):
        b = self.ps[self.ps_i % getattr(self, 'ps_mod', 6)]
        self.ps_i += 1
        return b

    def setup(self):
        p = self.p
        self.x_in = self.din("xT", [D, T])
        self.prm_in = self.din("prm", [128, self.npk])
        self.cb_in = self.din("cbf", [128, 5, 128])
        self.prm = p.sbuf([128, self.npk], F32, "prm")
        p.dma("sp", self.prm[:], self.prm_in[:, :], writes=[self.prm])
        self.cbf = p.sbuf([128, 5, 128], BF16, "cbf")
        p.dma("pool", self.cbf[:], self.cb_in[:, :, :], writes=[self.cbf])
        self.ident = self.cbf[:, 0, :]
        self.ones = self.cbf[:, 1, :]
        self.bd64 = self.cbf[:, 2, :]
        self.perm96 = self.cbf[:, 3, :]
        self.perm128 = self.cbf[:, 4, :]
        self.eps = p.sbuf([128, 1], F32, "eps")
        p.memset("dve", self.eps[:], EPS, writes=[self.eps])
        self.xT = p.sbuf([128, KC, T], F32, "xT")
        for kc in range(KC):
            p.dma("sp", self.xT[:, kc, :], self.x_in[kc * 128:(kc + 1) * 128, :], writes=XK(self.xT, kc, 0, T))
        self.hT = p.sbuf([128, KC, T], BF16, "hT")
        self.ring = [p.sbuf([128, RING_ELEMS], BF16, f"ring{i}") for i in range(3)]
        cm = self.nc.psum_tensor("psall", [128, 8 * 512], F32)
        self.psall = cm.__enter__()
        p.ctx.append(cm)
        self.ps = [Buf(self.psall[:, i * 512:(i + 1) * 512], f"psb{i}") for i in range(8)]
        self.ffn_up_in = self.din("ffn_up", [2, D, 2 * DFF])
        self.ffn_down_in = self.din("ffn_down", [2, DFF, D])
        self.mod = [p.sbuf([128, 48, 2], F32, f"mod{l}") for l in range(2)]
        self.gsc = [p.sbuf([128, 2, KC, 2], F32, f"gsc{l}") for l in range(2)]
        self.scond = p.sbuf([128, KC, 2], BF16, "scond")
        self.cond_in = self.din("condT", [128, KC, 2])
        condf = p.sbuf([128, KC, 2], F32, "condf")
        p.dma("sp", condf[:], self.cond_in[:, :, :], writes=[condf])
        p.act(self.scond[:], condf[:], AF.Silu, reads=[condf], writes=[self.scond])
        self.ada_in = self.din("ada_w", [2, D, 6 * D])
        self.a_w_in = self.din("a_w_in", [D, 1952])
        self.w_out_in = self.din("w_out", [2, D, D])
        self.b_w_in = self.din("b_w_in", [D, 3456])

    def mod_piece(self, l, piece, mps):
        p = self.p
        mview = mps.t[:, 0:12].rearrange("p (j c) -> p j c", c=2)
        src = self.ada_in[l].rearrange("(kc p) n -> p kc n", p=128)
        slot, wv = self.wload(src[:, :, piece * 768:(piece + 1) * 768], [128, KC, 768])
        for nch in range(6):
            for kc in range(KC):
                p.mm(mview[:, nch, :], wv[:, kc, nch * 128:(nch + 1) * 128], self.scond[:, kc, :],
                     start=(kc == 0), stop=(kc == KC - 1), reads=[slot, self.scond], writes=[mps])
        ab = self.P(f"ab{l}")[:, piece * 6:(piece + 1) * 6]
        mod = self.mod[l]
        p.tt("dve", mod[:, piece * 6:(piece + 1) * 6, :], mview, ab.unsqueeze(2).to_broadcast([128, 6, 2]), ALU.add,
             reads=[mps, self.prm], writes=[(mod, piece)])

    def mod_end(self, l):
        p = self.p
        mod = self.mod[l]
        gsc = self.gsc[l]
        for w, (scj, gname) in enumerate(((1, f"gm{l}"), (4, f"gf{l}"))):
            g = self.P(gname)
            p.stt(gsc[:, w, :, :], mod[:, scj * 8:(scj + 1) * 8, :], 1.0, g.unsqueeze(2).to_broadcast([128, KC, 2]),
                  ALU.add, ALU.mult, reads=[mod, self.prm], writes=[(gsc, w)])

    def modulation(self, l):
        for piece in range(8):
            self.mod_piece(l, piece, self.psn())
        self.mod_end(l)

    def norm_mod(self, l, w):
        p = self.p
        gsc = self.gsc[l]
        mod = self.mod[l]
        shj = 0 if w == 0 else 3
        with p.scope("norm_mod:598"):
            sqb = [p.sbuf([128, 512], BF16, f"sq{i}") for i in range(3)]
            tmpb = [p.sbuf([128, 1024], F32, f"nt{i}") for i in range(2)]
            rt = p.sbuf([128, 512], F32, "rt")
            self.rstd = p.sbuf([128, T], F32, "rstd")
            i = 0
            for tb, (c0, c1) in enumerate(TBLK):
                ps = self.psn()
                for kc in range(KC):
                    sq = sqb[i % 3]
                    i += 1
                    p.act(sq[:], self.xT[:, kc, c0:c1], AF.Square, reads=[(self.xT, (kc, tb))], writes=[sq])
                    p.mm(ps[:], self.ones, sq[:], start=(kc == 0), stop=(kc == KC - 1), reads=[sq, self.cbf],
                         writes=[ps])
                p.act(rt[:], ps[:], AF.Ln, bias=self.eps[:], scale=1.0 / D, reads=[ps, self.eps], writes=[rt])
                p.act(self.rstd[:, c0:c1], rt[:], AF.Exp, scale=-0.5, reads=[rt], writes=[(self.rstd, tb)])
            i = 0
            for kc in range(KC):
                for (c0, c1, ci) in GROUPS:
                    tmp = tmpb[i % 2]
                    i += 1
                    n = c1 - c0
                    p.stt(tmp[:, 0:n], self.xT[:, kc, c0:c1], gsc[:, w, kc, ci:ci + 1], self.rstd[:, c0:c1],
                          ALU.mult, ALU.mult, reads=XK(self.xT, kc, c0, c1) + [(gsc, w)] + [(self.rstd, t) for t in tbs_of(c0, c1)],
                          writes=[tmp])
                    p.act(self.hT[:, kc, c0:c1], tmp[:, 0:n], AF.Identity, bias=mod[:, shj * 8 + kc, ci:ci + 1], scale=1.0,
                          reads=[tmp, mod], writes=XK(self.hT, kc, c0, c1))


def build_pack(I):
    pk = Pack()
    for l in range(2):
        pk.add(f"ab{l}", fm(I["ada_b"][l]))
        pk.add(f"gm{l}", fm(I["norm_mix_g"][l]))
        pk.add(f"gf{l}", fm(I["norm_ffn_g"][l]))
        cw = np.stack([fm(I["ffn_conv_w"][l][k]) for k in range(3)], axis=2)
        pk.add(f"cw{l}", cw.reshape(128, 44 * 3))
        pk.add(f"cb{l}", fm(I["ffn_conv_b"][l]))
    pk.add("qnorm", fm(I["mla_q_norm"][0]))
    pk.add("kvnorm", fm(I["mla_kv_norm"][0]))
    pk.add("qn", I["mla_qn"][0])
    pk.add("kn", I["mla_kn"][0])
    pk.add("knr", I["mla_kn"][0][64:96])
    pk.add("retdec", np.broadcast_to(I["ret_decay"][0].reshape(1, 8), (128, 8)))
    pk.add("retgn", fm(I["ret_gn"][0]))
    pk.add("dqn", np.tile(I["diff_qn"][0], 2))
    pk.add("dkn", np.tile(I["diff_kn"][0], 2))
    pk.add("lam", np.broadcast_to(I["diff_lam"][0].reshape(1, 256), (128, 256)))
    pk.add("dgn", fm(I["diff_gn"][0]))
    pk.add("mu", fm(I["rwkv_mu"][0]))
    pk.add("w0", np.concatenate([fm(I["rwkv_w0"][0][d]) for d in range(2)], 1))
    pk.add("a0", np.concatenate([fm(I["rwkv_a0"][0][d]) for d in range(2)], 1))
    pk.add("kk", fm(I["rwkv_k_k"][0]))
    pk.add("ka", fm(I["rwkv_k_a"][0]))
    pk.add("rrk", fm(I["rwkv_r_k"][0].reshape(-1)))
    pk.add("rgn", fm(I["rwkv_gn"][0]))
    return pk


def build_in_maps(I):
    I = {k: np.asarray(v) for k, v in I.items()}
    cst = make_consts()
    pk = build_pack(I)
    prm = pk.array()
    cbf = np.zeros((128, 5, 128), np.float32)
    cbf[:, 0] = cst["ident"]
    cbf[:, 1] = cst["ones"]
    cbf[:, 2] = cst["bd64"]
    cbf[:96, 3, :96] = cst["perm96"]
    cbf[:, 4] = cst["perm128"]
    shared = {
        "prm": prm, "cbf": cbf,
        "ada_w": I["ada_w"], "w_out": I["w_out"], "ffn_up": I["ffn_up"], "ffn_down": I["ffn_down"],
        "a_w_in": I["a_w_in"][0], "w_uq": I["mla_w_uq"][0], "w_ukv": I["mla_w_ukv"][0],
        "b_w_in": I["b_w_in"][0], "rw_w_up": I["rwkv_w_up"][0], "rw_a_up": I["rwkv_a_up"][0],
        "rw_g_up": I["rwkv_g_up"][0],
    }
    for k in ("C96", "S96", "C32", "S32", "C128", "S128", "Dpos", "Dneg", "Ddiag", "tpl1", "tNm", "pj",
              "m_ab", "m_cd", "m_na", "eye64", "perm32"):
        shared["c_" + k] = cst[k]
    maps = []
    for c in range(8):
        s = c // 4
        xp = I["x_prompt"][2 * c:2 * c + 2].reshape(TP, D)
        xs = I["x_sample"][s]
        xT = np.ascontiguousarray(np.concatenate([xp, xs], 0).T)
        cond = np.stack([I["c_ctx"], I["c"][s]], 0)
        condT = np.ascontiguousarray(cond.T.reshape(KC, 128, 2).transpose(1, 0, 2))
        m = dict(shared)
        m["xT"] = xT
        m["condT"] = condT
        m["ckv_cT"] = np.ascontiguousarray(I["cache_mla_ckv"][s, 0].T)
        m["krope_cT"] = np.ascontiguousarray(I["cache_mla_krope"][s, 0].T)
        m["ret_s0"] = np.ascontiguousarray(I["state_ret"][s, 0])
        m["dk_cT"] = np.ascontiguousarray(I["cache_diff_k"][s, 0].reshape(PAST, 4, 128).transpose(1, 2, 0))
        m["dv_c"] = np.ascontiguousarray(I["cache_diff_v"][s, 0].reshape(PAST, 512))
        m["rw_s0"] = np.ascontiguousarray(I["state_rwkv"][s, 0].transpose(0, 1, 3, 2))
        maps.append(m)
    return maps, pk.cols, pk.n


def layer1(b, own_mod=False):
    p = b.p
    if own_mod:
        b.modulation(1)
    else:
        b.mod_end(1)
    b.norm_mod(1, 0)
    with p.scope("layer1:706"):
        oT = p.sbuf([128, 4, T], BF16, "oT1")
        if b.upto != "rwkv_only":
            diff_attn(b, oT)
            if "odiff" in b.dbg:
                dst = b.dout("dbg_odiff", [128, 4, T])
                b.out_dmas.append(p.dma("pool", dst[:, :, :], oT[:], reads=[oT], sem="dbg_odiff"))
            wout_half(b, 1, 0, oT)
        if b.upto == "diff":
            return
        rwkv(b, oT)
        if "orw" in b.dbg:
            dst = b.dout("dbg_orw", [128, 4, T])
            b.out_dmas.append(p.dma("pool", dst[:, :, :], oT[:], reads=[oT], sem="dbg_orw"))
        wout_half(b, 1, 1, oT)
    if b.upto in ("rwkv", "rwkv_only"):
        return
    b.norm_mod(1, 1)
    conv_ffn(b, 1)
    yo = b.dout("o_yT", [D, T])
    for kc in range(KC):
        b.store(yo[kc * 128:(kc + 1) * 128, :], b.xT[:, kc, :], XK(b.xT, kc, 0, T), ("o_y", kc))


def build_program(pack_cols, npk, dbg=(), upto="all"):
    b = Builder(pack_cols, npk, dbg=dbg, upto=upto)
    b.setup()
    b.modulation(0)
    b.norm_mod(0, 0)
    if "h0" in b.dbg:
        dst = b.dout("dbg_h0", [128, KC, T])
        b.out_dmas.append(b.p.dma("pool", dst[:, :, :], b.hT[:], reads=[b.hT], sem="dbg_h0"))
    if "mod0" in b.dbg:
        dst = b.dout("dbg_mod0", [128, 48, 2])
        b.store(dst[:, :, :], b.mod[0][:], [b.mod[0]], "dbg_mod0")
    if upto != "h0":
        with b.p.scope("build_program:743"):
            oT = b.p.sbuf([128, 4, T], BF16, "oT")
            mla(b, oT)
            if "omla" in b.dbg:
                dst = b.dout("dbg_omla", [128, 4, T])
                b.out_dmas.append(b.p.dma("pool", dst[:, :, :], oT[:], reads=[oT], sem="dbg_omla"))
            if upto != "mla":
                wout_half(b, 0, 0, oT)
                retention(b, oT)
                if "oret" in b.dbg:
                    dst = b.dout("dbg_oret", [128, 4, T])
                    b.out_dmas.append(b.p.dma("pool", dst[:, :, :], oT[:], reads=[oT], sem="dbg_oret"))
                wout_half(b, 0, 1, oT)
        if "xm0" in b.dbg:
            dst = b.dout("dbg_xm0", [128, KC, T])
            b.store(dst[:, :, :], b.xT[:], [b.xT], "dbg_xm0")
        if upto not in ("mla", "mix0"):
            b.norm_mod(0, 1)
            if upto == "l0":
                conv_ffn(b, 0)
            else:
                conv_ffn(b, 0, hook=lambda g: [b.mod_piece(1, 2 * g + i, b.ps[6 + i]) for i in range(2)])
            if "x0" in b.dbg:
                dst = b.dout("dbg_x0", [128, KC, T])
                b.store(dst[:, :, :], b.xT[:], [b.xT], "dbg_x0")
            if upto != "l0":
                layer1(b)
    b.p.emit(final_dma_ops=b.out_dmas)
    return b


def kernel(**inputs):
    I = {k: np.asarray(v) for k, v in inputs.items()}
    maps, cols, npk = build_in_maps(I)
    b = build_program(cols, npk)
    res = run_bass_kernel_spmd(b.nc, maps, core_ids=list(range(8)))
    R = res.results
    B = I["x_prompt"].shape[0]
    y_p = np.zeros((B, SEQ, D), np.float32)
    y_s = np.zeros((2, TS, D), np.float32)
    ckv = np.zeros((B, 1, SEQ, 128), np.float32)
    kro = np.zeros((B, 1, SEQ, 32), np.float32)
    rst = np.zeros((B, 1, 2, 4, 64, 128), np.float32)
    dk = np.zeros((B, 1, SEQ, 4, 2, 64), np.float32)
    dv = np.zeros((B, 1, SEQ, 4, 128), np.float32)
    rws = np.zeros((B, 1, 2, 8, 64, 64), np.float32)
    for c in range(8):
        r = R[c]
        sl = slice(2 * c, 2 * c + 2)
        yT = np.asarray(r["o_yT"])
        y_p[sl] = yT[:, 0:TP].T.reshape(NPS, SEQ, D)
        s_, q = c // 4, c % 4
        y_s[s_, q * 256:(q + 1) * 256] = yT[:, TP + q * 256:TP + (q + 1) * 256].T
        ckv[sl, 0] = np.asarray(r["o_ckvT"]).T.reshape(NPS, SEQ, 128)
        kro[sl, 0] = np.asarray(r["o_kropeT"]).T.reshape(NPS, SEQ, 32)
        rst[sl, 0] = np.asarray(r["o_retst"]).transpose(0, 1, 3, 2, 4)
        dk[sl, 0] = np.asarray(r["o_dkT"]).transpose(2, 0, 1).reshape(NPS, SEQ, 4, 2, 64)
        dv[sl, 0] = np.asarray(r["o_dv"]).reshape(NPS, SEQ, 4, 128)
        rws[sl, 0] = np.asarray(r["o_rwst"]).transpose(0, 1, 2, 4, 3)
    return (y_p, y_s, ckv, kro, rst, dk, dv, rws)


def _rms_rstd(b, ps_s, M, n, nfeat, rt, rstd, legacy=False):
    p = b.p
    if legacy:
        p.act(rt[0:M, 0:n], ps_s[0:M, 0:n], AF.Sqrt, bias=b.eps[0:M, :], scale=1.0 / nfeat, reads=[ps_s, b.eps], writes=[rt])
        p.recip(rstd[0:M, 0:n], rt[0:M, 0:n], reads=[rt], writes=[rstd])
        return
    p.act(rt[0:M, 0:n], ps_s[0:M, 0:n], AF.Ln, bias=b.eps[0:M, :], scale=1.0 / nfeat, reads=[ps_s, b.eps], writes=[rt])
    p.act(rstd[0:M, 0:n], rt[0:M, 0:n], AF.Exp, scale=-0.5, reads=[rt], writes=[rstd])


def mla(b, oT):
    p = b.p
    nc = b.nc
    a_w_in = b.a_w_in
    with p.scope("mla:816"):
        ckvn = p.sbuf([128, 2048], BF16, "ckvn")
        krg = p.sbuf([96, 2048], BF16, "krg")
        sqk = [p.sbuf([96, 512], BF16, f"sqk{i}") for i in range(4)]
        cqn = p.sbuf([128, 2, T], BF16, "cqn")
        vaug = p.sbuf([128, 16, 512], BF16, "vtok")
        C96 = p.sbuf([96, TS], F32, "C96")
        S96 = p.sbuf([96, TS], F32, "S96")
        p.dma("sp", C96[:], b.din("c_C96", [96, TS])[:, :], writes=[C96])
        p.dma("sp", S96[:], b.din("c_S96", [96, TS])[:, :], writes=[S96])
        p.dma("pool", ckvn[:, T:T + PAST], b.din("ckv_cT", [128, PAST])[:, :], writes=[(ckvn, 3)])
        krc = p.sbuf([96, PAST], F32, "krc")
        p.dma("sp", krc[64:96, :], b.din("krope_cT", [32, PAST])[:, :], writes=[krc])
        kn = b.P("kn")
        qn = b.P("qn")
        out_ckv = b.dout("o_ckvT", [128, TP])
        out_kr = b.dout("o_kropeT", [32, TP])

        slotA, wA = b.wload(a_w_in.rearrange("(kc p) n -> p kc n", p=128)[:, :, 0:416], [128, KC, 416])
        with p.scope("mla:837"):
            sqt = [p.sbuf([128, 512], BF16, f"sqt{i}") for i in range(2)]
            rt = p.sbuf([128, 512], F32, "rt")
            rs = p.sbuf([128, 512], F32, "rs")
            ckf = p.sbuf([128, 512], F32, "ckf")
            krf = p.sbuf([96, 512], F32, "krf")
            krf2 = p.sbuf([96, 512], F32, "krf2")
            krb = p.sbuf([96, 512], BF16, "krb")
            t1 = p.sbuf([96, 512], F32, "t1")
            t2 = p.sbuf([96, 512], F32, "t2")
            p.memset("dve", krb[:], 0.0, writes=[krb])

            def krope_block(src_ap, src_reads, c0, n, rope_t0):
                kc_ = c0 // 512
                p.act(sqk[kc_][64:96, 0:n], src_ap, AF.Square, reads=src_reads, writes=[(sqk[kc_], "r")])
                if rope_t0 is None:
                    p.ts("dve", krg[64:96, c0:c0 + n], src_ap, kn[64:96, :], None, ALU.mult,
                         reads=src_reads + [b.prm], writes=[(krg, kc_)])
                else:
                    p.ts("dve", krf2[64:96, 0:n], src_ap, kn[64:96, :], None, ALU.mult,
                         reads=src_reads + [b.prm], writes=[krf2])
                    p.copy("dve", krb[64:96, 0:n], krf2[64:96, 0:n], reads=[krf2], writes=[krb])
                    pp = b.psn()
                    p.mm(pp[0:96, 0:n], b.perm96[0:96, 0:96], krb[:, 0:n], reads=[krb, b.cbf], writes=[pp])
                    p.tt("dve", t1[64:96, 0:n], krf2[64:96, 0:n], C96[64:96, rope_t0:rope_t0 + n], ALU.mult,
                         reads=[krf2, C96], writes=[t1])
                    p.tt("dve", t2[64:96, 0:n], pp[64:96, 0:n], S96[64:96, rope_t0:rope_t0 + n], ALU.mult,
                         reads=[pp, S96], writes=[t2])
                    p.tt("dve", krg[64:96, c0:c0 + n], t1[64:96, 0:n], t2[64:96, 0:n], ALU.add,
                         reads=[t1, t2], writes=[(krg, kc_)])

            for tb, (c0, c1) in enumerate(TBLK):
                n = c1 - c0
                pcq = [b.psn(), b.psn()]
                pss = b.psn()
                for j in range(2):
                    for kc in range(KC):
                        p.mm(pcq[j][:, 0:n], wA[:, kc, j * 128:(j + 1) * 128], b.hT[:, kc, c0:c1],
                             start=(kc == 0), stop=(kc == KC - 1), reads=[slotA, (b.hT, (kc, tb))], writes=[pcq[j]])
                    sq = sqt[j]
                    p.act(sq[:, 0:n], pcq[j][:, 0:n], AF.Square, reads=[pcq[j]], writes=[sq])
                    p.mm(pss[:, 0:n], b.ones, sq[:, 0:n], start=(j == 0), stop=(j == 1), reads=[sq, b.cbf], writes=[pss])
                _rms_rstd(b, pss, 128, n, 256, rt, rs)
                for j in range(2):
                    p.stt(cqn[:, j, c0:c1], pcq[j][:, 0:n], b.P("qnorm")[:, j:j + 1], rs[:, 0:n], ALU.mult, ALU.mult,
                          reads=[pcq[j], rs, b.prm], writes=[(cqn, (j, tb))])
                pck = b.psn()
                pss = b.psn()
                for kc in range(KC):
                    p.mm(pck[:, 0:n], wA[:, kc, 256:384], b.hT[:, kc, c0:c1], start=(kc == 0), stop=(kc == KC - 1),
                         reads=[slotA, (b.hT, (kc, tb))], writes=[pck])
                sq = sqt[0]
                p.act(sq[:, 0:n], pck[:, 0:n], AF.Square, reads=[pck], writes=[sq])
                p.mm(pss[:, 0:n], b.ones, sq[:, 0:n], reads=[sq, b.cbf], writes=[pss])
                _rms_rstd(b, pss, 128, n, 128, rt, rs)
                if tb == 0:
                    p.stt(ckf[:, 0:n], pck[:, 0:n], b.P("kvnorm")[:, 0:1], rs[:, 0:n], ALU.mult, ALU.mult,
                          reads=[pck, rs, b.prm], writes=[ckf])
                    b.store(out_ckv[:, :], ckf[:, 0:n], [ckf], "o_ckv")
                    p.copy("act", ckvn[:, c0:c1], ckf[:, 0:n], reads=[ckf], writes=[(ckvn, tb)])
                else:
                    p.stt(ckvn[:, c0:c1], pck[:, 0:n], b.P("kvnorm")[:, 0:1], rs[:, 0:n], ALU.mult, ALU.mult,
                          reads=[pck, rs, b.prm], writes=[(ckvn, tb)])
                pkr = b.psn()
                for kc in range(KC):
                    p.mm(pkr[0:32, 0:n], wA[:, kc, 384:416], b.hT[:, kc, c0:c1], start=(kc == 0), stop=(kc == KC - 1),
                         reads=[slotA, (b.hT, (kc, tb))], writes=[pkr])
                p.copy("act", krf[64:96, 0:n], pkr[0:32, 0:n], reads=[pkr], writes=[krf])
                if tb == 0:
                    b.store(out_kr[:, :], krf[64:96, 0:n], [krf], "o_kr")
                krope_block(krf[64:96, 0:n], [krf], c0, n, None if tb == 0 else c0 - TP)
            krope_block(krc[64:96, :], [krc], T, PAST, None)

        slotU = p.sbuf([128, 2 * 768 + 1024], BF16, "wU")
        wuq = slotU.t[:, 0:1536].rearrange("p (a b) -> p a b", b=768)
        wukv = slotU.t[:, 1536:2560]
        p.dma("pool", wuq, b.din("w_uq", [256, 768]).rearrange("(kc p) n -> p kc n", p=128), writes=[(slotU, 0)])
        p.dma("pool", wukv, b.din("w_ukv", [128, 1024])[:, :], writes=[(slotU, 1)])
        wukv_h = wukv.rearrange("p (h two e) -> p h two e", two=2, e=64)

        for kt in range(16):
            pv = b.psn()
            pvv = pv.t[:, :].rearrange("p (h e) -> p h e", e=64)
            p.mm(pvv, ckvn[:, kt * 128:(kt + 1) * 128], wukv_h[:, :, 1, :], reads=[(ckvn, kt // 4), (slotU, 1)], writes=[pv])
            p.copy("act" if kt % 2 == 0 else "dve", vaug[:, kt, :], pv[:, :], reads=[pv], writes=[(vaug, kt)])

        with p.scope("mla:930"):
            Qh = [p.sbuf([96, T], BF16, f"Qh{i}") for i in range(2)]
            Kh = [p.sbuf([96, 2048], BF16, f"Kh{i}") for i in range(2)]
            sqt = [p.sbuf([96, 512], BF16, f"sqh{i}") for i in range(2)]
            rt = p.sbuf([96, 512], F32, "rt")
            rs = p.sbuf([96, 512], F32, "rs")
            t1 = p.sbuf([96, 512], F32, "t1")
            t2 = p.sbuf([96, 512], F32, "t2")
            qgb = p.sbuf([96, 512], BF16, "qgb")
            ex = [p.sbuf([128, 512], BF16, f"ex{i}") for i in range(3)]
            rden = p.sbuf([128, 512], F32, "rden")
            exi = 0
            acci = 0
            sc = 96.0 ** -0.5
            cnt = {"ex": 0, "acc": 0}

            def k_block(h, kb):
                Kt = Kh[h % 2]
                c0 = kb * 512
                pk = b.psn()
                p.mm(pk[0:64, :], wukv[:, h * 128:h * 128 + 64], ckvn[:, c0:c0 + 512], reads=[(slotU, 1), (ckvn, kb)], writes=[pk])
                sq = sqk[kb]
                p.act(sq[0:64, :], pk[0:64, :], AF.Square, reads=[pk], writes=[(sq, "n")])
                pss = b.psn()
                p.mm(pss[0:96, :], b.ones[0:96, 0:96], sq[0:96, :], reads=[(sq, "n"), (sq, "r"), b.cbf], writes=[pss])
                _rms_rstd(b, pss, 96, 512, 96, rt, rs)
                p.stt(Kt[0:64, c0:c0 + 512], pk[0:64, :], kn[0:64, :], rs[0:64, :], ALU.mult, ALU.mult,
                      reads=[pk, rs, b.prm], writes=[(Kt, kb)])
                p.tt("pool", Kt[64:96, c0:c0 + 512], krg[64:96, c0:c0 + 512], rs[64:96, :], ALU.mult,
                     reads=[(krg, kb), rs], writes=[(Kt, kb)])

            def q_block(h, tb):
                Q = Qh[h % 2]
                c0, c1 = TBLK[tb]
                pq = b.psn()
                for kc in range(2):
                    p.mm(pq[0:96, :], wuq[:, kc, h * 96:(h + 1) * 96], cqn[:, kc, c0:c1], start=(kc == 0), stop=(kc == 1),
                         reads=[(slotU, 0), (cqn, (kc, tb))], writes=[pq])
                sq = sqt[tb % 2]
                p.act(sq[0:96, :], pq[0:96, :], AF.Square, reads=[pq], writes=[sq])
                pss = b.psn()
                p.mm(pss[0:96, :], b.ones[0:96, 0:96], sq[0:96, :], reads=[sq, b.cbf], writes=[pss])
                _rms_rstd(b, pss, 96, 512, 96, rt, rs)
                if tb == 0:
                    p.stt(Q[:, c0:c1], pq[0:96, :], qn[0:96, :], rs[0:96, :], ALU.mult, ALU.mult,
                          reads=[pq, rs, b.prm], writes=[(Q, tb)])
                else:
                    r0 = c0 - TP
                    p.stt(t1[:, :], pq[0:96, :], qn[0:96, :], C96[:, r0:r0 + 512], ALU.mult, ALU.mult,
                          reads=[pq, C96, b.prm], writes=[t1])
                    p.act(qgb[:, :], pq[0:96, :], AF.Identity, scale=qn[0:96, :], reads=[pq, b.prm], writes=[qgb])
                    pp = b.psn()
                    p.mm(pp[0:96, :], b.perm96[0:96, 0:96], qgb[:, :], reads=[qgb, b.cbf], writes=[pp])
                    p.tt("dve", t2[:, :], pp[0:96, :], S96[:, r0:r0 + 512], ALU.mult, reads=[pp, S96], writes=[t2])
                    p.tt("pool", t1[:, :], t1[:, :], t2[:, :], ALU.add, reads=[t1, t2], writes=[t1])
                    p.tt("dve", Q[:, c0:c1], t1[:, :], rs[0:96, :], ALU.mult, reads=[t1, rs], writes=[(Q, tb)])

            def build_steps(h):
                return [lambda kb=kb: k_block(h, kb) for kb in range(4)] + [lambda tb=tb: q_block(h, tb) for tb in range(3)]

            def attn_steps(h):
                Q = Qh[h % 2]
                Kt = Kh[h % 2]
                jobs = [(a * SEQ, SEQ, [2 * a, 2 * a + 1]) for a in range(NPS)]
                jobs += [(TP + qb * 512, 512, list(range(4, 16))) for qb in range(2)]
                hc, hu = h // 2, h % 2
                lo, hi = (0, 64) if hu == 0 else (64, 128)
                dlo, dhi = (64, 128) if hu == 0 else (0, 64)
                steps = []
                for (q0, nq, kts) in jobs:
                    st = {}

                    def first(st=st):
                        st["acc"] = b.ps[4 + 2 * (cnt["acc"] % 2)]
                        st["accd"] = b.ps[5 + 2 * (cnt["acc"] % 2)]
                        cnt["acc"] += 1
                    for i, kt in enumerate(kts):
                        def tile(i=i, kt=kt, q0=q0, nq=nq, kts=kts, st=st, first=first):
                            def score(k):
                                ktk = kts[k]
                                ps_ = b.psn()
                                p.mm(ps_[:, 0:nq], Kt[:, ktk * 128:(ktk + 1) * 128], Q[:, q0:q0 + nq],
                                     reads=[(Kt, ktk // 4)] + [(Q, t) for t in tbs_of(q0, q0 + nq)], writes=[ps_])
                                st[("ps", k)] = ps_
                            LA = not (len(DBG_KNOB) > 2 and DBG_KNOB[2] == 2)
                            if i == 0:
                                first()
                                score(0)
                                if len(kts) > 1:
                                    score(1)
                                if len(kts) > 2:
                                    score(2)
                            if i + 3 < len(kts):
                                score(i + 3)
                            acc = st["acc"]
                            accd = st["accd"]
                            pscore = st.pop(("ps", i))
                            e = ex[cnt["ex"] % 3]
                            cnt["ex"] += 1
                            p.act(e[:, 0:nq], pscore[:, 0:nq], AF.Exp, scale=sc, reads=[pscore], writes=[e])
                            p.mm(acc[:, 0:nq], vaug[:, kt, hc * 128:(hc + 1) * 128], e[:, 0:nq], start=(i == 0),
                                 stop=(i == len(kts) - 1), reads=[(vaug, kt), e], writes=[acc])
                            p.mm(accd[:, 0:nq], b.ones, e[:, 0:nq], start=(i == 0),
                                 stop=(i == len(kts) - 1), reads=[b.cbf, e], writes=[accd])
                            if i == len(kts) - 1:
                                p.act(rden[lo:hi, 0:nq], accd[lo:hi, 0:nq], AF.Ln, reads=[accd], writes=[rden])
                                p.act(rden[lo:hi, 0:nq], rden[lo:hi, 0:nq], AF.Exp, scale=-1.0, reads=[rden], writes=[rden])
                                p.tt("dve", oT[lo:hi, hc, q0:q0 + nq], acc[lo:hi, 0:nq], rden[lo:hi, 0:nq], ALU.mult,
                                     reads=[acc, rden], writes=[(oT, (hc, hu, q0))])
                        steps.append(tile)
                return steps

            b.ps_mod = 4
            for f in build_steps(0):
                f()
            for h in range(8):
                A = attn_steps(h)
                B = build_steps(h + 1) if h < 7 else []
                bi = 0
                for ai, f in enumerate(A):
                    f()
                    if ai == 3:
                        while bi < len(B):
                            B[bi]()
                            bi += 1
                while bi < len(B):
                    B[bi]()
                    bi += 1


def wout_half(b, l, half, oT):
    p = b.p
    b.ps_mod = 6
    src = b.w_out_in[l][half * 512:(half + 1) * 512].rearrange("(c p) n -> p c n", p=128)
    slot, wv = b.wload(src, [128, 4, 1024])
    mod = b.mod[l]
    for n in range(KC):
        for tb, (c0, c1) in enumerate(TBLK):
            ps = b.psn()
            for c in range(4):
                p.mm(ps[:, :], wv[:, c, n * 128:(n + 1) * 128], oT[:, c, c0:c1], start=(c == 0), stop=(c == 3),
                     reads=[slot, oT], writes=[ps])
            ci = 0 if tb == 0 else 1
            p.stt(b.xT[:, n, c0:c1], ps[:, :], mod[:, 16 + n, ci:ci + 1], b.xT[:, n, c0:c1], ALU.mult, ALU.add,
                  reads=[ps, mod, (b.xT, (n, tb))], writes=[(b.xT, (n, tb))])


def retention(b, oT):
    p = b.p
    a_w = b.a_w_in.rearrange("(kc p) n -> p kc n", p=128)
    with p.scope("retention:1033"):
        G = p.sbuf([128, 4, 1920], BF16, "G")
        lg = p.sbuf([128, 8], F32, "lg")
        lgT = p.sbuf([128, 8], F32, "lgT")
        dec = p.sbuf([128, 2, 4, 2], F32, "decst")
        p.act(lg[:], b.P("retdec"), AF.Sigmoid, reads=[b.prm], writes=[lg])
        p.act(lg[:], lg[:], AF.Ln, reads=[lg], writes=[lg])
        p.ts("dve", lgT[:], lg[:], float(TS + 1), None, ALU.mult, reads=[lg], writes=[lgT])
        nlg = p.sbuf([128, 8], F32, "nlg")
        p.ts("dve", nlg[:], lg[:], -1.0, None, ALU.mult, reads=[lg], writes=[nlg])
        with p.scope("retention:1043"):
            Dp = p.sbuf([128, 1920], F32, "Dp")
            Dn = p.sbuf([128, 1920], F32, "Dn")
            Dd = p.sbuf([128, 1920], F32, "Dd")
            E1 = p.sbuf([128, 1920], F32, "E1")
            E2 = p.sbuf([128, 1920], F32, "E2")
            pj = p.sbuf([128, 2, 2], F32, "pj")
            p.dma("sp", Dp[:], b.din("c_Dpos", [128, 1920])[:, :], writes=[Dp])
            p.dma("sp", Dn[:], b.din("c_Dneg", [128, 1920])[:, :], writes=[Dn])
            p.dma("sp", Dd[:], b.din("c_Ddiag", [128, 1920])[:, :], writes=[Dd])
            p.dma("sp", pj[:], b.din("c_pj", [128, 2, 2])[:, :, :], writes=[pj])
            for h in range(4):
                p.act(E1[:], Dp[:], AF.Exp, scale=lg[:, h:h + 1], reads=[Dp, lg], writes=[E1])
                p.act(E2[:], Dn[:], AF.Exp, scale=lg[:, 4 + h:5 + h], reads=[Dn, lg], writes=[E2])
                p.tt("dve", E1[:], E1[:], E2[:], ALU.mult, reads=[E1, E2], writes=[E1])
                p.tt("dve", G[:, h, :], E1[:], Dd[:], ALU.add, reads=[E1, Dd], writes=[(G, h)])
                for d in range(2):
                    p.act(dec[:, d, h, :], pj[:, d, :], AF.Exp, scale=lg[:, d * 4 + h:d * 4 + h + 1],
                          reads=[pj, lg], writes=[(dec, (d, h))])
        rq = p.sbuf([128, 2, T], BF16, "rq")
        rk = p.sbuf([128, 2, T], BF16, "rk")
        rvt = p.sbuf([128, 12, 512], BF16, "rvt")
        S0 = p.sbuf([128, 2, 2, 128], BF16, "S0")
        tpos = p.sbuf([128, TS], F32, "tpos")
        p.dma("sp", tpos[:], b.din("c_tpl1", [128, TS])[:, :], writes=[tpos])
        p.dma("pool", S0[:], b.din("ret_s0", [2, 4, 64, 128]).rearrange("r (j u) d e -> (u d) r j e", u=2), writes=[S0])
        slotB, wB = b.wload(a_w[:, :, 416:928], [128, KC, 512])
        for tb, (c0, c1) in enumerate(TBLK):
            for j in range(4):
                ps = b.psn()
                for kc in range(KC):
                    p.mm(ps[:, :], wB[:, kc, j * 128:(j + 1) * 128], b.hT[:, kc, c0:c1], start=(kc == 0), stop=(kc == KC - 1),
                         reads=[slotB, (b.hT, (kc, tb))], writes=[ps])
                if j < 2:
                    p.copy("act", rq[:, j, c0:c1], ps[:, :], reads=[ps], writes=[(rq, (j, tb))])
                else:
                    p.ts("dve", rk[:, j - 2, c0:c1], ps[:, :], 0.125, None, ALU.mult, reads=[ps], writes=[(rk, (j - 2, tb))])
        slotC, wC = b.wload(a_w[:, :, 928:1440], [128, KC, 512])
        for tl in range(12):
            ps = b.psn()
            for kc in range(KC):
                p.mm(ps[:, :], b.hT[:, kc, tl * 128:(tl + 1) * 128], wC[:, kc, :], start=(kc == 0), stop=(kc == KC - 1),
                     reads=[slotC, (b.hT, (kc, tl // 4))], writes=[ps])
            p.copy("act" if tl % 2 else "dve", rvt[:, tl, :], ps[:, :], reads=[ps], writes=[(rvt, tl)])
        out_st = b.dout("o_retst", [NPS, 2, 64, 4, 128])
        with p.scope("retention:1090"):
            rkt = p.sbuf([128, 4, 256], BF16, "rkt")
            for tl in range(4):
                ps = b.psn()
                for kc in range(KC):
                    p.mm(ps[:, 0:256], b.hT[:, kc, tl * 128:(tl + 1) * 128], wB[:, kc, 256:512], start=(kc == 0), stop=(kc == KC - 1),
                         reads=[slotB, (b.hT, (kc, 0))], writes=[ps])
                p.ts("dve", rkt[:, tl, :], ps[:, 0:256], 0.125, None, ALU.mult, reads=[ps], writes=[(rkt, tl)])
            kd = [p.sbuf([128, 64], BF16, f"kd{i}") for i in range(4)]
            stt_ = [p.sbuf([64, 4, 128], F32, f"st{i}") for i in range(2)]
            ki = 0
            for a in range(NPS):
                for d in range(2):
                    ps = b.psn()
                    for h in range(4):
                        for tl2 in range(2):
                            tl = 2 * a + tl2
                            k_ = kd[ki % 4]
                            ki += 1
                            p.ts("dve", k_[:], rkt[:, tl, h * 64:(h + 1) * 64], dec[:, d, h, tl2:tl2 + 1], None, ALU.mult,
                                 reads=[(rkt, tl), (dec, (d, h))], writes=[k_])
                            p.mm(ps[0:64, h * 128:(h + 1) * 128], k_[:], rvt[:, tl, h * 128:(h + 1) * 128],
                                 start=(tl2 == 0), stop=(tl2 == 1), reads=[k_, (rvt, tl)], writes=[ps])
                    st = stt_[(a * 2 + d) % 2]
                    p.copy("act", st[:], ps[0:64, :].rearrange("p (h e) -> p h e", e=128), reads=[ps], writes=[st])
                    b.store(out_st[a, d], st[:], [st], ("o_retst", (a * 2 + d) % 2))
        slotD, wD = b.wload(a_w[:, :, 1440:1952], [128, KC, 512])
        with p.scope("retention:1118"):
            ms = [p.sbuf([128, 512], BF16, f"ms{i}") for i in range(2)]
            decr = p.sbuf([128, TS], F32, "decr")
            qd = [p.sbuf([128, TS], BF16, f"qd{i}") for i in range(2)]
            ro = p.sbuf([128, 512], F32, "ro")
            sq = p.sbuf([128, 512], BF16, "sq")
            rt = p.sbuf([128, 512], F32, "rt")
            rs = p.sbuf([128, 512], F32, "rs")
            yn = p.sbuf([128, 512], BF16, "yn")
            sg = p.sbuf([128, 512], BF16, "sg")
            msi = 0
            acci = 0
            pending = [None]
            for h in range(4):
                j, u = h // 2, h % 2
                r0, r1 = u * 64, (u + 1) * 64
                for d in range(2):
                    if d == 0:
                        p.act(decr[:], tpos[:], AF.Exp, scale=lg[:, h:h + 1], reads=[tpos, lg], writes=[decr])
                    else:
                        p.act(decr[:], tpos[:], AF.Exp, scale=nlg[:, 4 + h:5 + h], bias=lgT[:, 4 + h:5 + h],
                              reads=[tpos, nlg, lgT], writes=[decr])
                    p.tt("dve", qd[d][r0:r1, :], rq[r0:r1, j, TP:T], decr[r0:r1, :], ALU.mult,
                         reads=[(rq, (j, 1)), (rq, (j, 2)), decr], writes=[qd[d]])
                jobs = [(a * SEQ, SEQ, [2 * a, 2 * a + 1], False) for a in range(NPS)]
                jobs += [(TP + qb * 512, 512, list(range(4, 12)), True) for qb in range(2)]
                for (q0, nq, sts, init) in jobs:
                    acc = b.ps[6 + acci % 2]
                    acci += 1
                    pend = {}

                    def rscore(k):
                        sk = sts[k]
                        ps_ = b.psn()
                        p.mm(ps_[:, 0:nq], rk[r0:r1, j, sk * 128:(sk + 1) * 128], rq[r0:r1, j, q0:q0 + nq],
                             reads=[(rk, (j, sk // 4))] + [(rq, (j, t)) for t in tbs_of(q0, q0 + nq)], writes=[ps_])
                        pend[k] = ps_
                    rscore(0)
                    if len(sts) > 1:
                        rscore(1)
                    if len(sts) > 2:
                        rscore(2)
                    if len(sts) > 3:
                        rscore(3)
                    for i, st_ in enumerate(sts):
                        if i + 4 < len(sts):
                            rscore(i + 4)
                        pscore = pend.pop(i)
                        off = (q0 - st_ * 128) + 896
                        m = ms[msi % 2]
                        msi += 1
                        p.tt("dve", m[:, 0:nq], pscore[:, 0:nq], G[:, h, off:off + nq], ALU.mult, reads=[pscore, (G, h)], writes=[m])
                        p.mm(acc[:, 0:nq], rvt[:, st_, h * 128:(h + 1) * 128], m[:, 0:nq], start=(i == 0),
                             stop=(i == len(sts) - 1 and not init), reads=[(rvt, st_), m], writes=[acc])
                    if init:
                        qoff = q0 - TP
                        p.mm(acc[:, 0:nq], S0[r0:r1, 0, j, :], qd[0][r0:r1, qoff:qoff + nq], start=False, stop=False,
                             reads=[S0, qd[0]], writes=[acc])
                        p.mm(acc[:, 0:nq], S0[r0:r1, 1, j, :], qd[1][r0:r1, qoff:qoff + nq], start=False, stop=True,
                             reads=[S0, qd[1]], writes=[acc])
                    def tail(acc=acc, q0=q0, nq=nq, h=h):
                        p.copy("act", ro[:, 0:nq], acc[:, 0:nq], reads=[acc], writes=[ro])
                        p.act(sq[:, 0:nq], ro[:, 0:nq], AF.Square, reads=[ro], writes=[sq])
                        pss = b.psn()
                        p.mm(pss[:, 0:nq], b.ones, sq[:, 0:nq], reads=[sq, b.cbf], writes=[pss])
                        _rms_rstd(b, pss, 128, nq, 128, rt, rs)
                        p.stt(yn[:, 0:nq], ro[:, 0:nq], b.P("retgn")[:, h:h + 1], rs[:, 0:nq], ALU.mult, ALU.mult,
                              reads=[ro, rs, b.prm], writes=[yn])
                        pg = b.psn()
                        for kc in range(KC):
                            p.mm(pg[:, 0:nq], wD[:, kc, h * 128:(h + 1) * 128], b.hT[:, kc, q0:q0 + nq], start=(kc == 0), stop=(kc == KC - 1),
                                 reads=[slotD] + XK(b.hT, kc, q0, q0 + nq), writes=[pg])
                        p.act(sg[:, 0:nq], pg[:, 0:nq], AF.Silu, reads=[pg], writes=[sg])
                        p.tt("pool", oT[:, h, q0:q0 + nq], sg[:, 0:nq], yn[:, 0:nq], ALU.mult, reads=[sg, yn], writes=[(oT, (h, q0))])
                    if pending[0] is not None:
                        pending[0]()
                    pending[0] = tail
            if pending[0] is not None:
                pending[0]()


def conv_ffn(b, l, hook=None):
    p = b.p
    up = b.ffn_up_in[l].rearrange("(kc p) n -> p kc n", p=128)
    down = b.ffn_down_in[l]
    mod = b.mod[l]
    cw = b.P(f"cw{l}")
    cb = b.P(f"cb{l}")
    with p.scope("conv_ffn:1191"):
        ncw = p.sbuf([128, 44 * 3], F32, "ncw")
        p.ts("dve", ncw[:], cw, -1.0, None, ALU.mult, reads=[b.prm], writes=[ncw])
        actT = [p.sbuf([128, 6, T], BF16, f"actT{i}") for i in range(2)]
        acc = [[p.sbuf([128, T], F32, f"acc{i}{j}") for j in range(2)] for i in range(2)]
        sa = [p.sbuf([128, T], BF16, f"sa{i}") for i in range(2)]
        psets = [(b.psall[:, 0:1536], b.ps[0:3]), (b.psall[:, 1536:3072], b.ps[3:6])]
        upslot = None
        for g6 in range(4):
            nfc = 6 if g6 < 3 else 4
            at = actT[g6 % 2]
            for c in range(nfc):
                fc = g6 * 6 + c
                if fc % 3 == 0:
                    ng = min(3, NFC - fc)
                    upslot = b.ring[b.ring_i % 3]
                    b.ring_i += 1
                    upv = upslot.t[:, 0:2 * KC * 384].rearrange("p (h k n) -> p h k n", h=2, k=KC)
                    p.dma("pool", upv[:, 0, :, 0:ng * 128], up[:, :, fc * 128:(fc + ng) * 128], writes=[(upslot, "a")])
                    p.dma("pool", upv[:, 1, :, 0:ng * 128], up[:, :, DFF + fc * 128:DFF + (fc + ng) * 128], writes=[(upslot, "b")])
                ci3 = fc % 3
                par = fc % 2
                for half in range(2):
                    pview, pbufs = psets[half]
                    ch = half * NFC + fc
                    for tb, (c0, c1) in enumerate(TBLK):
                        for kc in range(KC):
                            p.mm(pbufs[tb][:, :], upv[:, half, kc, ci3 * 128:(ci3 + 1) * 128], b.hT[:, kc, c0:c1],
                                 start=(kc == 0), stop=(kc == KC - 1), reads=[(upslot, "ab"[half]), (b.hT, (kc, tb))], writes=[pbufs[tb]])
                    A = acc[par][half]
                    w0 = cw[:, ch * 3 + 0:ch * 3 + 1]
                    w1 = cw[:, ch * 3 + 1:ch * 3 + 2]
                    w2 = cw[:, ch * 3 + 2:ch * 3 + 3]
                    p.act(A[:], pview, AF.Identity, scale=w1, bias=cb[:, ch:ch + 1], reads=list(pbufs) + [b.prm], writes=[A])
                    p.stt(A[:, 1:T], pview[:, 0:T - 1], w0, A[:, 1:T], ALU.mult, ALU.add, reads=list(pbufs) + [A, b.prm], writes=[A])
                    p.stt(A[:, 0:T - 1], pview[:, 1:T], w2, A[:, 0:T - 1], ALU.mult, ALU.add, reads=list(pbufs) + [A, b.prm], writes=[A])
                    p.stt(A[:, SEQ:2 * SEQ + 1:SEQ], pview[:, SEQ - 1:2 * SEQ:SEQ], ncw[:, ch * 3 + 0:ch * 3 + 1], A[:, SEQ:2 * SEQ + 1:SEQ],
                          ALU.mult, ALU.add, reads=list(pbufs) + [A, ncw], writes=[A])
                    p.stt(A[:, SEQ - 1:2 * SEQ:SEQ], pview[:, SEQ:2 * SEQ + 1:SEQ], ncw[:, ch * 3 + 2:ch * 3 + 3], A[:, SEQ - 1:2 * SEQ:SEQ],
                          ALU.mult, ALU.add, reads=list(pbufs) + [A, ncw], writes=[A])
                s_ = sa[par]
                p.act(s_[:], acc[par][0][:], AF.Silu, reads=[acc[par][0]], writes=[s_])
                p.tt("pool", at[:, c, :], s_[:], acc[par][1][:], ALU.mult, reads=[s_, acc[par][1]], writes=[(at, c)])
            dslot, dv = b.wload(down[g6 * 768:g6 * 768 + nfc * 128].rearrange("(c p) n -> p c n", p=128), [128, nfc, 1024])
            for n in range(KC):
                for tb, (c0, c1) in enumerate(TBLK):
                    ps = b.ps[6 + (n * 3 + tb) % 2]
                    for c in range(nfc):
                        p.mm(ps[:, :], dv[:, c, n * 128:(n + 1) * 128], at[:, c, c0:c1], start=(c == 0), stop=(c == nfc - 1),
                             reads=[dslot, (at, c)], writes=[ps])
                    ci = 0 if tb == 0 else 1
                    p.stt(b.xT[:, n, c0:c1], ps[:, :], mod[:, 40 + n, ci:ci + 1], b.xT[:, n, c0:c1], ALU.mult, ALU.add,
                          reads=[ps, mod, (b.xT, (n, tb))], writes=[(b.xT, (n, tb))])
            if hook is not None:
                hook(g6)


def diff_attn(b, oT):
    p = b.p
    bw = b.b_w_in.rearrange("(kc p) n -> p kc n", p=128)
    c1m = 1.0 - LAM_INIT1
    with p.scope("diff_attn:1255"):
        Qd = p.sbuf([128, 4, T], BF16, "Qd")
        Kd = p.sbuf([128, 4, T + PAST], BF16, "Kd")
        Vd = p.sbuf([128, 16, 512], BF16, "Vd")
        C128 = p.sbuf([128, TS], F32, "C128")
        S128 = p.sbuf([128, TS], F32, "S128")
        p.dma("sp", C128[:], b.din("c_C128", [128, TS])[:, :], writes=[C128])
        p.dma("sp", S128[:], b.din("c_S128", [128, TS])[:, :], writes=[S128])
        dkc = b.din("dk_cT", [4, 128, PAST])
        for h in range(4):
            p.dma("pool", Kd[:, h, T:T + PAST], dkc[h], writes=[(Kd, (h, 3))])
        dvc = b.din("dv_c", [PAST, 512])
        p.dma("pool", Vd[:, 12:16, :], dvc.rearrange("(t p) n -> p t n", p=128), writes=[(Vd, 12), (Vd, 13), (Vd, 14), (Vd, 15)])
        lamt = p.sbuf([128, 4], F32, "lamt")
        lprod = p.sbuf([128, 2, 64], F32, "lprod")
        lam = b.P("lam").rearrange("p (r d) -> p r d", d=64)
        p.tt("dve", lprod[:, 0, :], lam[:, 0, :], lam[:, 1, :], ALU.mult, reads=[b.prm], writes=[lprod])
        p.tt("dve", lprod[:, 1, :], lam[:, 2, :], lam[:, 3, :], ALU.mult, reads=[b.prm, lprod], writes=[lprod])
        p.op("dve", lambda e: e.tensor_reduce(out=lamt[:, 0:2], in_=lprod[:], axis=AX.X, op=ALU.add), reads=[lprod], writes=[lamt])
        p.act(lamt[:, 0:2], lamt[:, 0:2], AF.Exp, reads=[lamt], writes=[lamt])
        p.tt("dve", lamt[:, 2:3], lamt[:, 1:2], lamt[:, 0:1], ALU.subtract, reads=[lamt], writes=[lamt])
        p.ts("dve", lamt[:, 3:4], lamt[:, 2:3], -LAM_INIT1, None, ALU.add, reads=[lamt], writes=[lamt])
        nlam = lamt[:, 3:4]
        epsc = p.sbuf([128, 1], F32, "epsc")
        p.memset("dve", epsc[:], EPS / (c1m * c1m), writes=[epsc])
        out_dk = b.dout("o_dkT", [4, 128, TP])
        out_dv = b.dout("o_dv", [TP, 512])
        with p.scope("diff_attn:1285"):
            sqt = [p.sbuf([128, 512], BF16, f"sq{i}") for i in range(2)]
            rt = p.sbuf([128, 512], F32, "rt")
            rs = p.sbuf([128, 512], F32, "rs")
            t1 = p.sbuf([128, 512], F32, "t1")
            t2 = p.sbuf([128, 512], F32, "t2")
            gb = p.sbuf([128, 512], BF16, "gb")
            kst = [p.sbuf([128, 512], F32, f"kst{i}") for i in range(2)]
            si = 0
            for which in range(2):
                slot, wv = b.wload(bw[:, :, which * 512:(which + 1) * 512], [128, KC, 512])
                gname = "dqn" if which == 0 else "dkn"
                gn = b.P(gname)
                dst = Qd if which == 0 else Kd
                for h in range(4):
                    for tb, (c0, c1) in enumerate(TBLK):
                        ps = b.psn()
                        for kc in range(KC):
                            p.mm(ps[:, :], wv[:, kc, h * 128:(h + 1) * 128], b.hT[:, kc, c0:c1], start=(kc == 0), stop=(kc == KC - 1),
                                 reads=[slot, (b.hT, (kc, tb))], writes=[ps])
                        sq = sqt[si % 2]
                        si += 1
                        p.act(sq[:], ps[:], AF.Square, reads=[ps], writes=[sq])
                        pss = b.psn()
                        p.mm(pss[:], b.bd64, sq[:], reads=[sq, b.cbf], writes=[pss])
                        _rms_rstd(b, pss, 128, 512, 64, rt, rs, legacy=True)
                        if tb == 0:
                            if which == 1:
                                ks = kst[h % 2]
                                p.stt(ks[:], ps[:], gn[:, 0:1], rs[:], ALU.mult, ALU.mult, reads=[ps, rs, b.prm], writes=[ks])
                                b.store(out_dk[h], ks[:], [ks], ("o_dk", h % 2))
                                p.copy("act", dst[:, h, c0:c1], ks[:], reads=[ks], writes=[(dst, (h, tb))])
                            else:
                                p.stt(dst[:, h, c0:c1], ps[:], gn[:, 0:1], rs[:], ALU.mult, ALU.mult, reads=[ps, rs, b.prm], writes=[(dst, (h, tb))])
                        else:
                            r0 = c0 - TP
                            p.stt(t1[:], ps[:], gn[:, 0:1], C128[:, r0:r0 + 512], ALU.mult, ALU.mult, reads=[ps, C128, b.prm], writes=[t1])
                            p.act(gb[:], ps[:], AF.Identity, scale=gn[:, 0:1], reads=[ps, b.prm], writes=[gb])
                            pp = b.psn()
                            p.mm(pp[:], b.perm128, gb[:], reads=[gb, b.cbf], writes=[pp])
                            p.tt("dve", t2[:], pp[:], S128[:, r0:r0 + 512], ALU.mult, reads=[pp, S128], writes=[t2])
                            p.tt("pool", t1[:], t1[:], t2[:], ALU.add, reads=[t1, t2], writes=[t1])
                            p.tt("dve", dst[:, h, c0:c1], t1[:], rs[:], ALU.mult, reads=[t1, rs], writes=[(dst, (h, tb))])
            slot, wv = b.wload(bw[:, :, 1024:1536], [128, KC, 512])
            for tl in range(12):
                ps = b.psn()
                for kc in range(KC):
                    p.mm(ps[:, :], b.hT[:, kc, tl * 128:(tl + 1) * 128], wv[:, kc, :], start=(kc == 0), stop=(kc == KC - 1),
                         reads=[slot, (b.hT, (kc, tl // 4))], writes=[ps])
                if tl < 4:
                    vs = kst[tl % 2]
                    p.copy("act", vs[:], ps[:], reads=[ps], writes=[vs])
                    b.store(out_dv[tl * 128:(tl + 1) * 128, :], vs[:], [vs], ("o_dv", tl % 2))
                    p.copy("dve", Vd[:, tl, :], vs[:], reads=[vs], writes=[(Vd, tl)])
                else:
                    p.copy("act" if tl % 2 else "dve", Vd[:, tl, :], ps[:], reads=[ps], writes=[(Vd, tl)])
        with p.scope("diff_attn:1344"):
            ex = [p.sbuf([128, 512], BF16, f"ex{i}") for i in range(3)]
            r1 = p.sbuf([128, 512], F32, "r1")
            r2 = p.sbuf([128, 512], F32, "r2")
            o1b = [p.sbuf([128, 512], F32, f"o1{i}") for i in range(2)]
            o2 = p.sbuf([128, 512], F32, "o2")
            jobi = 0
            pending = [None]
            sq = p.sbuf([128, 512], BF16, "sq")
            rt = p.sbuf([128, 512], F32, "rt")
            rs = p.sbuf([128, 512], F32, "rs")
            exi = 0
            sci = 0
            for h in range(4):
                jobs = [(a * SEQ, SEQ, [2 * a, 2 * a + 1]) for a in range(NPS)]
                jobs += [(TP + qb * 512, 512, list(range(4, 16))) for qb in range(2)]
                for (q0, nq, kts) in jobs:
                    num = [b.ps[4], b.ps[5]]
                    den = [b.ps[6], b.ps[7]]
                    for c in range(2):
                        ra, rb = c * 64, (c + 1) * 64
                        pend = {}

                        def dscore(k):
                            nonlocal sci
                            ktk = kts[k]
                            ps_ = b.ps[sci % 4]
                            sci += 1
                            p.mm(ps_[:, 0:nq], Kd[ra:rb, h, ktk * 128:(ktk + 1) * 128], Qd[ra:rb, h, q0:q0 + nq],
                                 reads=[(Kd, (h, ktk // 4))] + [(Qd, (h, t)) for t in tbs_of(q0, q0 + nq)], writes=[ps_])
                            pend[k] = ps_
                        dscore(0)
                        if len(kts) > 1:
                            dscore(1)
                        if len(kts) > 2:
                            dscore(2)
                        for i, kt in enumerate(kts):
                            if i + 3 < len(kts):
                                dscore(i + 3)
                            pscore = pend.pop(i)
                            e = ex[exi % 3]
                            exi += 1
                            p.act(e[:, 0:nq], pscore[:, 0:nq], AF.Exp, scale=0.125, reads=[pscore], writes=[e])
                            p.mm(num[c][:, 0:nq], Vd[:, kt, h * 128:(h + 1) * 128], e[:, 0:nq], start=(i == 0), stop=(i == len(kts) - 1),
                                 reads=[(Vd, kt), e], writes=[num[c]])
                            p.mm(den[c][:, 0:nq], b.ones, e[:, 0:nq], start=(i == 0), stop=(i == len(kts) - 1),
                                 reads=[b.cbf, e], writes=[den[c]])
                    if DBG_KNOB[1] in (0, 2):
                        p.recip(r1[:, 0:nq], den[0][:, 0:nq], reads=[den[0]], writes=[r1])
                        p.recip(r2[:, 0:nq], den[1][:, 0:nq], reads=[den[1]], writes=[r2])
                    else:
                        p.act(r1[:, 0:nq], den[0][:, 0:nq], AF.Ln, reads=[den[0]], writes=[r1])
                        p.act(r2[:, 0:nq], den[1][:, 0:nq], AF.Ln, reads=[den[1]], writes=[r2])
                        p.act(r1[:, 0:nq], r1[:, 0:nq], AF.Exp, scale=-1.0, reads=[r1], writes=[r1])
                        p.act(r2[:, 0:nq], r2[:, 0:nq], AF.Exp, scale=-1.0, reads=[r2], writes=[r2])
                    o1 = o1b[jobi % 2]
                    jobi += 1
                    p.tt("dve", o1[:, 0:nq], num[0][:, 0:nq], r1[:, 0:nq], ALU.mult, reads=[num[0], r1], writes=[o1])
                    p.tt("dve", o2[:, 0:nq], num[1][:, 0:nq], r2[:, 0:nq], ALU.mult, reads=[num[1], r2], writes=[o2])
                    p.stt(o1[:, 0:nq], o2[:, 0:nq], nlam, o1[:, 0:nq], ALU.mult, ALU.add, reads=[o1, o2, lamt], writes=[o1])

                    def tail(o1=o1, h=h, q0=q0, nq=nq):
                        nonlocal sci
                        p.act(sq[:, 0:nq], o1[:, 0:nq], AF.Square, reads=[o1], writes=[sq])
                        pss = b.ps[sci % 4]
                        sci += 1
                        p.mm(pss[:, 0:nq], b.ones, sq[:, 0:nq], reads=[sq, b.cbf], writes=[pss])
                        p.act(rt[:, 0:nq], pss[:, 0:nq], AF.Ln, bias=epsc[:], scale=1.0 / (128.0 * c1m * c1m), reads=[pss, epsc], writes=[rt])
                        p.act(rs[:, 0:nq], rt[:, 0:nq], AF.Exp, scale=-0.5, reads=[rt], writes=[rs])
                        p.stt(oT[:, h, q0:q0 + nq], o1[:, 0:nq], b.P("dgn")[:, h:h + 1], rs[:, 0:nq], ALU.mult, ALU.mult,
                              reads=[o1, rs, b.prm], writes=[(oT, (h, q0))])
                    if pending[0] is not None:
                        pending[0]()
                    pending[0] = tail
            if pending[0] is not None:
                pending[0]()


CH = 64
NCH = T // CH
SEQS = [(0, 4, False), (4, 4, False), (8, 16, True)]
CDEC = math.exp(-0.5)
DBG_KNOB = [99, 3]


def shift3(b, dst, pview, pbufs, w1, wn, nwn, bias=None, extra=()):
    p = b.p
    rd = list(pbufs) + list(extra)
    if bias is None:
        p.act(dst[:], pview, AF.Identity, scale=w1, reads=rd + [b.prm], writes=[dst])
    else:
        p.act(dst[:], pview, AF.Identity, scale=w1, bias=bias, reads=rd + [b.prm], writes=[dst])
    p.stt(dst[:, 1:T], pview[:, 0:T - 1], wn[0], dst[:, 1:T], ALU.mult, ALU.add, reads=rd + [dst], writes=[dst])
    p.stt(dst[:, 0:T - 1], pview[:, 1:T], wn[1], dst[:, 0:T - 1], ALU.mult, ALU.add, reads=rd + [dst], writes=[dst])
    p.stt(dst[:, SEQ:2 * SEQ + 1:SEQ], pview[:, SEQ - 1:2 * SEQ:SEQ], nwn[0], dst[:, SEQ:2 * SEQ + 1:SEQ], ALU.mult, ALU.add,
          reads=rd + [dst], writes=[dst])
    p.stt(dst[:, SEQ - 1:2 * SEQ:SEQ], pview[:, SEQ:2 * SEQ + 1:SEQ], nwn[1], dst[:, SEQ - 1:2 * SEQ:SEQ], ALU.mult, ALU.add,
          reads=rd + [dst], writes=[dst])


def rwkv(b, oT):
    p = b.p
    bw = b.b_w_in.rearrange("(kc p) n -> p kc n", p=128)
    RW0 = 1536
    psetA = (b.psall[:, 0:1536], b.ps[0:3])
    psetB = (b.psall[:, 1536:3072], b.ps[3:6])
    with p.scope("rwkv:1424"):
        mu = b.P("mu")
        mu1 = p.sbuf([128, 15], F32, "mu1")
        muh = p.sbuf([128, 15], F32, "muh")
        nmuh = p.sbuf([128, 15], F32, "nmuh")
        p.ts("dve", mu1[:], mu, -1.0, 1.0, ALU.mult, ALU.add, reads=[b.prm], writes=[mu1])
        p.ts("dve", muh[:], mu, 0.5, None, ALU.mult, reads=[b.prm], writes=[muh])
        p.ts("dve", nmuh[:], mu, -0.5, None, ALU.mult, reads=[b.prm], writes=[nmuh])
        omka = p.sbuf([128, 4], F32, "omka")
        p.ts("dve", omka[:], b.P("ka"), -1.0, 1.0, ALU.mult, ALU.add, reads=[b.prm], writes=[omka])
        lw_ = p.sbuf([128, 3, 512], BF16, "lora_w")
        wup_in = b.din("rw_w_up", [2, 64, 512])
        aup_in = b.din("rw_a_up", [2, 64, 512])
        p.dma("pool", lw_[:, 0, :], wup_in.rearrange("d l n -> (d l) n"), writes=[(lw_, 0)])
        p.dma("pool", lw_[:, 1, :], aup_in.rearrange("d l n -> (d l) n"), writes=[(lw_, 1)])
        p.dma("pool", lw_[:, 2, :], b.din("rw_g_up", [128, 512])[:, :], writes=[(lw_, 2)])
        masks = p.sbuf([128, 2, 2, 64], BF16, "masks")
        mna = p.sbuf([64, 2, 64], BF16, "mna")
        eye = p.sbuf([64, 64], BF16, "eye")
        p.dma("pool", masks[:, 0, :, :], b.din("c_m_ab", [2, 128, 64]).rearrange("d p c -> p d c"), writes=[(masks, 0)])
        p.dma("pool", masks[:, 1, :, :], b.din("c_m_cd", [2, 128, 64]).rearrange("d p c -> p d c"), writes=[(masks, 1)])
        p.dma("pool", mna[:], b.din("c_m_na", [2, 64, 64]).rearrange("d p c -> p d c"), writes=[mna])
        p.dma("pool", eye[:], b.din("c_eye64", [64, 64])[:, :], writes=[eye])
        onesf = p.sbuf([128, 512], BF16, "onesb")
        p.memset("pool", onesf[:], 1.0, writes=[onesf])
        s0_in = b.din("rw_s0", [2, 8, 64, 64])
        out_st = b.dout("o_rwst", [NPS, 2, 8, 64, 64])
        twd = p.sbuf([128, T], BF16, "twd")
        adb = p.sbuf([128, T], BF16, "adb")
        sgd = p.sbuf([128, T], BF16, "sgd")
        slotL, wL = b.wload(bw[:, :, RW0 + 1536:RW0 + 1920], [128, KC, 384])
        with p.scope("rwkv:1457"):
            tmp = p.sbuf([128, T], F32, "ltmp")
            for i, (dst, fn) in enumerate(((twd, AF.Tanh), (adb, AF.Identity), (sgd, AF.Sigmoid))):
                pview, pbufs = psetA if i % 2 == 0 else psetB
                for tb, (c0, c1) in enumerate(TBLK):
                    for kc in range(KC):
                        p.mm(pbufs[tb][:, :], wL[:, kc, i * 128:(i + 1) * 128], b.hT[:, kc, c0:c1], start=(kc == 0), stop=(kc == KC - 1),
                             reads=[slotL, (b.hT, (kc, tb))], writes=[pbufs[tb]])
                ch = 12 + i
                shift3(b, tmp, pview, pbufs, mu1[:, ch:ch + 1], (muh[:, ch:ch + 1], muh[:, ch:ch + 1]),
                       (nmuh[:, ch:ch + 1], nmuh[:, ch:ch + 1]), extra=[mu1, muh, nmuh])
                p.act(dst[:], tmp[:], fn, reads=[tmp], writes=[dst])
        for j in range(4):
            if DBG_KNOB[0] <= 1 or (DBG_KNOB[0] < 99 and j > 0):
                break
            with p.scope("rwkv:1473"):
                rwkv_pair(b, oT, j, bw, RW0, psetA, psetB, mu1, muh, nmuh, omka, lw_, masks, mna, eye, onesf, twd, adb, sgd,
                          s0_in, out_st)


def rwkv_pair(b, oT, j, bw, RW0, psetA, psetB, mu1, muh, nmuh, omka, lw_, masks, mna, eye, onesb, twd, adb, sgd, s0_in, out_st):
    p = b.p
    R0, R1 = slice(0, 64), slice(64, 128)
    RU = [R0, R1]
    vb = p.sbuf([128, T], BF16, "vb")
    rtile = [p.sbuf([128, T], BF16, f"rtl{d}") for d in range(2)]
    kkt = [p.sbuf([128, T], BF16, f"kkt{d}") for d in range(2)]
    BK = [p.sbuf([128, NCH, 2, CH], BF16, f"BK{d}") for d in range(2)]
    eLend = [p.sbuf([128, NCH], F32, f"eLe{d}") for d in range(2)]
    bsum = p.sbuf([128, T], BF16, "bsum")
    with p.scope("rwkv_pair:1490"):
        rb = p.sbuf([128, T], BF16, "rb")
        kb = p.sbuf([128, T], BF16, "kb")
        kkb = p.sbuf([128, T], BF16, "kkb")
        with p.scope("rwkv_pair:1494"):
            FB = [p.sbuf([128, T], F32, f"f{i}") for i in range(3)]
            slot = b.ring[b.ring_i % 3]
            b.ring_i += 1
            wv = slot.t[:, 0:3 * KC * 128].rearrange("p (i k n) -> p i k n", i=3, k=KC)
            for i in range(3):
                c_ = RW0 + i * 512 + j * 128
                p.dma("pool", wv[:, i, :, :], bw[:, :, c_:c_ + 128], writes=[(slot, i)])
            for i, dstb in enumerate((rb, kb, vb)):
                pview, pbufs = psetA if i % 2 == 0 else psetB
                for tb, (c0, c1) in enumerate(TBLK):
                    for kc in range(KC):
                        p.mm(pbufs[tb][:, :], wv[:, i, kc, :], b.hT[:, kc, c0:c1], start=(kc == 0), stop=(kc == KC - 1),
                             reads=[(slot, i), (b.hT, (kc, tb))], writes=[pbufs[tb]])
                ch = i * 4 + j
                f0 = FB[i]
                f1 = FB[i]
                shift3(b, f0, pview, pbufs, mu1[:, ch:ch + 1], (muh[:, ch:ch + 1], muh[:, ch:ch + 1]),
                       (nmuh[:, ch:ch + 1], nmuh[:, ch:ch + 1]), extra=[mu1, muh, nmuh])
                p.copy("act", dstb[:], f0[:], reads=[f0], writes=[dstb])
                if i == 1:
                    p.ts("dve", f1[:], f0[:], b.P("kk")[:, j:j + 1], None, ALU.mult, reads=[f0, b.prm], writes=[f1])
                    sq = p.sbuf([128, 512], BF16, "sq")
                    rt = p.sbuf([128, 512], F32, "rt")
                    rs = p.sbuf([128, 512], F32, "rs")
                    for tb, (c0, c1) in enumerate(TBLK):
                        p.act(sq[:], f1[:, c0:c1], AF.Square, reads=[f1], writes=[sq])
                        pss = b.ps[6 + tb % 2]
                        p.mm(pss[:], b.bd64, sq[:], reads=[sq, b.cbf], writes=[pss])
                        _rms_rstd(b, pss, 128, 512, 1.0, rt, rs)
                        p.tt("dve", kkb[:, c0:c1], f1[:, c0:c1], rs[:], ALU.mult, reads=[f1, rs], writes=[(kkb, tb)])
        with p.scope("rwkv_pair:C"):
            TS_ = []
            for d in range(2):
                TS_.append(dict(fs=p.sbuf([128, 512], F32, "fs"), fa=p.sbuf([128, 512], F32, "fa"), fe=p.sbuf([128, 512], F32, "fe"),
                                fg=p.sbuf([128, 512], F32, "fg"), fkd=p.sbuf([128, 512], BF16, "fkd"), fb=p.sbuf([128, 512], BF16, "fb"),
                                gs=p.sbuf([128, 8], F32, "gs")))
            rrk = b.P("rrk")
            ka = b.P("ka")

            def c_block(d, tb):
                t_ = TS_[d]
                fs, fa, fe, fg, fkd, fb, gs = t_["fs"], t_["fa"], t_["fe"], t_["fg"], t_["fkd"], t_["fb"], t_["gs"]
                DR = RU[d]
                c0, c1 = TBLK[tb]
                ch0 = c0 // CH
                ps1 = b.ps[6 - 2 * d]
                ps2 = b.ps[7 - 2 * d]
                fgv = fg.t[:, :].rearrange("p (c t) -> p c t", t=CH)
                fdv = fa.t[:, :].rearrange("p (c t) -> p c t", t=CH)
                fev = fe.t[:, :].rearrange("p (c t) -> p c t", t=CH)
                sg_ = -CDEC if d == 0 else CDEC
                ecol = CH - 1 if d == 0 else 0
                ops = []
                A_ = ops.append
                A_(lambda: p.mm(ps1[:], lw_[DR, 0, j * 128:(j + 1) * 128], twd[DR, c0:c1], reads=[(lw_, 0), twd], writes=[ps1]))
                A_(lambda: p.act(fs[:], ps1[:], AF.Sigmoid, bias=b.P("w0")[:, d * 4 + j:d * 4 + j + 1], scale=1.0, reads=[ps1, b.prm], writes=[fs]))
                A_(lambda: p.mm(ps2[:], lw_[DR, 1, j * 128:(j + 1) * 128], adb[DR, c0:c1], reads=[(lw_, 1), adb], writes=[ps2]))
                A_(lambda: p.act(fa[:], ps2[:], AF.Sigmoid, bias=b.P("a0")[:, d * 4 + j:d * 4 + j + 1], scale=1.0, reads=[ps2, b.prm], writes=[fa]))
                A_(lambda: p.ts("dve", fe[:], fa[:], ka[:, j:j + 1], omka[:, j:j + 1], ALU.mult, ALU.add, reads=[fa, b.prm, omka], writes=[fe]))
                A_(lambda: p.tt("dve", fkd[:], kb[:, c0:c1], fe[:], ALU.mult, reads=[kb, fe], writes=[fkd]))
                A_(lambda: p.stt(fe[:], rb[:, c0:c1], rrk[:, j:j + 1], fkd[:], ALU.mult, ALU.mult, reads=[rb, fkd, b.prm], writes=[fe]))
                A_(lambda: p.tt("pool", bsum[:, c0:c1], bsum[:, c0:c1], fe[:], ALU.add, reads=[(bsum, tb), fe], writes=[(bsum, tb)]))
                A_(lambda: p.tt("dve", fb[:], fa[:], kkb[:, c0:c1], ALU.mult, reads=[fa, (kkb, tb)], writes=[fb]))
                A_(lambda: p.op("dve", lambda e: e.tensor_tensor_scan(out=fg[:], data0=onesb[:, 0:512], data1=fs[:], initial=0.0,
                                                                      op0=ALU.mult, op1=ALU.add), reads=[fs, onesb], writes=[fg]))
                if d == 0:
                    A_(lambda: p.memset("dve", gs[:, 0:1], 0.0, writes=[gs]))
                    A_(lambda: p.copy("dve", gs[:, 1:8], fg[:, CH - 1:512 - 1:CH], reads=[fg], writes=[gs]))
                    A_(lambda: p.tt("dve", fdv, fgv, gs[:, :].unsqueeze(2).to_broadcast([128, 8, CH]), ALU.subtract, reads=[fg, gs], writes=[fa]))
                else:
                    A_(lambda: p.tt("dve", fe[:], fg[:], fs[:], ALU.subtract, reads=[fg, fs], writes=[fe]))
                    A_(lambda: p.tt("dve", fdv, fev, fgv[:, :, CH - 1:CH].to_broadcast([128, 8, CH]), ALU.subtract, reads=[fg, fe], writes=[fa]))
                A_(lambda: p.act(fe[:], fa[:], AF.Exp, scale=sg_, reads=[fa], writes=[fe]))
                A_(lambda: p.tt("dve", rtile[d][:, c0:c1], rb[:, c0:c1], fe[:], ALU.mult, reads=[rb, fe], writes=[(rtile[d], tb)]))
                A_(lambda: p.copy("dve", eLend[d][:, ch0:ch0 + 8], fe[:, ecol:512:CH], reads=[fe], writes=[(eLend[d], tb)]))
                A_(lambda: p.act(fe[:], fa[:], AF.Exp, scale=-sg_, reads=[fa], writes=[fe]))
                A_(lambda: p.tt("dve", BK[d][:, ch0:ch0 + 8, 0, :], fb.t[:, :].rearrange("p (c t) -> p c t", t=CH), fev, ALU.mult,
                                reads=[fb, fe], writes=[(BK[d], (tb, 0))]))
                A_(lambda: p.tt("pool", BK[d][:, ch0:ch0 + 8, 1, :], fkd.t[:, :].rearrange("p (c t) -> p c t", t=CH), fev, ALU.mult,
                                reads=[fkd, fe], writes=[(BK[d], (tb, 1))]))
                A_(lambda: p.tt("dve", fa[:], fa[:], fs[:], ALU.subtract if d == 0 else ALU.add, reads=[fa, fs], writes=[fa]))
                A_(lambda: p.act(fe[:], fa[:], AF.Exp, scale=sg_, reads=[fa], writes=[fe]))
                A_(lambda: p.tt("dve", kkt[d][:, c0:c1], kkb[:, c0:c1], fe[:], ALU.mult, reads=[(kkb, tb), fe], writes=[(kkt[d], tb)]))
                return ops

            p.memset("pool", bsum[:], 0.0, writes=[bsum])
            for tb in range(3):
                o0, o1 = c_block(0, tb), c_block(1, tb)
                for i in range(max(len(o0), len(o1))):
                    if i < len(o0):
                        o0[i]()
                    if i < len(o1):
                        o1[i]()
    if DBG_KNOB[0] <= 2:
        return
    Zall = p.sbuf([128, NCH, 2, CH], BF16, "Zall")
    y = p.sbuf([128, T], BF16, "y")
    p.memset("pool", y[:], 0.0, writes=[y])
    for g in range(NCH // 8):
        E = 6
        pvu = b.psall[0:64, E * 512:(E + 2) * 512].rearrange("p (u c v) -> p u c v", u=2, c=8)
        for u in range(2):
            for cc in range(8):
                ch = g * 8 + cc
                p.mm(pvu[:, u, cc, :], vb[RU[u], ch * CH:(ch + 1) * CH], b.ident[RU[u], u * 64:(u + 1) * 64],
                     reads=[vb, b.cbf], writes=[b.ps[E + u]])
        p.copy("act" if g % 2 else "dve", Zall[64:128, g * 8:(g + 1) * 8, :, :].rearrange("p c u v -> p u c v"), pvu,
               reads=[b.ps[E], b.ps[E + 1]], writes=[(Zall, ("v", g))])
    if DBG_KNOB[0] == 3 and j == 0:
        dz = b.dout("dbg_Z", [128, NCH, 2, CH])
        b.out_dmas.append(p.dma("pool", dz[64:128, :, :, :], Zall[64:128, :, :, :], reads=[Zall], sem="dbg_Z"))
        dvb = b.dout("dbg_vb", [128, T])
        b.out_dmas.append(p.dma("pool", dvb[:, :], vb[:], reads=[vb], sem="dbg_vb"))
    if DBG_KNOB[0] <= 3:
        return
    with p.scope("rwkv_pair:1613"):
        bufsets = []
        for _ in range(4):
            st = {}
            st["Hf"] = p.sbuf([128, 64], F32, "Hf")
            st["Hb"] = p.sbuf([128, 64], BF16, "Hb")
            st["ht"] = p.sbuf([128, 64], F32, "ht")
            for nm, shp in (("NtB", [128, 2, CH]), ("CDt", [128, 2, CH]), ("BKt", [128, 2, CH]), ("Pm", [64, 2, CH])):
                st[nm] = [p.sbuf(shp, BF16, nm) for _ in range(2)]
            st["Ntp"] = [p.sbuf([64, 2, CH], BF16, "Ntp") for _ in range(2)]
            st["Nap"] = [p.sbuf([64, 2, CH], BF16, "Nap") for _ in range(2)]
            st["Pp"] = p.sbuf([64, 2, CH], BF16, "Pp")
            st["Xs"] = p.sbuf([64, 2, CH], BF16, "Xs")
            st["X2"] = p.sbuf([64, 2, CH], F32, "X2")
            bufsets.append(st)

        def make_chains(seq_ids):
            chains = []
            for d in range(2):
                for si in seq_ids:
                    cs, n, init = SEQS[si]
                    order = list(range(cs, cs + n)) if d == 0 else list(range(cs + n - 1, cs - 1, -1))
                    st = dict(bufsets[len(chains)])
                    st.update(d=d, si=si, order=order, init=init)
                    if init:
                        p.dma("sp", st["Hf"][:], s0_in[d, 2 * j:2 * j + 2].rearrange("u k v -> (u k) v"), writes=[st["Hf"]])
                    else:
                        p.memset("dve", st["Hf"][:], 0.0, writes=[st["Hf"]])
                    p.copy("act", st["Hb"][:], st["Hf"][:], reads=[st["Hf"]], writes=[st["Hb"]])
                    chains.append(st)
            return chains

        psi = [0]

        def nps():
            psi[0] += 1
            return b.ps[psi[0] % 8]

        pri = [0]

        def npair():
            pri[0] += 1
            return 2 * (pri[0] % 4)

        def bc(ap):
            return ap.unsqueeze(1).to_broadcast([ap.shape[0], 2, CH])

        def pre_stages(st, k):
            d = st["d"]
            ch = st["order"][k]
            cc = slice(ch * CH, (ch + 1) * CH)
            tb = (ch * CH) // 512
            par = k % 2
            NtB, CDt, BKt, Pm = st["NtB"][par], st["CDt"][par], st["BKt"][par], st["Pm"][par]
            Ntp, Nap = st["Ntp"], st["Nap"]
            bkr = [(BK[d], (tb, 0)), (BK[d], (tb, 1))]

            def s0():
                E = npair()
                pe2 = [b.ps[E], b.ps[E + 1]]
                pu3 = b.psall[:, E * 512:(E + 2) * 512].rearrange("p (u x) -> p u x", u=2)
                for u in range(2):
                    bku = BK[d][RU[u], ch, :, :].rearrange("p a t -> p (a t)")
                    p.mm(pu3[:, u, 0:64], bku, kkt[d][RU[u], cc], reads=bkr + [(kkt[d], tb)], writes=[pe2[u]])
                    p.mm(pu3[0:64, u, 64:128], kkt[d][RU[u], cc], BK[d][RU[u], ch, 0, :], reads=bkr + [(kkt[d], tb)], writes=[pe2[u]])
                    p.mm(pu3[:, u, 128:192], bku, rtile[d][RU[u], cc], reads=bkr + [(rtile[d], tb)], writes=[pe2[u]])
                p.tt("dve", NtB[:], pu3[:, :, 0:64], bc(masks[:, 0, d, :]), ALU.mult, reads=pe2 + [(masks, 0)], writes=[NtB])
                p.tt("dve", Nap[0][:], pu3[0:64, :, 64:128], bc(mna[:, d, :]), ALU.mult, reads=pe2 + [mna], writes=[Nap[0]])
                p.tt("dve", CDt[:], pu3[:, :, 128:192], bc(masks[:, 1, d, :]), ALU.mult, reads=pe2 + [(masks, 1)], writes=[CDt])
                p.tt("pool", st["Pp"][:], NtB[0:64, :, :], bc(eye[:, :]), ALU.add, reads=[NtB, eye], writes=[st["Pp"]])
            yield s0

            for i in range(5):
                def lv(i=i):
                    Nt_i = NtB[0:64, :, :] if i == 0 else Ntp[i % 2]
                    Nt_r = NtB if i == 0 else Ntp[i % 2]
                    Na_i = Nap[i % 2]
                    Na_n = Nap[(i + 1) % 2]
                    pq = nps()
                    pqv = pq.t[0:64, 0:128].rearrange("p (u t) -> p u t", u=2)
                    for u in range(2):
                        p.mm(pqv[:, u, :], Nt_i[:, u, :], Na_i[:, u, :], reads=[Nt_r, Na_i], writes=[pq])
                    p.copy("act", Na_n[:], pqv, reads=[pq], writes=[Na_n])
                    if i < 4:
                        Nt_n = Ntp[(i + 1) % 2]
                        pq2 = nps()
                        pq2v = pq2.t[0:64, 0:128].rearrange("p (u t) -> p u t", u=2)
                        for u in range(2):
                            p.mm(pq2v[:, u, :], Na_i[:, u, :], Nt_i[:, u, :], reads=[Nt_r, Na_i], writes=[pq2])
                        p.copy("act", Nt_n[:], pq2v, reads=[pq2], writes=[Nt_n])
                yield lv

                def pu(i=i):
                    Na_n = Nap[(i + 1) % 2]
                    Pin = st["Pp"] if i % 2 == 0 else Pm
                    Pout = Pm if i % 2 == 0 else st["Pp"]
                    pq = nps()
                    pqv = pq.t[0:64, 0:128].rearrange("p (u t) -> p u t", u=2)
                    for u in range(2):
                        p.mm(pqv[:, u, :], Na_n[:, u, :], Pin[:, u, :], reads=[Na_n, Pin], writes=[pq])
                    p.tt("dve", Pout[:], Pin[:], pqv, ALU.add, reads=[Pin, pq], writes=[Pout])
                yield pu

            def s5():
                E = npair()
                pe2 = [b.ps[E], b.ps[E + 1]]
                pu3 = b.psall[:, E * 512:(E + 2) * 512].rearrange("p (u x) -> p u x", u=2)
                for u in range(2):
                    bku = BK[d][RU[u], ch, :, :].rearrange("p a t -> p (a t)")
                    p.mm(pu3[:, u, 0:64], bku, b.ident[RU[u], u * 64:(u + 1) * 64], reads=bkr + [b.cbf], writes=[pe2[u]])
                p.copy("act", BKt[:], pu3[:, :, 0:64], reads=pe2, writes=[BKt])
            yield s5

        def seq_stages(st, k):
            d = st["d"]
            ch = st["order"][k]
            cc = slice(ch * CH, (ch + 1) * CH)
            tb = (ch * CH) // 512
            par = k % 2
            NtB, CDt, BKt, Pm = st["NtB"][par], st["CDt"][par], st["BKt"][par], st["Pm"][par]
            Hf, Hb, Xs, ht = st["Hf"], st["Hb"], st["Xs"], st["ht"]
            zv = (Zall, ("v", ch // 8))
            zu = (Zall, ("u", ch))

            def s1():
                E = npair()
                pe2 = [b.ps[E], b.ps[E + 1]]
                pu3 = b.psall[0:64, E * 512:(E + 2) * 512].rearrange("p (u x) -> p u x", u=2)
                px = b.ps[npair()]
                pxv = px.t[0:64, 0:128].rearrange("p (u t) -> p u t", u=2)
                for u in range(2):
                    p.mm(pu3[:, u, 0:64], kkt[d][RU[u], cc], Hb[RU[u], :], reads=[(kkt[d], tb), Hb], writes=[pe2[u]])
                for u in range(2):
                    p.mm(pxv[:, u, :], NtB[64:128, u, :], Zall[64:128, ch, u, :], reads=[NtB, zv], writes=[px])
                p.copy("act", st["X2"][:], pxv, reads=[px], writes=[st["X2"]])
                p.tt("dve", Xs[:], pu3[:, :, 0:64], st["X2"][:], ALU.add, reads=pe2 + [st["X2"]], writes=[Xs])
            yield s1

            def s2():
                pu_ = nps()
                puv = pu_.t[0:64, 0:128].rearrange("p (u t) -> p u t", u=2)
                for u in range(2):
                    p.mm(puv[:, u, :], Pm[:, u, :], Xs[:, u, :], reads=[Pm, Xs], writes=[pu_])
                p.ts("dve", Zall[0:64, ch, :, :], puv, -1.0, None, ALU.mult, reads=[pu_], writes=[zu])
            yield s2

            def s3():
                py = nps()
                ph = nps()
                for u in range(2):
                    p.mm(py[RU[u], 0:CH], Hb[RU[u], :], rtile[d][RU[u], cc], start=True, stop=False, reads=[Hb, (rtile[d], tb)], writes=[py])
                    p.mm(py[RU[u], 0:CH], Zall[:, ch, u, :], CDt[:, u, :], start=False, stop=True, reads=[zv, zu, CDt], writes=[py])
                    p.mm(ph[RU[u], 0:CH], BKt[:, u, :], Zall[:, ch, u, :], reads=[BKt, zv, zu], writes=[ph])
                p.tt("dve", y[:, cc], y[:, cc], py[:, 0:CH], ALU.add, reads=[(y, ch), py], writes=[(y, ch)])
                p.tt("dve", ht[:], Hf[:], ph[:, 0:CH], ALU.add, reads=[Hf, ph], writes=[ht])
                p.ts("dve", Hf[:], ht[:], eLend[d][:, ch:ch + 1], None, ALU.mult, reads=[ht, (eLend[d], tb)], writes=[Hf])
                p.copy("act", Hb[:], Hf[:], reads=[Hf], writes=[Hb])
                if k == len(st["order"]) - 1 and not st["init"]:
                    a = st["si"]
                    b.store(out_st[a, d, 2 * j:2 * j + 2].rearrange("u k v -> (u k) v"), Hf[:], [Hf], ("o_rwst", a, d))
            yield s3

        for seq_ids in ((0, 1), (2,)):
            chains = make_chains(seq_ids)
            maxk = max(len(st["order"]) for st in chains)
            for k in range(-1, maxk):
                if DBG_KNOB[0] == 4 and k >= 0:
                    break
                if DBG_KNOB[0] == 5 and k >= 1:
                    break
                gens = []
                for st in chains:
                    n = len(st["order"])
                    if 0 <= k + 1 < n:
                        gens.append(pre_stages(st, k + 1))
                    if 0 <= k < n:
                        gens.append(seq_stages(st, k))
                active = [iter(g) for g in gens]
                while active:
                    nxt = []
                    for it in active:
                        try:
                            f = next(it)
                        except StopIteration:
                            continue
                        f()
                        nxt.append(it)
                    active = nxt
    if DBG_KNOB[0] <= 6:
        return
    with p.scope("rwkv_pair:1807"):
        sq = p.sbuf([128, 512], BF16, "sq")
        rt = p.sbuf([128, 512], F32, "rt")
        rs = p.sbuf([128, 512], F32, "rs")
        yn = p.sbuf([128, 512], F32, "yn")
        bo = p.sbuf([128, 512], F32, "bo")
        for tb, (c0, c1) in enumerate(TBLK):
            p.act(sq[:], y[:, c0:c1], AF.Square, reads=[y], writes=[sq])
            pss = b.psn()
            p.mm(pss[:], b.bd64, sq[:], reads=[sq, b.cbf], writes=[pss])
            _rms_rstd(b, pss, 128, 512, 64, rt, rs)
            p.stt(yn[:], y[:, c0:c1], b.P("rgn")[:, j:j + 1], rs[:], ALU.mult, ALU.mult, reads=[y, rs, b.prm], writes=[yn])
            psb = b.psn()
            p.mm(psb[:], b.bd64, bsum[:, c0:c1], reads=[(bsum, tb), b.cbf], writes=[psb])
            p.tt("dve", bo[:], psb[:], vb[:, c0:c1], ALU.mult, reads=[psb, vb], writes=[bo])
            p.tt("pool", yn[:], yn[:], bo[:], ALU.add, reads=[yn, bo], writes=[yn])
            psg = b.psn()
            p.mm(psg[:], lw_[:, 2, j * 128:(j + 1) * 128], sgd[:, c0:c1], reads=[(lw_, 2), sgd], writes=[psg])
            p.tt("dve", oT[:, j, c0:c1], yn[:], psg[:], ALU.mult, reads=[yn, psg], writes=[(oT, (j, tb))])
```

```python
import math
import numpy as np
import concourse.bass as bass
import concourse.mybir as mybir
from concourse.bass_utils import run_bass_kernel_spmd

F32 = mybir.dt.float32
BF16 = mybir.dt.bfloat16
AF = mybir.ActivationFunctionType
ALU = mybir.AluOpType
AX = mybir.AxisListType

ENGS = ["pe", "act", "dve", "pool", "sp"]


class Buf:
    def __init__(self, t, name):
        self.t = t
        self.name = name
        self.writers = {}
        self.readers = {}

    def __getitem__(self, idx):
        return self.t[idx]


class Op:
    __slots__ = ("eng", "idx", "fn", "deps", "is_dma", "dsem", "dval", "signal",
                 "sigval", "waits", "snap", "gidx")

    def __init__(self, eng, fn):
        self.eng = eng
        self.fn = fn
        self.deps = {}
        self.is_dma = False
        self.dsem = None
        self.dval = 0
        self.signal = False
        self.sigval = 0
        self.waits = []
        self.snap = None


def _norm(acc):
    out = []
    for a in acc:
        if a is None:
            continue
        if isinstance(a, Buf):
            out.append((a, None))
        else:
            out.append((a[0], a[1]))
    return out


class Prog:
    def __init__(self, nc):
        self.nc = nc
        self.ops = {e: [] for e in ENGS}
        self.allops = []
        self.dma_cnt = {}
        self.keymap = {}
        self.free_sems = []
        self.sem_q = {}
        self.ctx = []
        self.nbuf = 0
        self.dmas_since_bar = []
        self.bar_labels = []

    def sbuf(self, shape, dt, name=None):
        self.nbuf += 1
        name = (name or "sb") + f"_{self.nbuf}"
        cm = self.nc.sbuf_tensor(name, list(shape), dt)
        t = cm.__enter__()
        self.ctx.append(cm)
        return Buf(t, name)

    def psum(self, shape, dt, name=None):
        self.nbuf += 1
        name = (name or "ps") + f"_{self.nbuf}"
        cm = self.nc.psum_tensor(name, list(shape), dt)
        t = cm.__enter__()
        self.ctx.append(cm)
        return Buf(t, name)

    def _track(self, op, reads, writes):
        def add(d, raw):
            if d is op:
                return
            if raw or d not in op.deps:
                op.deps[d] = raw or op.deps.get(d, False)
        for b, k in _norm(reads):
            if k is None:
                for w in b.writers.values():
                    add(w, True)
            else:
                w = b.writers.get(k)
                if w is not None:
                    add(w, True)
                w = b.writers.get(None)
                if w is not None:
                    add(w, True)
            b.readers.setdefault(k, []).append(op)
        for b, k in _norm(writes):
            if k is None:
                for w in b.writers.values():
                    add(w, False)
                for rl in b.readers.values():
                    for r in rl:
                        add(r, False)
                b.writers = {None: op}
                b.readers = {}
            else:
                for kk in (k, None):
                    w = b.writers.get(kk)
                    if w is not None:
                        add(w, False)
                    for r in b.readers.get(kk, ()):
                        add(r, False)
                b.writers[k] = op
                b.readers[k] = []

    def op(self, eng, fn, reads=(), writes=()):
        o = Op(eng, fn)
        o.idx = len(self.ops[eng])
        o.gidx = len(self.allops)
        self.ops[eng].append(o)
        self.allops.append(o)
        self._track(o, reads, writes)
        return o

    def dma(self, q, out, in_, reads=(), writes=(), sem=None, **kw):
        def fn(e, out=out, in_=in_, kw=kw):
            return e.dma_start(out=out, in_=in_, **kw)
        o = self.op(q, fn, reads, writes)
        o.is_dma = True
        if sem is None:
            b, k = _norm(writes)[0]
            sem = (b.name, k)
        sem = (q, sem)
        if sem not in self.keymap:
            fl = [i for i in self.free_sems if self.sem_q[i] == q]
            if fl:
                self.free_sems.remove(fl[0])
                self.keymap[sem] = fl[0]
            else:
                self.keymap[sem] = len(self.dma_cnt)
                self.dma_cnt[self.keymap[sem]] = 0
                self.sem_q[self.keymap[sem]] = q
        sem = self.keymap[sem]
        o.dsem = sem
        self.dma_cnt[sem] += 1
        o.dval = 16 * self.dma_cnt[sem]
        self.dmas_since_bar.append(o)
        return o

    def barrier(self):
        if getattr(self, "_bar_mark", -1) == len(self.allops):
            return
        last = [self.ops[e][-1] for e in ENGS if self.ops[e]]
        dm = list(self.dmas_since_bar)
        self.dmas_since_bar = []
        self.free_sems = sorted(set(self.free_sems) | set(self.keymap.values()))
        self.keymap = {}
        for e in ENGS:
            o = self.op(e, lambda en: en.nop())
            for d in last + dm:
                if d is not o:
                    o.deps[d] = True
        self._bar_mark = len(self.allops)

    class _Scope:
        def __init__(self, p, name=None):
            self.p = p
            self.name = name

        def __enter__(self):
            self.n = len(self.p.ctx)
            return self

        def __exit__(self, *a):
            self.p.bar_labels.append(self.name)
            self.p.barrier()
            while len(self.p.ctx) > self.n:
                self.p.ctx.pop().__exit__(None, None, None)
            return False

    def scope(self, name=None):
        return Prog._Scope(self, name)

    def mm(self, out, lhsT, rhs, start=True, stop=True, reads=(), writes=()):
        return self.op("pe", lambda e: e.matmul(out, lhsT, rhs, start=start, stop=stop), reads, writes)

    def act(self, out, in_, func, reads=(), writes=(), **kw):
        return self.op("act", lambda e: e.activation(out=out, in_=in_, func=func, **kw), reads, writes)

    def tt(self, eng, out, in0, in1, op, reads=(), writes=()):
        return self.op(eng, lambda e: e.tensor_tensor(out=out, in0=in0, in1=in1, op=op), reads, writes)

    def ts(self, eng, out, in0, s1, s2, op0, op1=None, reads=(), writes=()):
        if op1 is None:
            return self.op(eng, lambda e: e.tensor_scalar(out=out, in0=in0, scalar1=s1, scalar2=None, op0=op0),
                           reads, writes)
        return self.op(eng, lambda e: e.tensor_scalar(out=out, in0=in0, scalar1=s1, scalar2=s2, op0=op0, op1=op1),
                       reads, writes)

    def stt(self, out, in0, scalar, in1, op0, op1, reads=(), writes=()):
        return self.op("dve", lambda e: e.scalar_tensor_tensor(out=out, in0=in0, scalar=scalar, in1=in1,
                                                               op0=op0, op1=op1), reads, writes)

    def copy(self, eng, out, in_, reads=(), writes=()):
        if eng == "act":
            return self.op(eng, lambda e: e.copy(out=out, in_=in_), reads, writes)
        return self.op(eng, lambda e: e.tensor_copy(out=out, in_=in_), reads, writes)

    def memset(self, eng, ap, val, writes=()):
        return self.op(eng, lambda e: e.memset(ap, val), (), writes)

    def recip(self, out, in_, reads=(), writes=()):
        return self.op("dve", lambda e: e.reciprocal(out=out, in_=in_), reads, writes)

    def emit(self, final_dma_ops=()):
        nc = self.nc
        obs_eng = {e: {p: -1 for p in ENGS} for e in ENGS}
        obs_dma = {e: {} for e in ENGS}
        for o in self.allops:
            oe = obs_eng[o.eng]
            od = obs_dma[o.eng]
            need = {}
            dneed = {}
            for d, raw in o.deps.items():
                if (not d.is_dma) and d.eng == o.eng and o.eng == "pe":
                    continue
                if d.is_dma:
                    if od.get(d.dsem, 0) < d.dval:
                        if dneed.get(d.dsem, (0, None))[0] < d.dval:
                            dneed[d.dsem] = (d.dval, d)
                else:
                    if oe[d.eng] < d.idx:
                        if d.eng not in need or need[d.eng].idx < d.idx:
                            need[d.eng] = d
            for pe_, d in need.items():
                if oe[pe_] >= d.idx:
                    continue
                d.signal = True
                o.waits.append(("eng", d))
                oe[pe_] = max(oe[pe_], d.idx)
                if d.snap is not None:
                    se, sd = d.snap
                    for k, v in se.items():
                        if k != o.eng and oe[k] < v:
                            oe[k] = v
                    for k, v in sd.items():
                        if od.get(k, 0) < v:
                            od[k] = v
            for sk, (v, d) in dneed.items():
                o.waits.append(("dma", sk, v))
                od[sk] = max(od.get(sk, 0), v)
                if d.snap is not None:
                    se, sd = d.snap
                    for k, vv in se.items():
                        if k != o.eng and oe[k] < vv:
                            oe[k] = vv
                    for k, vv in sd.items():
                        if od.get(k, 0) < vv:
                            od[k] = vv
            o.snap = (dict(oe), dict(od))
        fin_waits = {k: 16 * v for k, v in self.dma_cnt.items() if v > 0}
        for e in ENGS:
            c = 0
            for o in self.ops[e]:
                if o.signal:
                    c += 1
                    o.sigval = c
        sem_cms = []
        esem = {}
        for e in ENGS:
            cm = nc.semaphore(f"s_{e}")
            esem[e] = cm.__enter__()
            sem_cms.append(cm)
        dsem = {}
        for i, k in enumerate(self.dma_cnt.keys()):
            cm = nc.semaphore(f"d_{i}")
            dsem[k] = cm.__enter__()
            sem_cms.append(cm)
        self.n_sems = len(sem_cms)

        def run(engname, eobj):
            for o in self.ops[engname]:
                for w in o.waits:
                    if w[0] == "eng":
                        d = w[1]
                        eobj.wait_ge(esem[d.eng], d.sigval)
                    else:
                        eobj.wait_ge(dsem[w[1]], w[2])
                ins = o.fn(eobj)
                if o.is_dma:
                    ins.then_inc(dsem[o.dsem], 16)
                elif o.signal:
                    ins.then_inc(esem[o.eng], 1)
            if engname == "sp":
                for k, v in fin_waits.items():
                    eobj.wait_ge(dsem[k], v)

        with nc.Block() as block:
            @block.tensor
            def _(e):
                run("pe", e)

            @block.scalar
            def _(e):
                run("act", e)

            @block.vector
            def _(e):
                run("dve", e)

            @block.gpsimd
            def _(e):
                run("pool", e)

            @block.sync
            def _(e):
                run("sp", e)
        for cm in reversed(sem_cms):
            cm.__exit__(None, None, None)
        while self.ctx:
            self.ctx.pop().__exit__(None, None, None)


D = 1024
KC = 8
NPS = 2
SEQ = 256
TP = NPS * SEQ
TS = 1024
T = TP + TS
PAST = 512
EPS = 1e-6
DFF = 2816
NFC = 22
GRID_W = 64
GROUPS = [(0, TP, 0), (TP, T, 1)]
TBLK = [(0, 512), (512, 1024), (1024, 1536)]
RING_ELEMS = 6144
LAM_INIT1 = 0.8 - 0.6 * math.exp(-0.3 * 1)

EVEN_OFF = dict(cq=0, ckv=256, krope=384, rq=416, rk=672, rv=928, rg=1440)


def fm(v):
    v = np.asarray(v, np.float32).reshape(-1, 128)
    return np.ascontiguousarray(v.T)


class Pack:
    def __init__(self):
        self.cols = {}
        self.parts = []
        self.n = 0

    def add(self, name, arr):
        arr = np.asarray(arr, np.float32)
        if arr.ndim == 1:
            arr = arr[:, None]
        a = np.zeros((128, arr.shape[1]), np.float32)
        a[:arr.shape[0]] = arr
        self.cols[name] = (self.n, arr.shape[1])
        self.parts.append(a)
        self.n += arr.shape[1]

    def array(self):
        return np.ascontiguousarray(np.concatenate(self.parts, axis=1))


def rope_tables(rot_dim):
    n_freq = rot_dim // 4
    t = np.arange(TS)
    row, col = t // GRID_W, t % GRID_W
    inv = (10000.0 ** (-np.arange(n_freq, dtype=np.float32) / n_freq)).astype(np.float32)
    ang = np.concatenate([row.astype(np.float32)[:, None] * inv, col.astype(np.float32)[:, None] * inv], -1)
    cos, sin = np.cos(ang).astype(np.float32), np.sin(ang).astype(np.float32)
    C = np.repeat(cos.T, 2, axis=0)
    S = np.repeat(sin.T, 2, axis=0)
    S[0::2] *= -1.0
    return C.astype(np.float32), S.astype(np.float32)


def pair_swap(n, lo=0):
    m = np.zeros((n, n), np.float32)
    for i in range(lo, n, 2):
        m[i, i + 1] = 1.0
        m[i + 1, i] = 1.0
    return m


def make_consts():
    c = {}
    c["ident"] = np.eye(128, dtype=np.float32)
    c["ones"] = np.ones((128, 128), np.float32)
    bd = np.zeros((128, 128), np.float32)
    bd[:64, :64] = 1.0
    bd[64:, 64:] = 1.0
    c["bd64"] = bd
    C32, S32 = rope_tables(32)
    C96 = np.ones((96, TS), np.float32)
    S96 = np.zeros((96, TS), np.float32)
    C96[64:] = C32
    S96[64:] = S32
    c["C96"], c["S96"] = C96, S96
    c["C32"], c["S32"] = C32, S32
    c["perm96"] = pair_swap(96, 64)
    c["perm32"] = pair_swap(32)
    C64, S64 = rope_tables(64)
    c["C128"] = np.concatenate([C64, C64], 0)
    c["S128"] = np.concatenate([S64, S64], 0)
    c["perm128"] = pair_swap(128)
    Dm = (np.arange(1920)[None, :] - 896 - np.arange(128)[:, None]).astype(np.float32)
    c["Dpos"] = np.maximum(Dm, 0.0)
    c["Dneg"] = np.maximum(-Dm, 0.0)
    c["Ddiag"] = (Dm == 0).astype(np.float32)
    tt = np.arange(TS, dtype=np.float32)
    c["tpl1"] = np.broadcast_to(tt + 1.0, (128, TS)).copy()
    c["tNm"] = np.broadcast_to(TS - tt, (128, TS)).copy()
    j = (np.arange(2)[None, :] * 128 + np.arange(128)[:, None]).astype(np.float32)
    c["pj"] = np.stack([255.0 - j, j], axis=1).astype(np.float32)
    s_ = np.arange(64)[:, None]
    t_ = np.arange(64)[None, :]
    up_s = (s_ < t_).astype(np.float32)
    up_i = (s_ <= t_).astype(np.float32)
    lo_s = (s_ > t_).astype(np.float32)
    lo_i = (s_ >= t_).astype(np.float32)
    c["m_ab"] = np.stack([np.concatenate([-up_s, up_s], 0), np.concatenate([-lo_s, lo_s], 0)], 0)
    c["m_cd"] = np.stack([np.concatenate([up_i, up_i], 0), np.concatenate([lo_i, lo_i], 0)], 0)
    c["m_na"] = np.stack([-(lo_s), -(up_s)], 0)
    c["eye64"] = np.eye(64, dtype=np.float32)
    return c


def tbs_of(c0, c1):
    return [i for i, (a, b) in enumerate(TBLK) if a < c1 and c0 < b]


def XK(buf, kc, c0, c1):
    return [(buf, (kc, tb)) for tb in tbs_of(c0, c1)]


class Builder:
    def __init__(self, pack_cols, npk, dbg=(), upto="all"):
        self.nc = bass.Bass("TRN2", target_bir_lowering=False)
        self.p = Prog(self.nc)
        self.pc = pack_cols
        self.npk = npk
        self.dbg = set(dbg)
        self.upto = upto
        self.out_dmas = []
        self.ring_i = 0
        self.ps_i = 0
        self.outs = {}

    def din(self, name, shape):
        return self.nc.dram_tensor(name, list(shape), F32, kind="ExternalInput").ap()

    def dout(self, name, shape):
        ap = self.nc.dram_tensor(name, list(shape), F32, kind="ExternalOutput").ap()
        self.outs[name] = list(shape)
        return ap

    def store(self, dst, src, reads, sem):
        d = self.p.dma("sp", dst, src, reads=reads, sem=sem)
        self.out_dmas.append(d)
        return d

    def debug_dump(self, name, buf, ap, shape):
        if name not in self.dbg:
            return
        dst = self.dout("dbg_" + name, shape)
        self.store(dst, ap, [buf], "dbg_" + name)

    def P(self, name):
        o, n = self.pc[name]
        return self.prm[:, o:o + n]

    def wload(self, src, shape):
        slot = self.ring[self.ring_i % len(self.ring)]
        self.ring_i += 1
        n = int(np.prod(shape[1:]))
        assert n <= RING_ELEMS, (shape, n)
        view = slot.t[0:shape[0], 0:n]
        if len(shape) == 3:
            view = view.rearrange("p (a b) -> p a b", b=shape[2])
        self.p.dma("pool", view, src, writes=[slot])
        return slot, view

    def psn(self):
        b = self.ps[self.ps_i % getattr(self, 'ps_mod', 6)]
        self.ps_i += 1
        return b

    def setup(self):
        p = self.p
        self.x_in = self.din("xT", [D, T])
        self.prm_in = self.din("prm", [128, self.npk])
        self.cb_in = self.din("cbf", [128, 5, 128])
        self.prm = p.sbuf([128, self.npk], F32, "prm")
        p.dma("sp", self.prm[:], self.prm_in[:, :], writes=[self.prm])
        self.cbf = p.sbuf([128, 5, 128], BF16, "cbf")
        p.dma("pool", self.cbf[:], self.cb_in[:, :, :], writes=[self.cbf])
        self.ident = self.cbf[:, 0, :]
        self.ones = self.cbf[:, 1, :]
        self.bd64 = self.cbf[:, 2, :]
        self.perm96 = self.cbf[:, 3, :]
        self.perm128 = self.cbf[:, 4, :]
        self.eps = p.sbuf([128, 1], F32, "eps")
        p.memset("dve", self.eps[:], EPS, writes=[self.eps])
        self.xT = p.sbuf([128, KC, T], F32, "xT")
        for kc in range(KC):
            p.dma("sp", self.xT[:, kc, :], self.x_in[kc * 128:(kc + 1) * 128, :], writes=XK(self.xT, kc, 0, T))
        self.hT = p.sbuf([128, KC, T], BF16, "hT")
        self.ring = [p.sbuf([128, RING_ELEMS], BF16, f"ring{i}") for i in range(3)]
        cm = self.nc.psum_tensor("psall", [128, 8 * 512], F32)
        self.psall = cm.__enter__()
        p.ctx.append(cm)
        self.ps = [Buf(self.psall[:, i * 512:(i + 1) * 512], f"psb{i}") for i in range(8)]
        self.ffn_up_in = self.din("ffn_up", [2, D, 2 * DFF])
        self.ffn_down_in = self.din("ffn_down", [2, DFF, D])
        self.mod = [p.sbuf([128, 48, 2], F32, f"mod{l}") for l in range(2)]
        self.gsc = [p.sbuf([128, 2, KC, 2], F32, f"gsc{l}") for l in range(2)]
        self.scond = p.sbuf([128, KC, 2], BF16, "scond")
        self.cond_in = self.din("condT", [128, KC, 2])
        condf = p.sbuf([128, KC, 2], F32, "condf")
        p.dma("sp", condf[:], self.cond_in[:, :, :], writes=[condf])
        p.act(self.scond[:], condf[:], AF.Silu, reads=[condf], writes=[self.scond])
        self.ada_in = self.din("ada_w", [2, D, 6 * D])
        self.a_w_in = self.din("a_w_in", [D, 1952])
        self.w_out_in = self.din("w_out", [2, D, D])
        self.b_w_in = self.din("b_w_in", [D, 3456])

    def mod_piece(self, l, piece, mps):
        p = self.p
        mview = mps.t[:, 0:12].rearrange("p (j c) -> p j c", c=2)
        src = self.ada_in[l].rearrange("(kc p) n -> p kc n", p=128)
        slot, wv = self.wload(src[:, :, piece * 768:(piece + 1) * 768], [128, KC, 768])
        for nch in range(6):
            for kc in range(KC):
                p.mm(mview[:, nch, :], wv[:, kc, nch * 128:(nch + 1) * 128], self.scond[:, kc, :],
                     start=(kc == 0), stop=(kc == KC - 1), reads=[slot, self.scond], writes=[mps])
        ab = self.P(f"ab{l}")[:, piece * 6:(piece + 1) * 6]
        mod = self.mod[l]
        p.tt("dve", mod[:, piece * 6:(piece + 1) * 6, :], mview, ab.unsqueeze(2).to_broadcast([128, 6, 2]), ALU.add,
             reads=[mps, self.prm], writes=[(mod, piece)])

    def mod_end(self, l):
        p = self.p
        mod = self.mod[l]
        gsc = self.gsc[l]
        for w, (scj, gname) in enumerate(((1, f"gm{l}"), (4, f"gf{l}"))):
            g = self.P(gname)
            p.stt(gsc[:, w, :, :], mod[:, scj * 8:(scj + 1) * 8, :], 1.0, g.unsqueeze(2).to_broadcast([128, KC, 2]),
                  ALU.add, ALU.mult, reads=[mod, self.prm], writes=[(gsc, w)])

    def modulation(self, l):
        for piece in range(8):
            self.mod_piece(l, piece, self.psn())
        self.mod_end(l)

    def norm_mod(self, l, w):
        p = self.p
        gsc = self.gsc[l]
        mod = self.mod[l]
        shj = 0 if w == 0 else 3
        with p.scope("norm_mod:598"):
            sqb = [p.sbuf([128, 512], BF16, f"sq{i}") for i in range(3)]
            tmpb = [p.sbuf([128, 1024], F32, f"nt{i}") for i in range(2)]
            rt = p.sbuf([128, 512], F32, "rt")
            self.rstd = p.sbuf([128, T], F32, "rstd")
            i = 0
            for tb, (c0, c1) in enumerate(TBLK):
                ps = self.psn()
                for kc in range(KC):
                    sq = sqb[i % 3]
                    i += 1
                    p.act(sq[:], self.xT[:, kc, c0:c1], AF.Square, reads=[(self.xT, (kc, tb))], writes=[sq])
                    p.mm(ps[:], self.ones, sq[:], start=(kc == 0), stop=(kc == KC - 1), reads=[sq, self.cbf],
                         writes=[ps])
                p.act(rt[:], ps[:], AF.Ln, bias=self.eps[:], scale=1.0 / D, reads=[ps, self.eps], writes=[rt])
                p.act(self.rstd[:, c0:c1], rt[:], AF.Exp, scale=-0.5, reads=[rt], writes=[(self.rstd, tb)])
            i = 0
            for kc in range(KC):
                for (c0, c1, ci) in GROUPS:
                    tmp = tmpb[i % 2]
                    i += 1
                    n = c1 - c0
                    p.stt(tmp[:, 0:n], self.xT[:, kc, c0:c1], gsc[:, w, kc, ci:ci + 1], self.rstd[:, c0:c1],
                          ALU.mult, ALU.mult, reads=XK(self.xT, kc, c0, c1) + [(gsc, w)] + [(self.rstd, t) for t in tbs_of(c0, c1)],
                          writes=[tmp])
                    p.act(self.hT[:, kc, c0:c1], tmp[:, 0:n], AF.Identity, bias=mod[:, shj * 8 + kc, ci:ci + 1], scale=1.0,
                          reads=[tmp, mod], writes=XK(self.hT, kc, c0, c1))


def build_pack(I):
    pk = Pack()
    for l in range(2):
        pk.add(f"ab{l}", fm(I["ada_b"][l]))
        pk.add(f"gm{l}", fm(I["norm_mix_g"][l]))
        pk.add(f"gf{l}", fm(I["norm_ffn_g"][l]))
        cw = np.stack([fm(I["ffn_conv_w"][l][k]) for k in range(3)], axis=2)
        pk.add(f"cw{l}", cw.reshape(128, 44 * 3))
        pk.add(f"cb{l}", fm(I["ffn_conv_b"][l]))
    pk.add("qnorm", fm(I["mla_q_norm"][0]))
    pk.add("kvnorm", fm(I["mla_kv_norm"][0]))
    pk.add("qn", I["mla_qn"][0])
    pk.add("kn", I["mla_kn"][0])
    pk.add("knr", I["mla_kn"][0][64:96])
    pk.add("retdec", np.broadcast_to(I["ret_decay"][0].reshape(1, 8), (128, 8)))
    pk.add("retgn", fm(I["ret_gn"][0]))
    pk.add("dqn", np.tile(I["diff_qn"][0], 2))
    pk.add("dkn", np.tile(I["diff_kn"][0], 2))
    pk.add("lam", np.broadcast_to(I["diff_lam"][0].reshape(1, 256), (128, 256)))
    pk.add("dgn", fm(I["diff_gn"][0]))
    pk.add("mu", fm(I["rwkv_mu"][0]))
    pk.add("w0", np.concatenate([fm(I["rwkv_w0"][0][d]) for d in range(2)], 1))
    pk.add("a0", np.concatenate([fm(I["rwkv_a0"][0][d]) for d in range(2)], 1))
    pk.add("kk", fm(I["rwkv_k_k"][0]))
    pk.add("ka", fm(I["rwkv_k_a"][0]))
    pk.add("rrk", fm(I["rwkv_r_k"][0].reshape(-1)))
    pk.add("rgn", fm(I["rwkv_gn"][0]))
    return pk


def build_in_maps(I):
    I = {k: np.asarray(v) for k, v in I.items()}
    cst = make_consts()
    pk = build_pack(I)
    prm = pk.array()
    cbf = np.zeros((128, 5, 128), np.float32)
    cbf[:, 0] = cst["ident"]
    cbf[:, 1] = cst["ones"]
    cbf[:, 2] = cst["bd64"]
    cbf[:96, 3, :96] = cst["perm96"]
    cbf[:, 4] = cst["perm128"]
    shared = {
        "prm": prm, "cbf": cbf,
        "ada_w": I["ada_w"], "w_out": I["w_out"], "ffn_up": I["ffn_up"], "ffn_down": I["ffn_down"],
        "a_w_in": I["a_w_in"][0], "w_uq": I["mla_w_uq"][0], "w_ukv": I["mla_w_ukv"][0],
        "b_w_in": I["b_w_in"][0], "rw_w_up": I["rwkv_w_up"][0], "rw_a_up": I["rwkv_a_up"][0],
        "rw_g_up": I["rwkv_g_up"][0],
    }
    for k in ("C96", "S96", "C32", "S32", "C128", "S128", "Dpos", "Dneg", "Ddiag", "tpl1", "tNm", "pj",
              "m_ab", "m_cd", "m_na", "eye64", "perm32"):
        shared["c_" + k] = cst[k]
    maps = []
    for c in range(8):
        s = c // 4
        xp = I["x_prompt"][2 * c:2 * c + 2].reshape(TP, D)
        xs = I["x_sample"][s]
        xT = np.ascontiguousarray(np.concatenate([xp, xs], 0).T)
        cond = np.stack([I["c_ctx"], I["c"][s]], 0)
        condT = np.ascontiguousarray(cond.T.reshape(KC, 128, 2).transpose(1, 0, 2))
        m = dict(shared)
        m["xT"] = xT
        m["condT"] = condT
        m["ckv_cT"] = np.ascontiguousarray(I["cache_mla_ckv"][s, 0].T)
        m["krope_cT"] = np.ascontiguousarray(I["cache_mla_krope"][s, 0].T)
        m["ret_s0"] = np.ascontiguousarray(I["state_ret"][s, 0])
        m["dk_cT"] = np.ascontiguousarray(I["cache_diff_k"][s, 0].reshape(PAST, 4, 128).transpose(1, 2, 0))
        m["dv_c"] = np.ascontiguousarray(I["cache_diff_v"][s, 0].reshape(PAST, 512))
        m["rw_s0"] = np.ascontiguousarray(I["state_rwkv"][s, 0].transpose(0, 1, 3, 2))
        maps.append(m)
    return maps, pk.cols, pk.n


def layer1(b, own_mod=False):
    p = b.p
    if own_mod:
        b.modulation(1)
    else:
        b.mod_end(1)
    b.norm_mod(1, 0)
    with p.scope("layer1:706"):
        oT = p.sbuf([128, 4, T], BF16, "oT1")
        if b.upto != "rwkv_only":
            diff_attn(b, oT)
            if "odiff" in b.dbg:
                dst = b.dout("dbg_odiff", [128, 4, T])
                b.out_dmas.append(p.dma("pool", dst[:, :, :], oT[:], reads=[oT], sem="dbg_odiff"))
            wout_half(b, 1, 0, oT)
        if b.upto == "diff":
            return
        rwkv(b, oT)
        if "orw" in b.dbg:
            dst = b.dout("dbg_orw", [128, 4, T])
            b.out_dmas.append(p.dma("pool", dst[:, :, :], oT[:], reads=[oT], sem="dbg_orw"))
        wout_half(b, 1, 1, oT)
    if b.upto in ("rwkv", "rwkv_only"):
        return
    b.norm_mod(1, 1)
    conv_ffn(b, 1)
    yo = b.dout("o_yT", [D, T])
    for kc in range(KC):
        b.store(yo[kc * 128:(kc + 1) * 128, :], b.xT[:, kc, :], XK(b.xT, kc, 0, T), ("o_y", kc))


def build_program(pack_cols, npk, dbg=(), upto="all"):
    b = Builder(pack_cols, npk, dbg=dbg, upto=upto)
    b.setup()
    b.modulation(0)
    b.norm_mod(0, 0)
    if "h0" in b.dbg:
        dst = b.dout("dbg_h0", [128, KC, T])
        b.out_dmas.append(b.p.dma("pool", dst[:, :, :], b.hT[:], reads=[b.hT], sem="dbg_h0"))
    if "mod0" in b.dbg:
        dst = b.dout("dbg_mod0", [128, 48, 2])
        b.store(dst[:, :, :], b.mod[0][:], [b.mod[0]], "dbg_mod0")
    if upto != "h0":
        with b.p.scope("build_program:743"):
            oT = b.p.sbuf([128, 4, T], BF16, "oT")
            mla(b, oT)
            if "omla" in b.dbg:
                dst = b.dout("dbg_omla", [128, 4, T])
                b.out_dmas.append(b.p.dma("pool", dst[:, :, :], oT[:], reads=[oT], sem="dbg_omla"))
            if upto != "mla":
                wout_half(b, 0, 0, oT)
                retention(b, oT)
                if "oret" in b.dbg:
                    dst = b.dout("dbg_oret", [128, 4, T])
                    b.out_dmas.append(b.p.dma("pool", dst[:, :, :], oT[:], reads=[oT], sem="dbg_oret"))
                wout_half(b, 0, 1, oT)
        if "xm0" in b.dbg:
            dst = b.dout("dbg_xm0", [128, KC, T])
            b.store(dst[:, :, :], b.xT[:], [b.xT], "dbg_xm0")
        if upto not in ("mla", "mix0"):
            b.norm_mod(0, 1)
            if upto == "l0":
                conv_ffn(b, 0)
            else:
                conv_ffn(b, 0, hook=lambda g: [b.mod_piece(1, 2 * g + i, b.ps[6 + i]) for i in range(2)])
            if "x0" in b.dbg:
                dst = b.dout("dbg_x0", [128, KC, T])
                b.store(dst[:, :, :], b.xT[:], [b.xT], "dbg_x0")
            if upto != "l0":
                layer1(b)
    b.p.emit(final_dma_ops=b.out_dmas)
    return b


def kernel(**inputs):
    I = {k: np.asarray(v) for k, v in inputs.items()}
    maps, cols, npk = build_in_maps(I)
    b = build_program(cols, npk)
    res = run_bass_kernel_spmd(b.nc, maps, core_ids=list(range(8)))
    R = res.results
    B = I["x_prompt"].shape[0]
    y_p = np.zeros((B, SEQ, D), np.float32)
    y_s = np.zeros((2, TS, D), np.float32)
    ckv = np.zeros((B, 1, SEQ, 128), np.float32)
    kro = np.zeros((B, 1, SEQ, 32), np.float32)
    rst = np.zeros((B, 1, 2, 4, 64, 128), np.float32)
    dk = np.zeros((B, 1, SEQ, 4, 2, 64), np.float32)
    dv = np.zeros((B, 1, SEQ, 4, 128), np.float32)
    rws = np.zeros((B, 1, 2, 8, 64, 64), np.float32)
    for c in range(8):
        r = R[c]
        sl = slice(2 * c, 2 * c + 2)
        yT = np.asarray(r["o_yT"])
        y_p[sl] = yT[:, 0:TP].T.reshape(NPS, SEQ, D)
        s_, q = c // 4, c % 4
        y_s[s_, q * 256:(q + 1) * 256] = yT[:, TP + q * 256:TP + (q + 1) * 256].T
        ckv[sl, 0] = np.asarray(r["o_ckvT"]).T.reshape(NPS, SEQ, 128)
        kro[sl, 0] = np.asarray(r["o_kropeT"]).T.reshape(NPS, SEQ, 32)
        rst[sl, 0] = np.asarray(r["o_retst"]).transpose(0, 1, 3, 2, 4)
        dk[sl, 0] = np.asarray(r["o_dkT"]).transpose(2, 0, 1).reshape(NPS, SEQ, 4, 2, 64)
        dv[sl, 0] = np.asarray(r["o_dv"]).reshape(NPS, SEQ, 4, 128)
        rws[sl, 0] = np.asarray(r["o_rwst"]).transpose(0, 1, 2, 4, 3)
    return (y_p, y_s, ckv, kro, rst, dk, dv, rws)


def _rms_rstd(b, ps_s, M, n, nfeat, rt, rstd, legacy=False):
    p = b.p
    if legacy:
        p.act(rt[0:M, 0:n], ps_s[0:M, 0:n], AF.Sqrt, bias=b.eps[0:M, :], scale=1.0 / nfeat, reads=[ps_s, b.eps], writes=[rt])
        p.recip(rstd[0:M, 0:n], rt[0:M, 0:n], reads=[rt], writes=[rstd])
        return
    p.act(rt[0:M, 0:n], ps_s[0:M, 0:n], AF.Ln, bias=b.eps[0:M, :], scale=1.0 / nfeat, reads=[ps_s, b.eps], writes=[rt])
    p.act(rstd[0:M, 0:n], rt[0:M, 0:n], AF.Exp, scale=-0.5, reads=[rt], writes=[rstd])


def mla(b, oT):
    p = b.p
    nc = b.nc
    a_w_in = b.a_w_in
    with p.scope("mla:816"):
        ckvn = p.sbuf([128, 2048], BF16, "ckvn")
        krg = p.sbuf([96, 2048], BF16, "krg")
        sqk = [p.sbuf([96, 512], BF16, f"sqk{i}") for i in range(4)]
        cqn = p.sbuf([128, 2, T], BF16, "cqn")
        vaug = p.sbuf([128, 16, 512], BF16, "vtok")
        C96 = p.sbuf([96, TS], F32, "C96")
        S96 = p.sbuf([96, TS], F32, "S96")
        p.dma("sp", C96[:], b.din("c_C96", [96, TS])[:, :], writes=[C96])
        p.dma("sp", S96[:], b.din("c_S96", [96, TS])[:, :], writes=[S96])
        p.dma("pool", ckvn[:, T:T + PAST], b.din("ckv_cT", [128, PAST])[:, :], writes=[(ckvn, 3)])
        krc = p.sbuf([96, PAST], F32, "krc")
        p.dma("sp", krc[64:96, :], b.din("krope_cT", [32, PAST])[:, :], writes=[krc])
        kn = b.P("kn")
        qn = b.P("qn")
        out_ckv = b.dout("o_ckvT", [128, TP])
        out_kr = b.dout("o_kropeT", [32, TP])

        slotA, wA = b.wload(a_w_in.rearrange("(kc p) n -> p kc n", p=128)[:, :, 0:416], [128, KC, 416])
        with p.scope("mla:837"):
            sqt = [p.sbuf([128, 512], BF16, f"sqt{i}") for i in range(2)]
            rt = p.sbuf([128, 512], F32, "rt")
            rs = p.sbuf([128, 512], F32, "rs")
            ckf = p.sbuf([128, 512], F32, "ckf")
            krf = p.sbuf([96, 512], F32, "krf")
            krf2 = p.sbuf([96, 512], F32, "krf2")
            krb = p.sbuf([96, 512], BF16, "krb")
            t1 = p.sbuf([96, 512], F32, "t1")
            t2 = p.sbuf([96, 512], F32, "t2")
            p.memset("dve", krb[:], 0.0, writes=[krb])

            def krope_block(src_ap, src_reads, c0, n, rope_t0):
                kc_ = c0 // 512
                p.act(sqk[kc_][64:96, 0:n], src_ap, AF.Square, reads=src_reads, writes=[(sqk[kc_], "r")])
                if rope_t0 is None:
                    p.ts("dve", krg[64:96, c0:c0 + n], src_ap, kn[64:96, :], None, ALU.mult,
                         reads=src_reads + [b.prm], writes=[(krg, kc_)])
                else:
                    p.ts("dve", krf2[64:96, 0:n], src_ap, kn[64:96, :], None, ALU.mult,
                         reads=src_reads + [b.prm], writes=[krf2])
                    p.copy("dve", krb[64:96, 0:n], krf2[64:96, 0:n], reads=[krf2], writes=[krb])
                    pp = b.psn()
                    p.mm(pp[0:96, 0:n], b.perm96[0:96, 0:96], krb[:, 0:n], reads=[krb, b.cbf], writes=[pp])
                    p.tt("dve", t1[64:96, 0:n], krf2[64:96, 0:n], C96[64:96, rope_t0:rope_t0 + n], ALU.mult,
                         reads=[krf2, C96], writes=[t1])
                    p.tt("dve", t2[64:96, 0:n], pp[64:96, 0:n], S96[64:96, rope_t0:rope_t0 + n], ALU.mult,
                         reads=[pp, S96], writes=[t2])
                    p.tt("dve", krg[64:96, c0:c0 + n], t1[64:96, 0:n], t2[64:96, 0:n], ALU.add,
                         reads=[t1, t2], writes=[(krg, kc_)])

            for tb, (c0, c1) in enumerate(TBLK):
                n = c1 - c0
                pcq = [b.psn(), b.psn()]
                pss = b.psn()
                for j in range(2):
                    for kc in range(KC):
                        p.mm(pcq[j][:, 0:n], wA[:, kc, j * 128:(j + 1) * 128], b.hT[:, kc, c0:c1],
                             start=(kc == 0), stop=(kc == KC - 1), reads=[slotA, (b.hT, (kc, tb))], writes=[pcq[j]])
                    sq = sqt[j]
                    p.act(sq[:, 0:n], pcq[j][:, 0:n], AF.Square, reads=[pcq[j]], writes=[sq])
                    p.mm(pss[:, 0:n], b.ones, sq[:, 0:n], start=(j == 0), stop=(j == 1), reads=[sq, b.cbf], writes=[pss])
                _rms_rstd(b, pss, 128, n, 256, rt, rs)
                for j in range(2):
                    p.stt(cqn[:, j, c0:c1], pcq[j][:, 0:n], b.P("qnorm")[:, j:j + 1], rs[:, 0:n], ALU.mult, ALU.mult,
                          reads=[pcq[j], rs, b.prm], writes=[(cqn, (j, tb))])
                pck = b.psn()
                pss = b.psn()
                for kc in range(KC):
                    p.mm(pck[:, 0:n], wA[:, kc, 256:384], b.hT[:, kc, c0:c1], start=(kc == 0), stop=(kc == KC - 1),
                         reads=[slotA, (b.hT, (kc, tb))], writes=[pck])
                sq = sqt[0]
                p.act(sq[:, 0:n], pck[:, 0:n], AF.Square, reads=[pck], writes=[sq])
                p.mm(pss[:, 0:n], b.ones, sq[:, 0:n], reads=[sq, b.cbf], writes=[pss])
                _rms_rstd(b, pss, 128, n, 128, rt, rs)
                if tb == 0:
                    p.stt(ckf[:, 0:n], pck[:, 0:n], b.P("kvnorm")[:, 0:1], rs[:, 0:n], ALU.mult, ALU.mult,
                          reads=[pck, rs, b.prm], writes=[ckf])
                    b.store(out_ckv[:, :], ckf[:, 0:n], [ckf], "o_ckv")
                    p.copy("act", ckvn[:, c0:c1], ckf[:, 0:n], reads=[ckf], writes=[(ckvn, tb)])
                else:
                    p.stt(ckvn[:, c0:c1], pck[:, 0:n], b.P("kvnorm")[:, 0:1], rs[:, 0:n], ALU.mult, ALU.mult,
                          reads=[pck, rs, b.prm], writes=[(ckvn, tb)])
                pkr = b.psn()
                for kc in range(KC):
                    p.mm(pkr[0:32, 0:n], wA[:, kc, 384:416], b.hT[:, kc, c0:c1], start=(kc == 0), stop=(kc == KC - 1),
                         reads=[slotA, (b.hT, (kc, tb))], writes=[pkr])
                p.copy("act", krf[64:96, 0:n], pkr[0:32, 0:n], reads=[pkr], writes=[krf])
                if tb == 0:
                    b.store(out_kr[:, :], krf[64:96, 0:n], [krf], "o_kr")
                krope_block(krf[64:96, 0:n], [krf], c0, n, None if tb == 0 else c0 - TP)
            krope_block(krc[64:96, :], [krc], T, PAST, None)

        slotU = p.sbuf([128, 2 * 768 + 1024], BF16, "wU")
        wuq = slotU.t[:, 0:1536].rearrange("p (a b) -> p a b", b=768)
        wukv = slotU.t[:, 1536:2560]
        p.dma("pool", wuq, b.din("w_uq", [256, 768]).rearrange("(kc p) n -> p kc n", p=128), writes=[(slotU, 0)])
        p.dma("pool", wukv, b.din("w_ukv", [128, 1024])[:, :], writes=[(slotU, 1)])
        wukv_h = wukv.rearrange("p (h two e) -> p h two e", two=2, e=64)

        for kt in range(16):
            pv = b.psn()
            pvv = pv.t[:, :].rearrange("p (h e) -> p h e", e=64)
            p.mm(pvv, ckvn[:, kt * 128:(kt + 1) * 128], wukv_h[:, :, 1, :], reads=[(ckvn, kt // 4), (slotU, 1)], writes=[pv])
            p.copy("act" if kt % 2 == 0 else "dve", vaug[:, kt, :], pv[:, :], reads=[pv], writes=[(vaug, kt)])

        with p.scope("mla:930"):
            Qh = [p.sbuf([96, T], BF16, f"Qh{i}") for i in range(2)]
            Kh = [p.sbuf([96, 2048], BF16, f"Kh{i}") for i in range(2)]
            sqt = [p.sbuf([96, 512], BF16, f"sqh{i}") for i in range(2)]
            rt = p.sbuf([96, 512], F32, "rt")
            rs = p.sbuf([96, 512], F32, "rs")
            t1 = p.sbuf([96, 512], F32, "t1")
            t2 = p.sbuf([96, 512], F32, "t2")
            qgb = p.sbuf([96, 512], BF16, "qgb")
            ex = [p.sbuf([128, 512], BF16, f"ex{i}") for i in range(4)]
            rden = p.sbuf([128, 512], F32, "rden")
            exi = 0
            acci = 0
            sc = 96.0 ** -0.5
            cnt = {"ex": 0, "acc": 0}

            def k_block(h, kb):
                Kt = Kh[h % 2]
                c0 = kb * 512
                pk = b.psn()
                p.mm(pk[0:64, :], wukv[:, h * 128:h * 128 + 64], ckvn[:, c0:c0 + 512], reads=[(slotU, 1), (ckvn, kb)], writes=[pk])
                sq = sqk[kb]
                p.act(sq[0:64, :], pk[0:64, :], AF.Square, reads=[pk], writes=[(sq, "n")])
                pss = b.psn()
                p.mm(pss[0:96, :], b.ones[0:96, 0:96], sq[0:96, :], reads=[(sq, "n"), (sq, "r"), b.cbf], writes=[pss])
                _rms_rstd(b, pss, 96, 512, 96, rt, rs)
                p.stt(Kt[0:64, c0:c0 + 512], pk[0:64, :], kn[0:64, :], rs[0:64, :], ALU.mult, ALU.mult,
                      reads=[pk, rs, b.prm], writes=[(Kt, kb)])
                p.tt("pool", Kt[64:96, c0:c0 + 512], krg[64:96, c0:c0 + 512], rs[64:96, :], ALU.mult,
                     reads=[(krg, kb), rs], writes=[(Kt, kb)])

            def q_block(h, tb):
                Q = Qh[h % 2]
                c0, c1 = TBLK[tb]
                pq = b.psn()
                for kc in range(2):
                    p.mm(pq[0:96, :], wuq[:, kc, h * 96:(h + 1) * 96], cqn[:, kc, c0:c1], start=(kc == 0), stop=(kc == 1),
                         reads=[(slotU, 0), (cqn, (kc, tb))], writes=[pq])
                sq = sqt[tb % 2]
                p.act(sq[0:96, :], pq[0:96, :], AF.Square, reads=[pq], writes=[sq])
                pss = b.psn()
                p.mm(pss[0:96, :], b.ones[0:96, 0:96], sq[0:96, :], reads=[sq, b.cbf], writes=[pss])
                _rms_rstd(b, pss, 96, 512, 96, rt, rs)
                if tb == 0:
                    p.stt(Q[:, c0:c1], pq[0:96, :], qn[0:96, :], rs[0:96, :], ALU.mult, ALU.mult,
                          reads=[pq, rs, b.prm], writes=[(Q, tb)])
                else:
                    r0 = c0 - TP
                    p.stt(t1[:, :], pq[0:96, :], qn[0:96, :], C96[:, r0:r0 + 512], ALU.mult, ALU.mult,
                          reads=[pq, C96, b.prm], writes=[t1])
                    p.act(qgb[:, :], pq[0:96, :], AF.Identity, scale=qn[0:96, :], reads=[pq, b.prm], writes=[qgb])
                    pp = b.psn()
                    p.mm(pp[0:96, :], b.perm96[0:96, 0:96], qgb[:, :], reads=[qgb, b.cbf], writes=[pp])
                    p.tt("dve", t2[:, :], pp[0:96, :], S96[:, r0:r0 + 512], ALU.mult, reads=[pp, S96], writes=[t2])
                    p.tt("pool", t1[:, :], t1[:, :], t2[:, :], ALU.add, reads=[t1, t2], writes=[t1])
                    p.tt("dve", Q[:, c0:c1], t1[:, :], rs[0:96, :], ALU.mult, reads=[t1, rs], writes=[(Q, tb)])

            def build_steps(h):
                return [lambda kb=kb: k_block(h, kb) for kb in range(4)] + [lambda tb=tb: q_block(h, tb) for tb in range(3)]

            def attn_steps(h):
                Q = Qh[h % 2]
                Kt = Kh[h % 2]
                jobs = [(a * SEQ, SEQ, [2 * a, 2 * a + 1]) for a in range(NPS)]
                jobs += [(TP + qb * 512, 512, list(range(4, 16))) for qb in range(2)]
                hc, hu = h // 2, h % 2
                lo, hi = (0, 64) if hu == 0 else (64, 128)
                dlo, dhi = (64, 128) if hu == 0 else (0, 64)
                steps = []
                for (q0, nq, kts) in jobs:
                    st = {}

                    def first(st=st):
                        st["acc"] = b.ps[4 + 2 * (cnt["acc"] % 2)]
                        st["accd"] = b.ps[5 + 2 * (cnt["acc"] % 2)]
                        cnt["acc"] += 1
                    for i, kt in enumerate(kts):
                        def tile(i=i, kt=kt, q0=q0, nq=nq, kts=kts, st=st, first=first):
                            def score(k):
                                ktk = kts[k]
                                ps_ = b.psn()
                                p.mm(ps_[:, 0:nq], Kt[:, ktk * 128:(ktk + 1) * 128], Q[:, q0:q0 + nq],
                                     reads=[(Kt, ktk // 4)] + [(Q, t) for t in tbs_of(q0, q0 + nq)], writes=[ps_])
                                st[("ps", k)] = ps_
                            LA = not (len(DBG_KNOB) > 2 and DBG_KNOB[2] == 2)
                            if i == 0:
                                first()
                                score(0)
                                if len(kts) > 1:
                                    score(1)
                                if len(kts) > 2:
                                    score(2)
                            if i + 3 < len(kts):
                                score(i + 3)
                            acc = st["acc"]
                            accd = st["accd"]
                            pscore = st.pop(("ps", i))
                            e = ex[cnt["ex"] % 4]
                            cnt["ex"] += 1
                            p.act(e[:, 0:nq], pscore[:, 0:nq], AF.Exp, scale=sc, reads=[pscore], writes=[e])
                            p.mm(acc[:, 0:nq], vaug[:, kt, hc * 128:(hc + 1) * 128], e[:, 0:nq], start=(i == 0),
                                 stop=(i == len(kts) - 1), reads=[(vaug, kt), e], writes=[acc])
                            p.mm(accd[:, 0:nq], b.ones, e[:, 0:nq], start=(i == 0),
                                 stop=(i == len(kts) - 1), reads=[b.cbf, e], writes=[accd])
                            if i == len(kts) - 1:
                                p.act(rden[lo:hi, 0:nq], accd[lo:hi, 0:nq], AF.Ln, reads=[accd], writes=[rden])
                                p.act(rden[lo:hi, 0:nq], rden[lo:hi, 0:nq], AF.Exp, scale=-1.0, reads=[rden], writes=[rden])
                                p.tt("dve", oT[lo:hi, hc, q0:q0 + nq], acc[lo:hi, 0:nq], rden[lo:hi, 0:nq], ALU.mult,
                                     reads=[acc, rden], writes=[(oT, (hc, hu, q0))])
                        steps.append(tile)
                return steps

            b.ps_mod = 4
            for f in build_steps(0):
                f()
            for h in range(8):
                A = attn_steps(h)
                B = build_steps(h + 1) if h < 7 else []
                bi = 0
                for ai, f in enumerate(A):
                    f()
                    if ai == 3:
                        while bi < len(B):
                            B[bi]()
                            bi += 1
                while bi < len(B):
                    B[bi]()
                    bi += 1


def wout_half(b, l, half, oT):
    p = b.p
    b.ps_mod = 6
    src = b.w_out_in[l][half * 512:(half + 1) * 512].rearrange("(c p) n -> p c n", p=128)
    slot, wv = b.wload(src, [128, 4, 1024])
    mod = b.mod[l]
    for n in range(KC):
        for tb, (c0, c1) in enumerate(TBLK):
            ps = b.psn()
            for c in range(4):
                p.mm(ps[:, :], wv[:, c, n * 128:(n + 1) * 128], oT[:, c, c0:c1], start=(c == 0), stop=(c == 3),
                     reads=[slot, oT], writes=[ps])
            ci = 0 if tb == 0 else 1
            p.stt(b.xT[:, n, c0:c1], ps[:, :], mod[:, 16 + n, ci:ci + 1], b.xT[:, n, c0:c1], ALU.mult, ALU.add,
                  reads=[ps, mod, (b.xT, (n, tb))], writes=[(b.xT, (n, tb))])


def retention(b, oT):
    p = b.p
    a_w = b.a_w_in.rearrange("(kc p) n -> p kc n", p=128)
    with p.scope("retention:1033"):
        G = p.sbuf([128, 4, 1920], BF16, "G")
        lg = p.sbuf([128, 8], F32, "lg")
        lgT = p.sbuf([128, 8], F32, "lgT")
        dec = p.sbuf([128, 2, 4, 2], F32, "decst")
        p.act(lg[:], b.P("retdec"), AF.Sigmoid, reads=[b.prm], writes=[lg])
        p.act(lg[:], lg[:], AF.Ln, reads=[lg], writes=[lg])
        p.ts("dve", lgT[:], lg[:], float(TS + 1), None, ALU.mult, reads=[lg], writes=[lgT])
        nlg = p.sbuf([128, 8], F32, "nlg")
        p.ts("dve", nlg[:], lg[:], -1.0, None, ALU.mult, reads=[lg], writes=[nlg])
        with p.scope("retention:1043"):
            Dp = p.sbuf([128, 1920], F32, "Dp")
            Dn = p.sbuf([128, 1920], F32, "Dn")
            Dd = p.sbuf([128, 1920], F32, "Dd")
            E1 = p.sbuf([128, 1920], F32, "E1")
            E2 = p.sbuf([128, 1920], F32, "E2")
            pj = p.sbuf([128, 2, 2], F32, "pj")
            p.dma("sp", Dp[:], b.din("c_Dpos", [128, 1920])[:, :], writes=[Dp])
            p.dma("sp", Dn[:], b.din("c_Dneg", [128, 1920])[:, :], writes=[Dn])
            p.dma("sp", Dd[:], b.din("c_Ddiag", [128, 1920])[:, :], writes=[Dd])
            p.dma("sp", pj[:], b.din("c_pj", [128, 2, 2])[:, :, :], writes=[pj])
            for h in range(4):
                p.act(E1[:], Dp[:], AF.Exp, scale=lg[:, h:h + 1], reads=[Dp, lg], writes=[E1])
                p.act(E2[:], Dn[:], AF.Exp, scale=lg[:, 4 + h:5 + h], reads=[Dn, lg], writes=[E2])
                p.tt("dve", E1[:], E1[:], E2[:], ALU.mult, reads=[E1, E2], writes=[E1])
                p.tt("dve", G[:, h, :], E1[:], Dd[:], ALU.add, reads=[E1, Dd], writes=[(G, h)])
                for d in range(2):
                    p.act(dec[:, d, h, :], pj[:, d, :], AF.Exp, scale=lg[:, d * 4 + h:d * 4 + h + 1],
                          reads=[pj, lg], writes=[(dec, (d, h))])
        rq = p.sbuf([128, 2, T], BF16, "rq")
        rk = p.sbuf([128, 2, T], BF16, "rk")
        rvt = p.sbuf([128, 12, 512], BF16, "rvt")
        S0 = p.sbuf([128, 2, 2, 128], BF16, "S0")
        tpos = p.sbuf([128, TS], F32, "tpos")
        p.dma("sp", tpos[:], b.din("c_tpl1", [128, TS])[:, :], writes=[tpos])
        p.dma("pool", S0[:], b.din("ret_s0", [2, 4, 64, 128]).rearrange("r (j u) d e -> (u d) r j e", u=2), writes=[S0])
        slotB, wB = b.wload(a_w[:, :, 416:928], [128, KC, 512])
        for tb, (c0, c1) in enumerate(TBLK):
            for j in range(4):
                ps = b.psn()
                for kc in range(KC):
                    p.mm(ps[:, :], wB[:, kc, j * 128:(j + 1) * 128], b.hT[:, kc, c0:c1], start=(kc == 0), stop=(kc == KC - 1),
                         reads=[slotB, (b.hT, (kc, tb))], writes=[ps])
                if j < 2:
                    p.copy("act", rq[:, j, c0:c1], ps[:, :], reads=[ps], writes=[(rq, (j, tb))])
                else:
                    p.ts("dve", rk[:, j - 2, c0:c1], ps[:, :], 0.125, None, ALU.mult, reads=[ps], writes=[(rk, (j - 2, tb))])
        slotC, wC = b.wload(a_w[:, :, 928:1440], [128, KC, 512])
        for tl in range(12):
            ps = b.psn()
            for kc in range(KC):
                p.mm(ps[:, :], b.hT[:, kc, tl * 128:(tl + 1) * 128], wC[:, kc, :], start=(kc == 0), stop=(kc == KC - 1),
                     reads=[slotC, (b.hT, (kc, tl // 4))], writes=[ps])
            p.copy("act" if tl % 2 else "dve", rvt[:, tl, :], ps[:, :], reads=[ps], writes=[(rvt, tl)])
        out_st = b.dout("o_retst", [NPS, 2, 64, 4, 128])
        with p.scope("retention:1090"):
            rkt = p.sbuf([128, 4, 256], BF16, "rkt")
            for tl in range(4):
                ps = b.psn()
                for kc in range(KC):
                    p.mm(ps[:, 0:256], b.hT[:, kc, tl * 128:(tl + 1) * 128], wB[:, kc, 256:512], start=(kc == 0), stop=(kc == KC - 1),
                         reads=[slotB, (b.hT, (kc, 0))], writes=[ps])
                p.ts("dve", rkt[:, tl, :], ps[:, 0:256], 0.125, None, ALU.mult, reads=[ps], writes=[(rkt, tl)])
            kd = [p.sbuf([128, 64], BF16, f"kd{i}") for i in range(4)]
            stt_ = [p.sbuf([64, 4, 128], F32, f"st{i}") for i in range(2)]
            ki = 0
            for a in range(NPS):
                for d in range(2):
                    ps = b.psn()
                    for h in range(4):
                        for tl2 in range(2):
                            tl = 2 * a + tl2
                            k_ = kd[ki % 4]
                            ki += 1
                            p.ts("dve", k_[:], rkt[:, tl, h * 64:(h + 1) * 64], dec[:, d, h, tl2:tl2 + 1], None, ALU.mult,
                                 reads=[(rkt, tl), (dec, (d, h))], writes=[k_])
                            p.mm(ps[0:64, h * 128:(h + 1) * 128], k_[:], rvt[:, tl, h * 128:(h + 1) * 128],
                                 start=(tl2 == 0), stop=(tl2 == 1), reads=[k_, (rvt, tl)], writes=[ps])
                    st = stt_[(a * 2 + d) % 2]
                    p.copy("act", st[:], ps[0:64, :].rearrange("p (h e) -> p h e", e=128), reads=[ps], writes=[st])
                    b.store(out_st[a, d], st[:], [st], ("o_retst", (a * 2 + d) % 2))
        slotD, wD = b.wload(a_w[:, :, 1440:1952], [128, KC, 512])
        with p.scope("retention:1118"):
            ms = [p.sbuf([128, 512], BF16, f"ms{i}") for i in range(2)]
            decr = p.sbuf([128, TS], F32, "decr")
            qd = [p.sbuf([128, TS], BF16, f"qd{i}") for i in range(2)]
            ro = p.sbuf([128, 512], F32, "ro")
            sq = p.sbuf([128, 512], BF16, "sq")
            rt = p.sbuf([128, 512], F32, "rt")
            rs = p.sbuf([128, 512], F32, "rs")
            yn = p.sbuf([128, 512], BF16, "yn")
            sg = p.sbuf([128, 512], BF16, "sg")
            msi = 0
            acci = 0
            pending = [None]
            for h in range(4):
                j, u = h // 2, h % 2
                r0, r1 = u * 64, (u + 1) * 64
                for d in range(2):
                    if d == 0:
                        p.act(decr[:], tpos[:], AF.Exp, scale=lg[:, h:h + 1], reads=[tpos, lg], writes=[decr])
                    else:
                        p.act(decr[:], tpos[:], AF.Exp, scale=nlg[:, 4 + h:5 + h], bias=lgT[:, 4 + h:5 + h],
                              reads=[tpos, nlg, lgT], writes=[decr])
                    p.tt("dve", qd[d][r0:r1, :], rq[r0:r1, j, TP:T], decr[r0:r1, :], ALU.mult,
                         reads=[(rq, (j, 1)), (rq, (j, 2)), decr], writes=[qd[d]])
                jobs = [(a * SEQ, SEQ, [2 * a, 2 * a + 1], False) for a in range(NPS)]
                jobs += [(TP + qb * 512, 512, list(range(4, 12)), True) for qb in range(2)]
                for (q0, nq, sts, init) in jobs:
                    acc = b.ps[6 + acci % 2]
                    acci += 1
                    pend = {}

                    def rscore(k):
                        sk = sts[k]
                        ps_ = b.psn()
                        p.mm(ps_[:, 0:nq], rk[r0:r1, j, sk * 128:(sk + 1) * 128], rq[r0:r1, j, q0:q0 + nq],
                             reads=[(rk, (j, sk // 4))] + [(rq, (j, t)) for t in tbs_of(q0, q0 + nq)], writes=[ps_])
                        pend[k] = ps_
                    rscore(0)
                    if len(sts) > 1:
                        rscore(1)
                    if len(sts) > 2:
                        rscore(2)
                    if len(sts) > 3:
                        rscore(3)
                    for i, st_ in enumerate(sts):
                        if i + 4 < len(sts):
                            rscore(i + 4)
                        pscore = pend.pop(i)
                        off = (q0 - st_ * 128) + 896
                        m = ms[msi % 2]
                        msi += 1
                        p.tt("dve", m[:, 0:nq], pscore[:, 0:nq], G[:, h, off:off + nq], ALU.mult, reads=[pscore, (G, h)], writes=[m])
                        p.mm(acc[:, 0:nq], rvt[:, st_, h * 128:(h + 1) * 128], m[:, 0:nq], start=(i == 0),
                             stop=(i == len(sts) - 1 and not init), reads=[(rvt, st_), m], writes=[acc])
                    if init:
                        qoff = q0 - TP
                        p.mm(acc[:, 0:nq], S0[r0:r1, 0, j, :], qd[0][r0:r1, qoff:qoff + nq], start=False, stop=False,
                             reads=[S0, qd[0]], writes=[acc])
                        p.mm(acc[:, 0:nq], S0[r0:r1, 1, j, :], qd[1][r0:r1, qoff:qoff + nq], start=False, stop=True,
                             reads=[S0, qd[1]], writes=[acc])
                    def tail(acc=acc, q0=q0, nq=nq, h=h):
                        p.copy("act", ro[:, 0:nq], acc[:, 0:nq], reads=[acc], writes=[ro])
                        p.act(sq[:, 0:nq], ro[:, 0:nq], AF.Square, reads=[ro], writes=[sq])
                        pss = b.psn()
                        p.mm(pss[:, 0:nq], b.ones, sq[:, 0:nq], reads=[sq, b.cbf], writes=[pss])
                        _rms_rstd(b, pss, 128, nq, 128, rt, rs)
                        p.stt(yn[:, 0:nq], ro[:, 0:nq], b.P("retgn")[:, h:h + 1], rs[:, 0:nq], ALU.mult, ALU.mult,
                              reads=[ro, rs, b.prm], writes=[yn])
                        pg = b.psn()
                        for kc in range(KC):
                            p.mm(pg[:, 0:nq], wD[:, kc, h * 128:(h + 1) * 128], b.hT[:, kc, q0:q0 + nq], start=(kc == 0), stop=(kc == KC - 1),
                                 reads=[slotD] + XK(b.hT, kc, q0, q0 + nq), writes=[pg])
                        p.act(sg[:, 0:nq], pg[:, 0:nq], AF.Silu, reads=[pg], writes=[sg])
                        p.tt("pool", oT[:, h, q0:q0 + nq], sg[:, 0:nq], yn[:, 0:nq], ALU.mult, reads=[sg, yn], writes=[(oT, (h, q0))])
                    if pending[0] is not None:
                        pending[0]()
                    pending[0] = tail
            if pending[0] is not None:
                pending[0]()


def conv_ffn(b, l, hook=None):
    p = b.p
    up = b.ffn_up_in[l].rearrange("(kc p) n -> p kc n", p=128)
    down = b.ffn_down_in[l]
    mod = b.mod[l]
    cw = b.P(f"cw{l}")
    cb = b.P(f"cb{l}")
    with p.scope("conv_ffn:1191"):
        ncw = p.sbuf([128, 44 * 3], F32, "ncw")
        p.ts("dve", ncw[:], cw, -1.0, None, ALU.mult, reads=[b.prm], writes=[ncw])
        actT = [p.sbuf([128, 6, T], BF16, f"actT{i}") for i in range(2)]
        acc = [[p.sbuf([128, T], F32, f"acc{i}{j}") for j in range(2)] for i in range(2)]
        sa = [p.sbuf([128, T], BF16, f"sa{i}") for i in range(2)]
        psets = [(b.psall[:, 0:1536], b.ps[0:3]), (b.psall[:, 1536:3072], b.ps[3:6])]
        upslot = None
        for g6 in range(4):
            nfc = 6 if g6 < 3 else 4
            at = actT[g6 % 2]
            for c in range(nfc):
                fc = g6 * 6 + c
                if fc % 3 == 0:
                    ng = min(3, NFC - fc)
                    upslot = b.ring[b.ring_i % 3]
                    b.ring_i += 1
                    upv = upslot.t[:, 0:2 * KC * 384].rearrange("p (h k n) -> p h k n", h=2, k=KC)
                    p.dma("pool", upv[:, 0, :, 0:ng * 128], up[:, :, fc * 128:(fc + ng) * 128], writes=[(upslot, "a")])
                    p.dma("pool", upv[:, 1, :, 0:ng * 128], up[:, :, DFF + fc * 128:DFF + (fc + ng) * 128], writes=[(upslot, "b")])
                ci3 = fc % 3
                par = fc % 2
                for half in range(2):
                    pview, pbufs = psets[half]
                    ch = half * NFC + fc
                    for tb, (c0, c1) in enumerate(TBLK):
                        for kc in range(KC):
                            p.mm(pbufs[tb][:, :], upv[:, half, kc, ci3 * 128:(ci3 + 1) * 128], b.hT[:, kc, c0:c1],
                                 start=(kc == 0), stop=(kc == KC - 1), reads=[(upslot, "ab"[half]), (b.hT, (kc, tb))], writes=[pbufs[tb]])
                    A = acc[par][half]
                    w0 = cw[:, ch * 3 + 0:ch * 3 + 1]
                    w1 = cw[:, ch * 3 + 1:ch * 3 + 2]
                    w2 = cw[:, ch * 3 + 2:ch * 3 + 3]
                    p.act(A[:], pview, AF.Identity, scale=w1, bias=cb[:, ch:ch + 1], reads=list(pbufs) + [b.prm], writes=[A])
                    p.stt(A[:, 1:T], pview[:, 0:T - 1], w0, A[:, 1:T], ALU.mult, ALU.add, reads=list(pbufs) + [A, b.prm], writes=[A])
                    p.stt(A[:, 0:T - 1], pview[:, 1:T], w2, A[:, 0:T - 1], ALU.mult, ALU.add, reads=list(pbufs) + [A, b.prm], writes=[A])
                    p.stt(A[:, SEQ:2 * SEQ + 1:SEQ], pview[:, SEQ - 1:2 * SEQ:SEQ], ncw[:, ch * 3 + 0:ch * 3 + 1], A[:, SEQ:2 * SEQ + 1:SEQ],
                          ALU.mult, ALU.add, reads=list(pbufs) + [A, ncw], writes=[A])
                    p.stt(A[:, SEQ - 1:2 * SEQ:SEQ], pview[:, SEQ:2 * SEQ + 1:SEQ], ncw[:, ch * 3 + 2:ch * 3 + 3], A[:, SEQ - 1:2 * SEQ:SEQ],
                          ALU.mult, ALU.add, reads=list(pbufs) + [A, ncw], writes=[A])
                s_ = sa[par]
                p.act(s_[:], acc[par][0][:], AF.Silu, reads=[acc[par][0]], writes=[s_])
                p.tt("pool", at[:, c, :], s_[:], acc[par][1][:], ALU.mult, reads=[s_, acc[par][1]], writes=[(at, c)])
            dslot, dv = b.wload(down[g6 * 768:g6 * 768 + nfc * 128].rearrange("(c p) n -> p c n", p=128), [128, nfc, 1024])
            for n in range(KC):
                for tb, (c0, c1) in enumerate(TBLK):
                    ps = b.ps[6 + (n * 3 + tb) % 2]
                    for c in range(nfc):
                        p.mm(ps[:, :], dv[:, c, n * 128:(n + 1) * 128], at[:, c, c0:c1], start=(c == 0), stop=(c == nfc - 1),
                             reads=[dslot, (at, c)], writes=[ps])
                    ci = 0 if tb == 0 else 1
                    p.stt(b.xT[:, n, c0:c1], ps[:, :], mod[:, 40 + n, ci:ci + 1], b.xT[:, n, c0:c1], ALU.mult, ALU.add,
                          reads=[ps, mod, (b.xT, (n, tb))], writes=[(b.xT, (n, tb))])
            if hook is not None:
                hook(g6)


def diff_attn(b, oT):
    p = b.p
    bw = b.b_w_in.rearrange("(kc p) n -> p kc n", p=128)
    c1m = 1.0 - LAM_INIT1
    with p.scope("diff_attn:1255"):
        Qd = p.sbuf([128, 4, T], BF16, "Qd")
        Kd = p.sbuf([128, 4, T + PAST], BF16, "Kd")
        Vd = p.sbuf([128, 16, 512], BF16, "Vd")
        C128 = p.sbuf([128, TS], F32, "C128")
        S128 = p.sbuf([128, TS], F32, "S128")
        p.dma("sp", C128[:], b.din("c_C128", [128, TS])[:, :], writes=[C128])
        p.dma("sp", S128[:], b.din("c_S128", [128, TS])[:, :], writes=[S128])
        dkc = b.din("dk_cT", [4, 128, PAST])
        for h in range(4):
            p.dma("pool", Kd[:, h, T:T + PAST], dkc[h], writes=[(Kd, (h, 3))])
        dvc = b.din("dv_c", [PAST, 512])
        p.dma("pool", Vd[:, 12:16, :], dvc.rearrange("(t p) n -> p t n", p=128), writes=[(Vd, 12), (Vd, 13), (Vd, 14), (Vd, 15)])
        lamt = p.sbuf([128, 4], F32, "lamt")
        lprod = p.sbuf([128, 2, 64], F32, "lprod")
        lam = b.P("lam").rearrange("p (r d) -> p r d", d=64)
        p.tt("dve", lprod[:, 0, :], lam[:, 0, :], lam[:, 1, :], ALU.mult, reads=[b.prm], writes=[lprod])
        p.tt("dve", lprod[:, 1, :], lam[:, 2, :], lam[:, 3, :], ALU.mult, reads=[b.prm, lprod], writes=[lprod])
        p.op("dve", lambda e: e.tensor_reduce(out=lamt[:, 0:2], in_=lprod[:], axis=AX.X, op=ALU.add), reads=[lprod], writes=[lamt])
        p.act(lamt[:, 0:2], lamt[:, 0:2], AF.Exp, reads=[lamt], writes=[lamt])
        p.tt("dve", lamt[:, 2:3], lamt[:, 1:2], lamt[:, 0:1], ALU.subtract, reads=[lamt], writes=[lamt])
        p.ts("dve", lamt[:, 3:4], lamt[:, 2:3], -LAM_INIT1, None, ALU.add, reads=[lamt], writes=[lamt])
        nlam = lamt[:, 3:4]
        epsc = p.sbuf([128, 1], F32, "epsc")
        p.memset("dve", epsc[:], EPS / (c1m * c1m), writes=[epsc])
        out_dk = b.dout("o_dkT", [4, 128, TP])
        out_dv = b.dout("o_dv", [TP, 512])
        with p.scope("diff_attn:1285"):
            sqt = [p.sbuf([128, 512], BF16, f"sq{i}") for i in range(2)]
            rt = p.sbuf([128, 512], F32, "rt")
            rs = p.sbuf([128, 512], F32, "rs")
            t1 = p.sbuf([128, 512], F32, "t1")
            t2 = p.sbuf([128, 512], F32, "t2")
            gb = p.sbuf([128, 512], BF16, "gb")
            kst = [p.sbuf([128, 512], F32, f"kst{i}") for i in range(2)]
            si = 0
            for which in range(2):
                slot, wv = b.wload(bw[:, :, which * 512:(which + 1) * 512], [128, KC, 512])
                gname = "dqn" if which == 0 else "dkn"
                gn = b.P(gname)
                dst = Qd if which == 0 else Kd
                for h in range(4):
                    for tb, (c0, c1) in enumerate(TBLK):
                        ps = b.psn()
                        for kc in range(KC):
                            p.mm(ps[:, :], wv[:, kc, h * 128:(h + 1) * 128], b.hT[:, kc, c0:c1], start=(kc == 0), stop=(kc == KC - 1),
                                 reads=[slot, (b.hT, (kc, tb))], writes=[ps])
                        sq = sqt[si % 2]
                        si += 1
                        p.act(sq[:], ps[:], AF.Square, reads=[ps], writes=[sq])
                        pss = b.psn()
                        p.mm(pss[:], b.bd64, sq[:], reads=[sq, b.cbf], writes=[pss])
                        _rms_rstd(b, pss, 128, 512, 64, rt, rs, legacy=True)
                        if tb == 0:
                            if which == 1:
                                ks = kst[h % 2]
                                p.stt(ks[:], ps[:], gn[:, 0:1], rs[:], ALU.mult, ALU.mult, reads=[ps, rs, b.prm], writes=[ks])
                                b.store(out_dk[h], ks[:], [ks], ("o_dk", h % 2))
                                p.copy("act", dst[:, h, c0:c1], ks[:], reads=[ks], writes=[(dst, (h, tb))])
                            else:
                                p.stt(dst[:, h, c0:c1], ps[:], gn[:, 0:1], rs[:], ALU.mult, ALU.mult, reads=[ps, rs, b.prm], writes=[(dst, (h, tb))])
                        else:
                            r0 = c0 - TP
                            p.stt(t1[:], ps[:], gn[:, 0:1], C128[:, r0:r0 + 512], ALU.mult, ALU.mult, reads=[ps, C128, b.prm], writes=[t1])
                            p.act(gb[:], ps[:], AF.Identity, scale=gn[:, 0:1], reads=[ps, b.prm], writes=[gb])
                            pp = b.psn()
                            p.mm(pp[:], b.perm128, gb[:], reads=[gb, b.cbf], writes=[pp])
                            p.tt("dve", t2[:], pp[:], S128[:, r0:r0 + 512], ALU.mult, reads=[pp, S128], writes=[t2])
                            p.tt("pool", t1[:], t1[:], t2[:], ALU.add, reads=[t1, t2], writes=[t1])
                            p.tt("dve", dst[:, h, c0:c1], t1[:], rs[:], ALU.mult, reads=[t1, rs], writes=[(dst, (h, tb))])
            slot, wv = b.wload(bw[:, :, 1024:1536], [128, KC, 512])
            for tl in range(12):
                ps = b.psn()
                for kc in range(KC):
                    p.mm(ps[:, :], b.hT[:, kc, tl * 128:(tl + 1) * 128], wv[:, kc, :], start=(kc == 0), stop=(kc == KC - 1),
                         reads=[slot, (b.hT, (kc, tl // 4))], writes=[ps])
                if tl < 4:
                    vs = kst[tl % 2]
                    p.copy("act", vs[:], ps[:], reads=[ps], writes=[vs])
                    b.store(out_dv[tl * 128:(tl + 1) * 128, :], vs[:], [vs], ("o_dv", tl % 2))
                    p.copy("dve", Vd[:, tl, :], vs[:], reads=[vs], writes=[(Vd, tl)])
                else:
                    p.copy("act" if tl % 2 else "dve", Vd[:, tl, :], ps[:], reads=[ps], writes=[(Vd, tl)])
        with p.scope("diff_attn:1344"):
            ex = [p.sbuf([128, 512], BF16, f"ex{i}") for i in range(4)]
            r1 = p.sbuf([128, 512], F32, "r1")
            r2 = p.sbuf([128, 512], F32, "r2")
            o1b = [p.sbuf([128, 512], F32, f"o1{i}") for i in range(2)]
            o2 = p.sbuf([128, 512], F32, "o2")
            jobi = 0
            pending = [None]
            sq = p.sbuf([128, 512], BF16, "sq")
            rt = p.sbuf([128, 512], F32, "rt")
            rs = p.sbuf([128, 512], F32, "rs")
            exi = 0
            sci = 0
            for h in range(4):
                jobs = [(a * SEQ, SEQ, [2 * a, 2 * a + 1]) for a in range(NPS)]
                jobs += [(TP + qb * 512, 512, list(range(4, 16))) for qb in range(2)]
                for (q0, nq, kts) in jobs:
                    num = [b.ps[4], b.ps[5]]
                    den = [b.ps[6], b.ps[7]]
                    for c in range(2):
                        ra, rb = c * 64, (c + 1) * 64
                        pend = {}

                        def dscore(k):
                            nonlocal sci
                            ktk = kts[k]
                            ps_ = b.ps[sci % 4]
                            sci += 1
                            p.mm(ps_[:, 0:nq], Kd[ra:rb, h, ktk * 128:(ktk + 1) * 128], Qd[ra:rb, h, q0:q0 + nq],
                                 reads=[(Kd, (h, ktk // 4))] + [(Qd, (h, t)) for t in tbs_of(q0, q0 + nq)], writes=[ps_])
                            pend[k] = ps_
                        dscore(0)
                        if len(kts) > 1:
                            dscore(1)
                        if len(kts) > 2:
                            dscore(2)
                        for i, kt in enumerate(kts):
                            if i + 3 < len(kts):
                                dscore(i + 3)
                            pscore = pend.pop(i)
                            e = ex[exi % 4]
                            exi += 1
                            p.act(e[:, 0:nq], pscore[:, 0:nq], AF.Exp, scale=0.125, reads=[pscore], writes=[e])
                            p.mm(num[c][:, 0:nq], Vd[:, kt, h * 128:(h + 1) * 128], e[:, 0:nq], start=(i == 0), stop=(i == len(kts) - 1),
                                 reads=[(Vd, kt), e], writes=[num[c]])
                            p.mm(den[c][:, 0:nq], b.ones, e[:, 0:nq], start=(i == 0), stop=(i == len(kts) - 1),
                                 reads=[b.cbf, e], writes=[den[c]])
                    if DBG_KNOB[1] in (0, 2):
                        p.recip(r1[:, 0:nq], den[0][:, 0:nq], reads=[den[0]], writes=[r1])
                        p.recip(r2[:, 0:nq], den[1][:, 0:nq], reads=[den[1]], writes=[r2])
                    else:
                        p.act(r1[:, 0:nq], den[0][:, 0:nq], AF.Ln, reads=[den[0]], writes=[r1])
                        p.act(r2[:, 0:nq], den[1][:, 0:nq], AF.Ln, reads=[den[1]], writes=[r2])
                        p.act(r1[:, 0:nq], r1[:, 0:nq], AF.Exp, scale=-1.0, reads=[r1], writes=[r1])
                        p.act(r2[:, 0:nq], r2[:, 0:nq], AF.Exp, scale=-1.0, reads=[r2], writes=[r2])
                    o1 = o1b[jobi % 2]
                    jobi += 1
                    p.tt("dve", o1[:, 0:nq], num[0][:, 0:nq], r1[:, 0:nq], ALU.mult, reads=[num[0], r1], writes=[o1])
                    p.tt("dve", o2[:, 0:nq], num[1][:, 0:nq], r2[:, 0:nq], ALU.mult, reads=[num[1], r2], writes=[o2])
                    p.stt(o1[:, 0:nq], o2[:, 0:nq], nlam, o1[:, 0:nq], ALU.mult, ALU.add, reads=[o1, o2, lamt], writes=[o1])

                    def tail(o1=o1, h=h, q0=q0, nq=nq):
                        nonlocal sci
                        p.act(sq[:, 0:nq], o1[:, 0:nq], AF.Square, reads=[o1], writes=[sq])
                        pss = b.ps[sci % 4]
                        sci += 1
                        p.mm(pss[:, 0:nq], b.ones, sq[:, 0:nq], reads=[sq, b.cbf], writes=[pss])
                        p.act(rt[:, 0:nq], pss[:, 0:nq], AF.Ln, bias=epsc[:], scale=1.0 / (128.0 * c1m * c1m), reads=[pss, epsc], writes=[rt])
                        p.act(rs[:, 0:nq], rt[:, 0:nq], AF.Exp, scale=-0.5, reads=[rt], writes=[rs])
                        p.stt(oT[:, h, q0:q0 + nq], o1[:, 0:nq], b.P("dgn")[:, h:h + 1], rs[:, 0:nq], ALU.mult, ALU.mult,
                              reads=[o1, rs, b.prm], writes=[(oT, (h, q0))])
                    if pending[0] is not None:
                        pending[0]()
                    pending[0] = tail
            if pending[0] is not None:
                pending[0]()


CH = 64
NCH = T // CH
SEQS = [(0, 4, False), (4, 4, False), (8, 16, True)]
CDEC = math.exp(-0.5)
DBG_KNOB = [99, 3]


def shift3(b, dst, pview, pbufs, w1, wn, nwn, bias=None, extra=()):
    p = b.p
    rd = list(pbufs) + list(extra)
    if bias is None:
        p.act(dst[:], pview, AF.Identity, scale=w1, reads=rd + [b.prm], writes=[dst])
    else:
        p.act(dst[:], pview, AF.Identity, scale=w1, bias=bias, reads=rd + [b.prm], writes=[dst])
    p.stt(dst[:, 1:T], pview[:, 0:T - 1], wn[0], dst[:, 1:T], ALU.mult, ALU.add, reads=rd + [dst], writes=[dst])
    p.stt(dst[:, 0:T - 1], pview[:, 1:T], wn[1], dst[:, 0:T - 1], ALU.mult, ALU.add, reads=rd + [dst], writes=[dst])
    p.stt(dst[:, SEQ:2 * SEQ + 1:SEQ], pview[:, SEQ - 1:2 * SEQ:SEQ], nwn[0], dst[:, SEQ:2 * SEQ + 1:SEQ], ALU.mult, ALU.add,
          reads=rd + [dst], writes=[dst])
    p.stt(dst[:, SEQ - 1:2 * SEQ:SEQ], pview[:, SEQ:2 * SEQ + 1:SEQ], nwn[1], dst[:, SEQ - 1:2 * SEQ:SEQ], ALU.mult, ALU.add,
          reads=rd + [dst], writes=[dst])


def rwkv(b, oT):
    p = b.p
    bw = b.b_w_in.rearrange("(kc p) n -> p kc n", p=128)
    RW0 = 1536
    psetA = (b.psall[:, 0:1536], b.ps[0:3])
    psetB = (b.psall[:, 1536:3072], b.ps[3:6])
    with p.scope("rwkv:1424"):
        mu = b.P("mu")
        mu1 = p.sbuf([128, 15], F32, "mu1")
        muh = p.sbuf([128, 15], F32, "muh")
        nmuh = p.sbuf([128, 15], F32, "nmuh")
        p.ts("dve", mu1[:], mu, -1.0, 1.0, ALU.mult, ALU.add, reads=[b.prm], writes=[mu1])
        p.ts("dve", muh[:], mu, 0.5, None, ALU.mult, reads=[b.prm], writes=[muh])
        p.ts("dve", nmuh[:], mu, -0.5, None, ALU.mult, reads=[b.prm], writes=[nmuh])
        omka = p.sbuf([128, 4], F32, "omka")
        p.ts("dve", omka[:], b.P("ka"), -1.0, 1.0, ALU.mult, ALU.add, reads=[b.prm], writes=[omka])
        lw_ = p.sbuf([128, 3, 512], BF16, "lora_w")
        wup_in = b.din("rw_w_up", [2, 64, 512])
        aup_in = b.din("rw_a_up", [2, 64, 512])
        p.dma("pool", lw_[:, 0, :], wup_in.rearrange("d l n -> (d l) n"), writes=[(lw_, 0)])
        p.dma("pool", lw_[:, 1, :], aup_in.rearrange("d l n -> (d l) n"), writes=[(lw_, 1)])
        p.dma("pool", lw_[:, 2, :], b.din("rw_g_up", [128, 512])[:, :], writes=[(lw_, 2)])
        masks = p.sbuf([128, 2, 2, 64], BF16, "masks")
        mna = p.sbuf([64, 2, 64], BF16, "mna")
        eye = p.sbuf([64, 64], BF16, "eye")
        p.dma("pool", masks[:, 0, :, :], b.din("c_m_ab", [2, 128, 64]).rearrange("d p c -> p d c"), writes=[(masks, 0)])
        p.dma("pool", masks[:, 1, :, :], b.din("c_m_cd", [2, 128, 64]).rearrange("d p c -> p d c"), writes=[(masks, 1)])
        p.dma("pool", mna[:], b.din("c_m_na", [2, 64, 64]).rearrange("d p c -> p d c"), writes=[mna])
        p.dma("pool", eye[:], b.din("c_eye64", [64, 64])[:, :], writes=[eye])
        onesf = p.sbuf([128, 512], BF16, "onesb")
        p.memset("pool", onesf[:], 1.0, writes=[onesf])
        s0_in = b.din("rw_s0", [2, 8, 64, 64])
        out_st = b.dout("o_rwst", [NPS, 2, 8, 64, 64])
        twd = p.sbuf([128, T], BF16, "twd")
        adb = p.sbuf([128, T], BF16, "adb")
        sgd = p.sbuf([128, T], BF16, "sgd")
        slotL, wL = b.wload(bw[:, :, RW0 + 1536:RW0 + 1920], [128, KC, 384])
        with p.scope("rwkv:1457"):
            tmp = p.sbuf([128, T], F32, "ltmp")
            for i, (dst, fn) in enumerate(((twd, AF.Tanh), (adb, AF.Identity), (sgd, AF.Sigmoid))):
                pview, pbufs = psetA if i % 2 == 0 else psetB
                for tb, (c0, c1) in enumerate(TBLK):
                    for kc in range(KC):
                        p.mm(pbufs[tb][:, :], wL[:, kc, i * 128:(i + 1) * 128], b.hT[:, kc, c0:c1], start=(kc == 0), stop=(kc == KC - 1),
                             reads=[slotL, (b.hT, (kc, tb))], writes=[pbufs[tb]])
                ch = 12 + i
                shift3(b, tmp, pview, pbufs, mu1[:, ch:ch + 1], (muh[:, ch:ch + 1], muh[:, ch:ch + 1]),
                       (nmuh[:, ch:ch + 1], nmuh[:, ch:ch + 1]), extra=[mu1, muh, nmuh])
                p.act(dst[:], tmp[:], fn, reads=[tmp], writes=[dst])
        for j in range(4):
            if DBG_KNOB[0] <= 1 or (DBG_KNOB[0] < 99 and j > 0):
                break
            with p.scope("rwkv:1473"):
                rwkv_pair(b, oT, j, bw, RW0, psetA, psetB, mu1, muh, nmuh, omka, lw_, masks, mna, eye, onesf, twd, adb, sgd,
                          s0_in, out_st)


def rwkv_pair(b, oT, j, bw, RW0, psetA, psetB, mu1, muh, nmuh, omka, lw_, masks, mna, eye, onesb, twd, adb, sgd, s0_in, out_st):
    p = b.p
    R0, R1 = slice(0, 64), slice(64, 128)
    RU = [R0, R1]
    vb = p.sbuf([128, T], BF16, "vb")
    rtile = [p.sbuf([128, T], BF16, f"rtl{d}") for d in range(2)]
    kkt = [p.sbuf([128, T], BF16, f"kkt{d}") for d in range(2)]
    BK = [p.sbuf([128, NCH, 2, CH], BF16, f"BK{d}") for d in range(2)]
    eLend = [p.sbuf([128, NCH], F32, f"eLe{d}") for d in range(2)]
    bsum = p.sbuf([128, T], BF16, "bsum")
    with p.scope("rwkv_pair:1490"):
        rb = p.sbuf([128, T], BF16, "rb")
        kb = p.sbuf([128, T], BF16, "kb")
        kkb = p.sbuf([128, T], BF16, "kkb")
        with p.scope("rwkv_pair:1494"):
            FB = [p.sbuf([128, T], F32, f"f{i}") for i in range(3)]
            slot = b.ring[b.ring_i % 3]
            b.ring_i += 1
            wv = slot.t[:, 0:3 * KC * 128].rearrange("p (i k n) -> p i k n", i=3, k=KC)
            for i in range(3):
                c_ = RW0 + i * 512 + j * 128
                p.dma("pool", wv[:, i, :, :], bw[:, :, c_:c_ + 128], writes=[(slot, i)])
            for i, dstb in enumerate((rb, kb, vb)):
                pview, pbufs = psetA if i % 2 == 0 else psetB
                for tb, (c0, c1) in enumerate(TBLK):
                    for kc in range(KC):
                        p.mm(pbufs[tb][:, :], wv[:, i, kc, :], b.hT[:, kc, c0:c1], start=(kc == 0), stop=(kc == KC - 1),
                             reads=[(slot, i), (b.hT, (kc, tb))], writes=[pbufs[tb]])
                ch = i * 4 + j
                f0 = FB[i]
                f1 = FB[i]
                shift3(b, f0, pview, pbufs, mu1[:, ch:ch + 1], (muh[:, ch:ch + 1], muh[:, ch:ch + 1]),
                       (nmuh[:, ch:ch + 1], nmuh[:, ch:ch + 1]), extra=[mu1, muh, nmuh])
                p.copy("act", dstb[:], f0[:], reads=[f0], writes=[dstb])
                if i == 1:
                    p.ts("dve", f1[:], f0[:], b.P("kk")[:, j:j + 1], None, ALU.mult, reads=[f0, b.prm], writes=[f1])
                    sq = p.sbuf([128, 512], BF16, "sq")
                    rt = p.sbuf([128, 512], F32, "rt")
                    rs = p.sbuf([128, 512], F32, "rs")
                    for tb, (c0, c1) in enumerate(TBLK):
                        p.act(sq[:], f1[:, c0:c1], AF.Square, reads=[f1], writes=[sq])
                        pss = b.ps[6 + tb % 2]
                        p.mm(pss[:], b.bd64, sq[:], reads=[sq, b.cbf], writes=[pss])
                        _rms_rstd(b, pss, 128, 512, 1.0, rt, rs)
                        p.tt("dve", kkb[:, c0:c1], f1[:, c0:c1], rs[:], ALU.mult, reads=[f1, rs], writes=[(kkb, tb)])
        with p.scope("rwkv_pair:C"):
            TS_ = []
            for d in range(2):
                TS_.append(dict(fs=p.sbuf([128, 512], F32, "fs"), fa=p.sbuf([128, 512], F32, "fa"), fe=p.sbuf([128, 512], F32, "fe"),
                                fg=p.sbuf([128, 512], F32, "fg"), fkd=p.sbuf([128, 512], BF16, "fkd"), fb=p.sbuf([128, 512], BF16, "fb"),
                                gs=p.sbuf([128, 8], F32, "gs")))
            rrk = b.P("rrk")
            ka = b.P("ka")

            def c_block(d, tb):
                t_ = TS_[d]
                fs, fa, fe, fg, fkd, fb, gs = t_["fs"], t_["fa"], t_["fe"], t_["fg"], t_["fkd"], t_["fb"], t_["gs"]
                DR = RU[d]
                c0, c1 = TBLK[tb]
                ch0 = c0 // CH
                ps1 = b.ps[6 - 2 * d]
                ps2 = b.ps[7 - 2 * d]
                fgv = fg.t[:, :].rearrange("p (c t) -> p c t", t=CH)
                fdv = fa.t[:, :].rearrange("p (c t) -> p c t", t=CH)
                fev = fe.t[:, :].rearrange("p (c t) -> p c t", t=CH)
                sg_ = -CDEC if d == 0 else CDEC
                ecol = CH - 1 if d == 0 else 0
                ops = []
                A_ = ops.append
                A_(lambda: p.mm(ps1[:], lw_[DR, 0, j * 128:(j + 1) * 128], twd[DR, c0:c1], reads=[(lw_, 0), twd], writes=[ps1]))
                A_(lambda: p.act(fs[:], ps1[:], AF.Sigmoid, bias=b.P("w0")[:, d * 4 + j:d * 4 + j + 1], scale=1.0, reads=[ps1, b.prm], writes=[fs]))
                A_(lambda: p.mm(ps2[:], lw_[DR, 1, j * 128:(j + 1) * 128], adb[DR, c0:c1], reads=[(lw_, 1), adb], writes=[ps2]))
                A_(lambda: p.act(fa[:], ps2[:], AF.Sigmoid, bias=b.P("a0")[:, d * 4 + j:d * 4 + j + 1], scale=1.0, reads=[ps2, b.prm], writes=[fa]))
                A_(lambda: p.ts("dve", fe[:], fa[:], ka[:, j:j + 1], omka[:, j:j + 1], ALU.mult, ALU.add, reads=[fa, b.prm, omka], writes=[fe]))
                A_(lambda: p.tt("dve", fkd[:], kb[:, c0:c1], fe[:], ALU.mult, reads=[kb, fe], writes=[fkd]))
                A_(lambda: p.stt(fe[:], rb[:, c0:c1], rrk[:, j:j + 1], fkd[:], ALU.mult, ALU.mult, reads=[rb, fkd, b.prm], writes=[fe]))
                A_(lambda: p.tt("pool", bsum[:, c0:c1], bsum[:, c0:c1], fe[:], ALU.add, reads=[(bsum, tb), fe], writes=[(bsum, tb)]))
                A_(lambda: p.tt("dve", fb[:], fa[:], kkb[:, c0:c1], ALU.mult, reads=[fa, (kkb, tb)], writes=[fb]))
                A_(lambda: p.op("dve", lambda e: e.tensor_tensor_scan(out=fg[:], data0=onesb[:, 0:512], data1=fs[:], initial=0.0,
                                                                      op0=ALU.mult, op1=ALU.add), reads=[fs, onesb], writes=[fg]))
                if d == 0:
                    A_(lambda: p.memset("dve", gs[:, 0:1], 0.0, writes=[gs]))
                    A_(lambda: p.copy("dve", gs[:, 1:8], fg[:, CH - 1:512 - 1:CH], reads=[fg], writes=[gs]))
                    A_(lambda: p.tt("dve", fdv, fgv, gs[:, :].unsqueeze(2).to_broadcast([128, 8, CH]), ALU.subtract, reads=[fg, gs], writes=[fa]))
                else:
                    A_(lambda: p.tt("dve", fe[:], fg[:], fs[:], ALU.subtract, reads=[fg, fs], writes=[fe]))
                    A_(lambda: p.tt("dve", fdv, fev, fgv[:, :, CH - 1:CH].to_broadcast([128, 8, CH]), ALU.subtract, reads=[fg, fe], writes=[fa]))
                A_(lambda: p.act(fe[:], fa[:], AF.Exp, scale=sg_, reads=[fa], writes=[fe]))
                A_(lambda: p.tt("dve", rtile[d][:, c0:c1], rb[:, c0:c1], fe[:], ALU.mult, reads=[rb, fe], writes=[(rtile[d], tb)]))
                A_(lambda: p.copy("dve", eLend[d][:, ch0:ch0 + 8], fe[:, ecol:512:CH], reads=[fe], writes=[(eLend[d], tb)]))
                A_(lambda: p.act(fe[:], fa[:], AF.Exp, scale=-sg_, reads=[fa], writes=[fe]))
                A_(lambda: p.tt("dve", BK[d][:, ch0:ch0 + 8, 0, :], fb.t[:, :].rearrange("p (c t) -> p c t", t=CH), fev, ALU.mult,
                                reads=[fb, fe], writes=[(BK[d], (tb, 0))]))
                A_(lambda: p.tt("pool", BK[d][:, ch0:ch0 + 8, 1, :], fkd.t[:, :].rearrange("p (c t) -> p c t", t=CH), fev, ALU.mult,
                                reads=[fkd, fe], writes=[(BK[d], (tb, 1))]))
                A_(lambda: p.tt("dve", fa[:], fa[:], fs[:], ALU.subtract if d == 0 else ALU.add, reads=[fa, fs], writes=[fa]))
                A_(lambda: p.act(fe[:], fa[:], AF.Exp, scale=sg_, reads=[fa], writes=[fe]))
                A_(lambda: p.tt("dve", kkt[d][:, c0:c1], kkb[:, c0:c1], fe[:], ALU.mult, reads=[(kkb, tb), fe], writes=[(kkt[d], tb)]))
                return ops

            p.memset("pool", bsum[:], 0.0, writes=[bsum])
            for tb in range(3):
                o0, o1 = c_block(0, tb), c_block(1, tb)
                for i in range(max(len(o0), len(o1))):
                    if i < len(o0):
                        o0[i]()
                    if i < len(o1):
                        o1[i]()
    if DBG_KNOB[0] <= 2:
        return
    Zall = p.sbuf([128, NCH, 2, CH], BF16, "Zall")
    y = p.sbuf([128, T], BF16, "y")
    p.memset("pool", y[:], 0.0, writes=[y])
    for g in range(NCH // 8):
        E = 6
        pvu = b.psall[0:64, E * 512:(E + 2) * 512].rearrange("p (u c v) -> p u c v", u=2, c=8)
        for u in range(2):
            for cc in range(8):
                ch = g * 8 + cc
                p.mm(pvu[:, u, cc, :], vb[RU[u], ch * CH:(ch + 1) * CH], b.ident[RU[u], u * 64:(u + 1) * 64],
                     reads=[vb, b.cbf], writes=[b.ps[E + u]])
        p.copy("act" if g % 2 else "dve", Zall[64:128, g * 8:(g + 1) * 8, :, :].rearrange("p c u v -> p u c v"), pvu,
               reads=[b.ps[E], b.ps[E + 1]], writes=[(Zall, ("v", g))])
    if DBG_KNOB[0] == 3 and j == 0:
        dz = b.dout("dbg_Z", [128, NCH, 2, CH])
        b.out_dmas.append(p.dma("pool", dz[64:128, :, :, :], Zall[64:128, :, :, :], reads=[Zall], sem="dbg_Z"))
        dvb = b.dout("dbg_vb", [128, T])
        b.out_dmas.append(p.dma("pool", dvb[:, :], vb[:], reads=[vb], sem="dbg_vb"))
    if DBG_KNOB[0] <= 3:
        return
    with p.scope("rwkv_pair:1613"):
        bufsets = []
        for _ in range(4):
            st = {}
            st["Hf"] = p.sbuf([128, 64], F32, "Hf")
            st["Hb"] = p.sbuf([128, 64], BF16, "Hb")
            st["ht"] = p.sbuf([128, 64], F32, "ht")
            for nm, shp in (("NtB", [128, 2, CH]), ("CDt", [128, 2, CH]), ("BKt", [128, 2, CH]), ("Pm", [64, 2, CH])):
                st[nm] = [p.sbuf(shp, BF16, nm) for _ in range(2)]
            st["Ntp"] = [p.sbuf([64, 2, CH], BF16, "Ntp") for _ in range(2)]
            st["Nap"] = [p.sbuf([64, 2, CH], BF16, "Nap") for _ in range(2)]
            st["Pp"] = p.sbuf([64, 2, CH], BF16, "Pp")
            st["Xs"] = p.sbuf([64, 2, CH], BF16, "Xs")
            st["X2"] = p.sbuf([64, 2, CH], F32, "X2")
            bufsets.append(st)

        def make_chains(seq_ids):
            chains = []
            for d in range(2):
                for si in seq_ids:
                    cs, n, init = SEQS[si]
                    order = list(range(cs, cs + n)) if d == 0 else list(range(cs + n - 1, cs - 1, -1))
                    st = dict(bufsets[len(chains)])
                    st.update(d=d, si=si, order=order, init=init)
                    if init:
                        p.dma("sp", st["Hf"][:], s0_in[d, 2 * j:2 * j + 2].rearrange("u k v -> (u k) v"), writes=[st["Hf"]])
                    else:
                        p.memset("dve", st["Hf"][:], 0.0, writes=[st["Hf"]])
                    p.copy("act", st["Hb"][:], st["Hf"][:], reads=[st["Hf"]], writes=[st["Hb"]])
                    chains.append(st)
            return chains

        psi = [0]

        def nps():
            psi[0] += 1
            return b.ps[psi[0] % 8]

        pri = [0]

        def npair():
            pri[0] += 1
            return 2 * (pri[0] % 4)

        def bc(ap):
            return ap.unsqueeze(1).to_broadcast([ap.shape[0], 2, CH])

        def pre_stages(st, k):
            d = st["d"]
            ch = st["order"][k]
            cc = slice(ch * CH, (ch + 1) * CH)
            tb = (ch * CH) // 512
            par = k % 2
            NtB, CDt, BKt, Pm = st["NtB"][par], st["CDt"][par], st["BKt"][par], st["Pm"][par]
            Ntp, Nap = st["Ntp"], st["Nap"]
            bkr = [(BK[d], (tb, 0)), (BK[d], (tb, 1))]

            def s0():
                E = npair()
                pe2 = [b.ps[E], b.ps[E + 1]]
                pu3 = b.psall[:, E * 512:(E + 2) * 512].rearrange("p (u x) -> p u x", u=2)
                for u in range(2):
                    bku = BK[d][RU[u], ch, :, :].rearrange("p a t -> p (a t)")
                    p.mm(pu3[:, u, 0:64], bku, kkt[d][RU[u], cc], reads=bkr + [(kkt[d], tb)], writes=[pe2[u]])
                    p.mm(pu3[0:64, u, 64:128], kkt[d][RU[u], cc], BK[d][RU[u], ch, 0, :], reads=bkr + [(kkt[d], tb)], writes=[pe2[u]])
                    p.mm(pu3[:, u, 128:192], bku, rtile[d][RU[u], cc], reads=bkr + [(rtile[d], tb)], writes=[pe2[u]])
                p.tt("dve", NtB[:], pu3[:, :, 0:64], bc(masks[:, 0, d, :]), ALU.mult, reads=pe2 + [(masks, 0)], writes=[NtB])
                p.tt("dve", Nap[0][:], pu3[0:64, :, 64:128], bc(mna[:, d, :]), ALU.mult, reads=pe2 + [mna], writes=[Nap[0]])
                p.tt("dve", CDt[:], pu3[:, :, 128:192], bc(masks[:, 1, d, :]), ALU.mult, reads=pe2 + [(masks, 1)], writes=[CDt])
                p.tt("pool", st["Pp"][:], NtB[0:64, :, :], bc(eye[:, :]), ALU.add, reads=[NtB, eye], writes=[st["Pp"]])
            yield s0

            for i in range(5):
                def lv(i=i):
                    Nt_i = NtB[0:64, :, :] if i == 0 else Ntp[i % 2]
                    Nt_r = NtB if i == 0 else Ntp[i % 2]
                    Na_i = Nap[i % 2]
                    Na_n = Nap[(i + 1) % 2]
                    pq = nps()
                    pqv = pq.t[0:64, 0:128].rearrange("p (u t) -> p u t", u=2)
                    for u in range(2):
                        p.mm(pqv[:, u, :], Nt_i[:, u, :], Na_i[:, u, :], reads=[Nt_r, Na_i], writes=[pq])
                    p.copy("act", Na_n[:], pqv, reads=[pq], writes=[Na_n])
                    if i < 4:
                        Nt_n = Ntp[(i + 1) % 2]
                        pq2 = nps()
                        pq2v = pq2.t[0:64, 0:128].rearrange("p (u t) -> p u t", u=2)
                        for u in range(2):
                            p.mm(pq2v[:, u, :], Na_i[:, u, :], Nt_i[:, u, :], reads=[Nt_r, Na_i], writes=[pq2])
                        p.copy("act", Nt_n[:], pq2v, reads=[pq2], writes=[Nt_n])
                yield lv

                def pu(i=i):
                    Na_n = Nap[(i + 1) % 2]
                    Pin = st["Pp"] if i % 2 == 0 else Pm
                    Pout = Pm if i % 2 == 0 else st["Pp"]
                    pq = nps()
                    pqv = pq.t[0:64, 0:128].rearrange("p (u t) -> p u t", u=2)
                    for u in range(2):
                        p.mm(pqv[:, u, :], Na_n[:, u, :], Pin[:, u, :], reads=[Na_n, Pin], writes=[pq])
                    p.tt("dve", Pout[:], Pin[:], pqv, ALU.add, reads=[Pin, pq], writes=[Pout])
                yield pu

            def s5():
                E = npair()
                pe2 = [b.ps[E], b.ps[E + 1]]
                pu3 = b.psall[:, E * 512:(E + 2) * 512].rearrange("p (u x) -> p u x", u=2)
                for u in range(2):
                    bku = BK[d][RU[u], ch, :, :].rearrange("p a t -> p (a t)")
                    p.mm(pu3[:, u, 0:64], bku, b.ident[RU[u], u * 64:(u + 1) * 64], reads=bkr + [b.cbf], writes=[pe2[u]])
                p.copy("act", BKt[:], pu3[:, :, 0:64], reads=pe2, writes=[BKt])
            yield s5

        def seq_stages(st, k):
            d = st["d"]
            ch = st["order"][k]
            cc = slice(ch * CH, (ch + 1) * CH)
            tb = (ch * CH) // 512
            par = k % 2
            NtB, CDt, BKt, Pm = st["NtB"][par], st["CDt"][par], st["BKt"][par], st["Pm"][par]
            Hf, Hb, Xs, ht = st["Hf"], st["Hb"], st["Xs"], st["ht"]
            zv = (Zall, ("v", ch // 8))
            zu = (Zall, ("u", ch))

            def s1():
                E = npair()
                pe2 = [b.ps[E], b.ps[E + 1]]
                pu3 = b.psall[0:64, E * 512:(E + 2) * 512].rearrange("p (u x) -> p u x", u=2)
                px = b.ps[npair()]
                pxv = px.t[0:64, 0:128].rearrange("p (u t) -> p u t", u=2)
                for u in range(2):
                    p.mm(pu3[:, u, 0:64], kkt[d][RU[u], cc], Hb[RU[u], :], reads=[(kkt[d], tb), Hb], writes=[pe2[u]])
                for u in range(2):
                    p.mm(pxv[:, u, :], NtB[64:128, u, :], Zall[64:128, ch, u, :], reads=[NtB, zv], writes=[px])
                p.copy("act", st["X2"][:], pxv, reads=[px], writes=[st["X2"]])
                p.tt("dve", Xs[:], pu3[:, :, 0:64], st["X2"][:], ALU.add, reads=pe2 + [st["X2"]], writes=[Xs])
            yield s1

            def s2():
                pu_ = nps()
                puv = pu_.t[0:64, 0:128].rearrange("p (u t) -> p u t", u=2)
                for u in range(2):
                    p.mm(puv[:, u, :], Pm[:, u, :], Xs[:, u, :], reads=[Pm, Xs], writes=[pu_])
                p.ts("dve", Zall[0:64, ch, :, :], puv, -1.0, None, ALU.mult, reads=[pu_], writes=[zu])
            yield s2

            def s3():
                py = nps()
                ph = nps()
                for u in range(2):
                    p.mm(py[RU[u], 0:CH], Hb[RU[u], :], rtile[d][RU[u], cc], start=True, stop=False, reads=[Hb, (rtile[d], tb)], writes=[py])
                    p.mm(py[RU[u], 0:CH], Zall[:, ch, u, :], CDt[:, u, :], start=False, stop=True, reads=[zv, zu, CDt], writes=[py])
                    p.mm(ph[RU[u], 0:CH], BKt[:, u, :], Zall[:, ch, u, :], reads=[BKt, zv, zu], writes=[ph])
                p.tt("dve", y[:, cc], y[:, cc], py[:, 0:CH], ALU.add, reads=[(y, ch), py], writes=[(y, ch)])
                p.tt("dve", ht[:], Hf[:], ph[:, 0:CH], ALU.add, reads=[Hf, ph], writes=[ht])
                p.ts("dve", Hf[:], ht[:], eLend[d][:, ch:ch + 1], None, ALU.mult, reads=[ht, (eLend[d], tb)], writes=[Hf])
                p.copy("act", Hb[:], Hf[:], reads=[Hf], writes=[Hb])
                if k == len(st["order"]) - 1 and not st["init"]:
                    a = st["si"]
                    b.store(out_st[a, d, 2 * j:2 * j + 2].rearrange("u k v -> (u k) v"), Hf[:], [Hf], ("o_rwst", a, d))
            yield s3

        for seq_ids in ((0, 1), (2,)):
            chains = make_chains(seq_ids)
            maxk = max(len(st["order"]) for st in chains)
            for k in range(-1, maxk):
                if DBG_KNOB[0] == 4 and k >= 0:
                    break
                if DBG_KNOB[0] == 5 and k >= 1:
                    break
                gens = []
                for st in chains:
                    n = len(st["order"])
                    if 0 <= k + 1 < n:
                        gens.append(pre_stages(st, k + 1))
                    if 0 <= k < n:
                        gens.append(seq_stages(st, k))
                active = [iter(g) for g in gens]
                while active:
                    nxt = []
                    for it in active:
                        try:
                            f = next(it)
                        except StopIteration:
                            continue
                        f()
                        nxt.append(it)
                    active = nxt
    if DBG_KNOB[0] <= 6:
        return
    with p.scope("rwkv_pair:1807"):
        sq = p.sbuf([128, 512], BF16, "sq")
        rt = p.sbuf([128, 512], F32, "rt")
        rs = p.sbuf([128, 512], F32, "rs")
        yn = p.sbuf([128, 512], F32, "yn")
        bo = p.sbuf([128, 512], F32, "bo")
        for tb, (c0, c1) in enumerate(TBLK):
            p.act(sq[:], y[:, c0:c1], AF.Square, reads=[y], writes=[sq])
            pss = b.psn()
            p.mm(pss[:], b.bd64, sq[:], reads=[sq, b.cbf], writes=[pss])
            _rms_rstd(b, pss, 128, 512, 64, rt, rs)
            p.stt(yn[:], y[:, c0:c1], b.P("rgn")[:, j:j + 1], rs[:], ALU.mult, ALU.mult, reads=[y, rs, b.prm], writes=[yn])
            psb = b.psn()
            p.mm(psb[:], b.bd64, bsum[:, c0:c1], reads=[(bsum, tb), b.cbf], writes=[psb])
            p.tt("dve", bo[:], psb[:], vb[:, c0:c1], ALU.mult, reads=[psb, vb], writes=[bo])
            p.tt("pool", yn[:], yn[:], bo[:], ALU.add, reads=[yn, bo], writes=[yn])
            psg = b.psn()
            p.mm(psg[:], lw_[:, 2, j * 128:(j + 1) * 128], sgd[:, c0:c1], reads=[(lw_, 2), sgd], writes=[psg])
            p.tt("dve", oT[:, j, c0:c1], yn[:], psg[:], ALU.mult, reads=[yn, psg], writes=[(oT, (j, tb))])
```
